# Optimizing a Trainium2 kernel written in Bass

```python
import math
import jax, jax.numpy as jnp
from jax import lax
import numpy as np

D_MODEL = 4096
BATCH = 4
SEQ = 4096
DEPTH = 1

CHUNK = 64
Q_BLOCK = 128
SSM_EXPAND = 2
D_INNER = SSM_EXPAND * D_MODEL
SSM_HEAD_DIM = 64
SSM_HEADS = D_INNER // SSM_HEAD_DIM
SSM_GROUPS = 8
SSM_STATE = 128
CONV_WIDTH = 4
CONV_CH = D_INNER + 2 * SSM_GROUPS * SSM_STATE
MLA_HEADS = D_MODEL // 128
QK_NOPE = 128
QK_ROPE = 64
V_HEAD = 128
Q_LORA = D_MODEL // 4
KV_LORA = 512
ROPE_THETA = 10000.0
N_BRANCH = 2
D_FF = ((8 * D_MODEL // 3 + 255) // 256) * 256
NORM_EPS = 1e-6
GATED_NORM_EPS = 1e-5

COL_Z = D_INNER
COL_XBC = CONV_CH
COL_DT = SSM_HEADS
COL_QA = Q_LORA
COL_KVA = KV_LORA + QK_ROPE
COL_GATE = N_BRANCH * D_MODEL
IN_SPLITS = list(np.cumsum([COL_Z, COL_XBC, COL_DT, COL_QA, COL_KVA]))
IN_WIDTH = COL_Z + COL_XBC + COL_DT + COL_QA + COL_KVA + COL_GATE

kernel_name = "hybrid_ssd_mla_gated_block"


def rms_norm(t, g, eps=NORM_EPS):
    tf = t.astype(jnp.float32)
    tf = tf * lax.rsqrt(jnp.mean(tf * tf, axis=-1, keepdims=True) + eps)
    return (tf * g.astype(jnp.float32)).astype(t.dtype)


def gated_group_rmsnorm(y, z, g):
    yf = y.astype(jnp.float32) * jax.nn.silu(z.astype(jnp.float32))
    yg = yf.reshape(y.shape[:-1] + (SSM_GROUPS, -1))
    yg = yg * lax.rsqrt(jnp.mean(yg * yg, axis=-1, keepdims=True) + GATED_NORM_EPS)
    return yg.reshape(y.shape) * g.astype(jnp.float32)


def causal_depthwise_conv(u, w, bias):
    out = lax.conv_general_dilated(
        u, w[:, None, :].astype(u.dtype), window_strides=(1,),
        padding=[(CONV_WIDTH - 1, 0)], dimension_numbers=("NWC", "WIO", "NWC"),
        feature_group_count=u.shape[-1])
    return out + bias.astype(u.dtype)


def ssd_scan(xh, dt, a, bmat, cmat):
    b_, s_, h_, p_ = xh.shape
    nc = s_ // CHUNK
    r = h_ // SSM_GROUPS
    adt = (dt * a).reshape(b_, s_, SSM_GROUPS, r)
    xdt = (xh * dt[..., None]).reshape(b_, s_, SSM_GROUPS, r, p_)

    def to_chunks(t):
        return jnp.moveaxis(t.reshape((b_, nc, CHUNK) + t.shape[2:]), 1, 0)

    causal = jnp.tril(jnp.ones((CHUNK, CHUNK), dtype=bool))

    def step(state, inp):
        xc, ac, bc, cc = inp
        acs = jnp.cumsum(ac, axis=1)
        seg = acs[:, :, None] - acs[:, None, :]
        decay = jnp.exp(jnp.where(causal[None, :, :, None, None], seg, -jnp.inf))
        cb = jnp.einsum('blgn,bsgn->blsg', cc, bc)
        y_diag = jnp.einsum('blsg,blsgr,bsgrp->blgrp', cb, decay, xc)
        y_off = jnp.einsum('blgn,bgrpn,blgr->blgrp', cc, state, jnp.exp(acs))
        last = acs[:, -1]
        w_in = jnp.exp(last[:, None] - acs)
        state = state * jnp.exp(last)[..., None, None] + jnp.einsum(
            'bsgn,bsgr,bsgrp->bgrpn', bc, w_in, xc)
        return state, y_diag + y_off

    state0 = jnp.zeros((b_, SSM_GROUPS, r, p_, SSM_STATE), jnp.float32)
    _, ys = lax.scan(step, state0, (to_chunks(xdt), to_chunks(adt),
                                    to_chunks(bmat), to_chunks(cmat)))
    return jnp.moveaxis(ys, 0, 1).reshape(b_, s_, h_, p_)


def apply_rope(t, cos, sin):
    half = t.shape[-1] // 2
    t1, t2 = t[..., :half], t[..., half:]
    cos = cos.astype(t.dtype)
    sin = sin.astype(t.dtype)
    return jnp.concatenate([t1 * cos - t2 * sin, t2 * cos + t1 * sin], axis=-1)


def mla_attention(q_nope, q_rope, k_nope, k_rope, v):
    b_, s_, h_, _ = q_nope.shape
    nb = s_ // Q_BLOCK
    scale = (QK_NOPE + QK_ROPE) ** -0.5
    key_chunk = jnp.arange(s_) // CHUNK

    def blockify(t):
        return jnp.moveaxis(t.reshape((b_, nb, Q_BLOCK) + t.shape[2:]), 1, 0)

    def one_block(args):
        qn, qr, start = args
        sc = (jnp.einsum('bqhd,bkhd->bhqk', qn, k_nope)
              + jnp.einsum('bqhd,bkd->bhqk', qr, k_rope)).astype(jnp.float32) * scale
        q_chunk = (start + jnp.arange(Q_BLOCK)) // CHUNK
        mask = key_chunk[None, :] <= q_chunk[:, None]
        sc = jnp.where(mask[None, None], sc, -jnp.inf)
        p = jax.nn.softmax(sc, axis=-1).astype(v.dtype)
        return jnp.einsum('bhqk,bkhd->bqhd', p, v)

    starts = jnp.arange(nb, dtype=jnp.int32) * Q_BLOCK
    out = lax.map(one_block, (blockify(q_nope), blockify(q_rope), starts))
    return jnp.moveaxis(out, 0, 1).reshape(b_, s_, h_, V_HEAD)


def setup_inputs(seed: int = 0) -> dict:
    key = jax.random.key(seed)
    ks = jax.random.split(key, 24)

    def dense(k, shape, fan_in):
        return jax.random.normal(k, shape, jnp.float32) * (fan_in ** -0.5)

    def gain(k, shape):
        return 1.0 + 0.02 * jax.random.normal(k, shape, jnp.float32)

    x = jax.random.normal(ks[0], (BATCH, SEQ, D_MODEL), jnp.float32)
    start = jax.random.randint(ks[1], (BATCH, 1), 0, 4096, dtype=jnp.int32)
    positions = start + jnp.arange(SEQ, dtype=jnp.int32)[None, :]
    dt0 = jnp.exp(jax.random.uniform(ks[6], (DEPTH, SSM_HEADS), jnp.float32,
                                     math.log(1e-3), math.log(1e-1)))
    dt_bias = dt0 + jnp.log(-jnp.expm1(-dt0))
    a_log = jnp.log(jax.random.uniform(ks[7], (DEPTH, SSM_HEADS), jnp.float32, 1.0, 16.0))
    return {
        "x": x,
        "positions": positions,
        "g_mix": gain(ks[2], (DEPTH, D_MODEL)),
        "w_in": dense(ks[3], (DEPTH, D_MODEL, IN_WIDTH), D_MODEL),
        "conv_w": dense(ks[4], (DEPTH, CONV_WIDTH, CONV_CH), CONV_WIDTH),
        "conv_b": 0.02 * jax.random.normal(ks[5], (DEPTH, CONV_CH), jnp.float32),
        "dt_bias": dt_bias,
        "a_log": a_log,
        "d_skip": gain(ks[8], (DEPTH, SSM_HEADS)),
        "ssm_norm_g": gain(ks[9], (DEPTH, D_INNER)),
        "w_ssm_out": dense(ks[10], (DEPTH, D_INNER, D_MODEL), D_INNER),
        "q_norm_g": gain(ks[11], (DEPTH, Q_LORA)),
        "w_q_up": dense(ks[12], (DEPTH, Q_LORA, MLA_HEADS * (QK_NOPE + QK_ROPE)), Q_LORA),
        "kv_norm_g": gain(ks[13], (DEPTH, KV_LORA)),
        "w_kv_up": dense(ks[14], (DEPTH, KV_LORA, MLA_HEADS * (QK_NOPE + V_HEAD)), KV_LORA),
        "w_mla_out": dense(ks[15], (DEPTH, MLA_HEADS * V_HEAD, D_MODEL), MLA_HEADS * V_HEAD),
        "gate_bias": 0.02 * jax.random.normal(ks[16], (DEPTH, N_BRANCH, D_MODEL), jnp.float32),
        "w_out": dense(ks[17], (DEPTH, D_MODEL, D_MODEL), D_MODEL),
        "g_ffn": gain(ks[18], (DEPTH, D_MODEL)),
        "w_ffn_gate": dense(ks[19], (DEPTH, D_MODEL, D_FF), D_MODEL),
        "w_ffn_up": dense(ks[20], (DEPTH, D_MODEL, D_FF), D_MODEL),
        "w_ffn_down": dense(ks[21], (DEPTH, D_FF, D_MODEL), D_FF),
        "g_final": gain(ks[22], (D_MODEL,)),
    }


def reference(x, positions, g_mix, w_in, conv_w, conv_b, dt_bias, a_log, d_skip, ssm_norm_g,
              w_ssm_out, q_norm_g, w_q_up, kv_norm_g, w_kv_up, w_mla_out, gate_bias, w_out,
              g_ffn, w_ffn_gate, w_ffn_up, w_ffn_down, g_final):
    b_, s_, _ = x.shape
    half = QK_ROPE // 2
    inv_freq = ROPE_THETA ** (-jnp.arange(half, dtype=jnp.float32) / half)
    ang = positions.astype(jnp.float32)[..., None] * inv_freq
    cos, sin = jnp.cos(ang), jnp.sin(ang)

    h = x
    for l in range(DEPTH):
        u = rms_norm(h, g_mix[l])
        proj = u @ w_in[l]
        z, xbc, dt_raw, cq, ckv_full, gate_logits = jnp.split(proj, IN_SPLITS, axis=-1)

        xbc = jax.nn.silu(causal_depthwise_conv(xbc, conv_w[l], conv_b[l]))
        xs_, bm, cm = jnp.split(xbc, [D_INNER, D_INNER + SSM_GROUPS * SSM_STATE], axis=-1)
        xh = xs_.reshape(b_, s_, SSM_HEADS, SSM_HEAD_DIM).astype(jnp.float32)
        dt = jax.nn.softplus(dt_raw.astype(jnp.float32) + dt_bias[l].astype(jnp.float32))
        a = -jnp.exp(a_log[l].astype(jnp.float32))
        y = ssd_scan(xh, dt, a,
                     bm.reshape(b_, s_, SSM_GROUPS, SSM_STATE).astype(jnp.float32),
                     cm.reshape(b_, s_, SSM_GROUPS, SSM_STATE).astype(jnp.float32))
        y = y + xh * d_skip[l].astype(jnp.float32)[:, None]
        y = gated_group_rmsnorm(y.reshape(b_, s_, D_INNER), z, ssm_norm_g[l])
        y_ssm = y.astype(h.dtype) @ w_ssm_out[l]

        q = (rms_norm(cq, q_norm_g[l]) @ w_q_up[l]).reshape(
            b_, s_, MLA_HEADS, QK_NOPE + QK_ROPE)
        q_nope, q_rope = q[..., :QK_NOPE], q[..., QK_NOPE:]
        q_rope = apply_rope(q_rope, cos[:, :, None, :], sin[:, :, None, :])
        ckv, k_rope = ckv_full[..., :KV_LORA], ckv_full[..., KV_LORA:]
        k_rope = apply_rope(k_rope, cos, sin)
        kv = (rms_norm(ckv, kv_norm_g[l]) @ w_kv_up[l]).reshape(
            b_, s_, MLA_HEADS, QK_NOPE + V_HEAD)
        k_nope, v = kv[..., :QK_NOPE], kv[..., QK_NOPE:]
        attn = mla_attention(q_nope, q_rope, k_nope, k_rope, v)
        y_mla = attn.reshape(b_, s_, MLA_HEADS * V_HEAD) @ w_mla_out[l]

        gates = jax.nn.sigmoid(gate_logits.reshape(b_, s_, N_BRANCH, D_MODEL)
                               + gate_bias[l].astype(gate_logits.dtype))
        merged = gates[:, :, 0] * y_ssm + gates[:, :, 1] * y_mla
        h = h + merged @ w_out[l]

        n = rms_norm(h, g_ffn[l])
        h = h + (jax.nn.silu(n @ w_ffn_gate[l]) * (n @ w_ffn_up[l])) @ w_ffn_down[l]

    return rms_norm(h, g_final)
```

```python
import contextlib
import numpy as np
import ml_dtypes
import concourse.bass as bass
import concourse.mybir as mybir
from concourse.bass_utils import run_bass_kernel_spmd

F32 = mybir.dt.float32
BF16 = mybir.dt.bfloat16
I32 = mybir.dt.int32
ALU = mybir.AluOpType
AF = mybir.ActivationFunctionType
PI = float(np.pi)


def make_cfg(D, SEQ):
    c = dict(D=D, SEQ=SEQ, T=SEQ // 2, PT=SEQ // 2)
    c["TT"] = c["T"] + c["PT"]
    c["DI"] = 2 * D
    c["H"] = c["DI"] // 64
    c["G"] = 8
    c["R"] = c["H"] // 8
    c["NST"] = 128
    c["CONVC"] = c["DI"] + 2 * 8 * 128
    c["QL"] = D // 4
    c["KVL"] = 512
    c["MH"] = D // 128
    c["DFF"] = ((8 * D // 3 + 255) // 256) * 256
    c["INW"] = c["DI"] + c["CONVC"] + c["H"] + c["QL"] + 576 + 2 * D
    return c


class Buf:
    __slots__ = ("t", "w", "r", "sem", "name")

    def __init__(self, t, name):
        self.t = t
        self.w = {}
        self.r = {}
        self.sem = None
        self.name = name


class DSem:
    def __init__(self, sem):
        self.sem = sem
        self.total = 0


class FW:
    def __init__(self, nc, stack, n_dsem=84):
        self.nc = nc
        self.eng = {"pe": nc.tensor, "act": nc.scalar, "dve": nc.vector, "pool": nc.gpsimd, "sp": nc.sync}
        self.psem = {}
        self.cnt = {}
        for e in ("pe", "act", "dve", "pool"):
            self.psem[e] = stack.enter_context(nc.semaphore("p_" + e))
            self.cnt[e] = 0
        self.seen = {e: {} for e in self.eng}
        self.dsems = [DSem(stack.enter_context(nc.semaphore("d%d" % i))) for i in range(n_dsem)]
        self.free = list(range(n_dsem))
        self.phase_sems = []

    def buf(self, t, name="", dma=False):
        b = Buf(t, name)
        if dma:
            b.sem = self.free.pop(0)
            self.phase_sems.append(b.sem)
        return b

    def _wait(self, e, key, val):
        if key == ("e", "pe") and e == "pe":
            return
        if self.seen[e].get(key, 0) >= val:
            return
        if key[0] == "e":
            assert val <= self.cnt[key[1]], ("wait on a not-yet-signalled instruction", e, key, val)
        self.seen[e][key] = val
        sem = self.psem[key[1]] if key[0] == "e" else self.dsems[key[1]].sem
        self.eng[e].wait_ge(sem, val)

    def _deps(self, e, reads, writes, join):
        me = ("e", e)
        for b in reads:
            for k, v in b.w.items():
                self._wait(e, k, v)
        for b in writes:
            for k, v in b.w.items():
                if k != me:
                    self._wait(e, k, v)
            for k, v in b.r.items():
                if k != me:
                    self._wait(e, k, v)
        for b in join:
            for k, v in b.w.items():
                if k != me:
                    self._wait(e, k, v)
            for k, v in b.r.items():
                if k != me:
                    self._wait(e, k, v)

    def _upd(self, key, val, reads, writes, join):
        for b in reads:
            b.r[key] = val
        for b in writes:
            b.w = {key: val}
            b.r = {}
        for b in join:
            b.w[key] = val

    def op(self, e, fn, reads=(), writes=(), join=(), inc=True):
        self._deps(e, reads, writes, join)
        ins = fn(self.eng[e])
        if inc:
            self.cnt[e] += 1
            ins.then_inc(self.psem[e], 1)
            self._upd(("e", e), self.cnt[e], reads, writes, join)
        else:
            self._upd(("e", e), self.cnt[e] + 1, reads, writes, join)

    def dma(self, q, out, in_, semb, reads=(), writes=(), join=()):
        self._deps(q, reads, writes, join)
        d = self.dsems[semb.sem]
        d.total += 16
        self.eng[q].dma_start(out=out, in_=in_).then_inc(d.sem, 16)
        self._upd(("d", semb.sem), d.total, reads, writes, join)

    def barrier(self):
        for e in self.eng:
            for e2 in self.psem:
                if self.cnt[e2]:
                    self._wait(e, ("e", e2), self.cnt[e2])
            for i, d in enumerate(self.dsems):
                if d.total:
                    self._wait(e, ("d", i), d.total)
        self.free = self.free + self.phase_sems
        self.phase_sems = []


def bc_last(ap, n):
    s = list(ap.shape)
    return ap.unsqueeze(len(s)).broadcast_to(s + [n])


def bc_mid(ap, n):
    s = list(ap.shape)
    return ap.unsqueeze(1).broadcast_to([s[0], n] + s[1:])


class Prog:
    def __init__(self, cfg, dbg=()):
        self.c = cfg
        self.dbg = set(dbg)
        self.nc = bass.Bass("TRN2", target_bir_lowering=False)
        self.top = contextlib.ExitStack()
        self.fw = None
        self.uid = 0

    def din(self, name, shape, dt=F32):
        return self.nc.dram_tensor(name, list(shape), dt, kind="ExternalInput")

    def dscr(self, name, shape, dt=BF16):
        kind = "ExternalOutput" if name in self.dbg else "Internal"
        return self.nc.dram_tensor(name, list(shape), dt, kind=kind)

    def sb(self, st, shape, dt, name=None, dma=False):
        self.uid += 1
        nm = "%s_%d" % (name or "sb", self.uid)
        t = st.enter_context(self.nc.sbuf_tensor(nm, list(shape), dt))
        return self.fw.buf(t, nm, dma=dma)

    def ps(self, st, shape, dt, name=None):
        self.uid += 1
        nm = "%s_%d" % (name or "ps", self.uid)
        t = st.enter_context(self.nc.psum_tensor(nm, list(shape), dt))
        return self.fw.buf(t, nm)

    def declare(self):
        c = self.c
        D, T, PT, TT, DI, H, CONVC, QL, KVL, MH, DFF = (c[k] for k in
                                                          ("D", "T", "PT", "TT", "DI", "H", "CONVC", "QL", "KVL", "MH", "DFF"))
        DC, CC, QC, FC = D // 128, CONVC // 128, QL // 128, DFF // 128
        i = {}
        i["x_own"] = self.din("x_own", [T, D])
        i["x_pre"] = self.din("x_pre", [PT, D])
        i["pos"] = self.din("pos", [1, TT], I32)
        i["w_in_z"] = self.din("w_in_z", [D, DI])
        i["w_in_xbc"] = self.din("w_in_xbc", [D, CONVC])
        i["w_in_r"] = self.din("w_in_r", [D, H + QL + 2 * D])
        i["w_ckv"] = self.din("w_ckv", [D, KVL + 128])
        i["w_ssm_out"] = self.din("w_ssm_out", [DI, D])
        i["w_qn"] = self.din("w_qn", [QL, MH * 128])
        i["w_qr"] = self.din("w_qr", [QL, MH * 64])
        i["w_kn"] = self.din("w_kn", [KVL, MH * 128])
        i["w_v"] = self.din("w_v", [KVL, MH * 128])
        i["w_mla_out"] = self.din("w_mla_out", [MH * 128, D])
        i["w_out"] = self.din("w_out", [D, D])
        i["w_gate"] = self.din("w_gate", [D, DFF])
        i["w_up"] = self.din("w_up", [D, DFF])
        i["w_down"] = self.din("w_down", [DFF, D])
        self.sp_cols = small_layout(c)
        i["smallp"] = self.din("smallp", [128, self.sp_cols["_n"]])
        i["constb"] = self.din("constb", [128, 3 * 128], BF16)
        i["rowp"] = self.din("rowp", [1, 3 * H + D])
        self.i = i
        self.out = self.nc.dram_tensor("out", [T, D], F32, kind="ExternalOutput")
        s = {}

        class View:
            def __init__(self, a):
                self._a = a

            def ap(self):
                return self._a

        def arena(name, nelem):
            return self.nc.dram_tensor(name, [int(nelem)], BF16, kind="Internal")

        def carve(ar, off, shape, dt=BF16, pat=None):
            n = int(np.prod(shape)) * (2 if dt == F32 else 1)
            a = ar.ap()[off:off + n]
            if dt == F32:
                a = a.bitcast(F32)
            names = " ".join("d%d" % k for k in range(len(shape)))
            kw = {"d%d" % k: int(shape[k]) for k in range(len(shape) - 1)}
            return View(a.rearrange("(%s) -> %s" % (names, names), **kw)), off + n

        MHd = MH * 128
        n_xp = CC * 128 * TT
        n_p4a = MHd * TT * 2 + MHd * T
        n_act = T * FC * 128
        R1 = arena("R1", max(n_xp, n_p4a, n_act))
        n_p45 = (MH // 2) * 128 * T * 2 + T * MHd + D * T * 2
        R2 = arena("R2", max(n_xp, n_p45, T * D + 2 * T * D))
        R3 = arena("R3", max(TT * D, T * DI))
        R4 = arena("R4", max(T * DI, 2 * T * D))
        R5 = arena("R5", 2 * D * T)
        s["xp"], _ = carve(R1, 0, [CC, 128, TT])
        s["kn"], o = carve(R1, 0, [MH, 128, TT])
        s["v"], o = carve(R1, o, [TT, MHd])
        s["qn"], o = carve(R1, o, [MH, 128, T])
        s["act"], _ = carve(R1, 0, [T // 128, 128, FC * 128])
        s["xc"], _ = carve(R2, 0, [TT // 128, 128, CC * 128])
        s["qrr"], o = carve(R2, 0, [MH // 2, 128, T])
        s["qr"], o = carve(R2, o, [MH // 2, 128, T])
        s["attnT"], o = carve(R2, o, [T // 128, 128, MHd])
        s["yg"], o = carve(R2, o, [DC, 128, T])
        s["mg"], o = carve(R2, o, [T // 128, 128, DC * 128])
        s["nT"], o = carve(R2, 0, [T // 128, 128, DC * 128])
        s["h2"], o = carve(R2, o, [T, D], F32)
        s["uT"], _ = carve(R3, 0, [TT // 128, 128, DC * 128])
        s["ynT"], _ = carve(R3, 0, [T // 128, 128, DI])
        s["sz"], _ = carve(R4, 0, [T, DI])
        s["h"], _ = carve(R4, 0, [T, D], F32)
        s["gates"], _ = carve(R5, 0, [2 * DC, 128, T])
        s["dt"] = self.dscr("dt_d", [TT, H], F32)
        s["cqr"] = self.dscr("cqr_d", [QC, 128, T])
        s["cq"] = self.dscr("cq_d", [T // 128, 128, QC * 128])
        s["ckvr"] = self.dscr("ckvr_d", [5, 128, TT])
        s["ckv"] = self.dscr("ckv_d", [TT // 128, 128, 4 * 128])
        s["kr"] = self.dscr("kr_d", [128, TT])
        s["rope"] = self.dscr("rope_d", [2, 128, TT], F32)
        self.s = s

    def load_consts(self):
        fw, st = self.fw, self.top
        n = self.sp_cols["_n"]
        H, D = self.c["H"], self.c["D"]
        self.smallp = self.sb(st, [128, n], F32, "smallp", dma=True)
        self.constb = self.sb(st, [128, 384], BF16, "constb", dma=True)
        self.rowp = self.sb(st, [128, 3 * H], F32, "rowp", dma=True)
        fw.dma("sp", self.smallp.t[:, :], self.i["smallp"].ap(), self.smallp, writes=[self.smallp])
        fw.dma("sp", self.constb.t[:, :], self.i["constb"].ap(), self.constb, writes=[self.constb])
        fw.dma("sp", self.rowp.t[:, :], self.i["rowp"].ap()[0:1, 0:3 * H].broadcast_to([128, 3 * H]), self.rowp,
               writes=[self.rowp])
        self.derived = self.sb(st, [128, H + 1], F32, "derived")
        fw.op("act", lambda e: e.activation(out=self.derived.t[:, 0:H], in_=self.rowp.t[:, H:2 * H], func=AF.Exp),
              reads=[self.rowp], writes=[self.derived])
        fw.op("dve", lambda e: e.tensor_scalar(out=self.derived.t[:, 0:H], in0=self.derived.t[:, 0:H], scalar1=-1.0,
                                               scalar2=None, op0=ALU.mult), reads=[self.derived], join=[self.derived])
        fc = self.sp_cols["flag"]
        fw.op("dve", lambda e: e.tensor_scalar(out=self.derived.t[:, H:H + 1], in0=self.smallp.t[:, fc:fc + 1],
                                               scalar1=-1.0, scalar2=30000.0, op0=ALU.add, op1=ALU.mult),
              reads=[self.smallp, self.derived], join=[self.derived])

    def spc(self, name, j=0, n=1):
        o = self.sp_cols[name] + j
        return self.smallp.t[:, o:o + n]

    def ident(self):
        return self.constb.t[:, 0:128]

    def ones_bf(self):
        return self.constb.t[:, 128:256]

    def perm(self):
        return self.constb.t[:, 256:384]

    def norm_transpose(self, srcs, gname, dst):
        c, fw = self.c, self.fw
        D = c["D"]
        DC = D // 128
        with contextlib.ExitStack() as st:
            xb = [self.sb(st, [128, D], F32, "xb", dma=True) for _ in range(2)]
            junk = self.sb(st, [128, D], BF16, "junk")
            xs = [self.sb(st, [128, D], BF16, "xs") for _ in range(2)]
            ss = [self.sb(st, [128, 2], F32, "ss") for _ in range(2)]
            uT = [self.sb(st, [128, DC, 128], BF16, "uT", dma=True) for _ in range(2)]
            pst = [self.ps(st, [128, 4, 128], BF16, "pst") for _ in range(2)]
            npt = 0
            for j, src in enumerate(srcs):
                x, s_, xs_, u = xb[j % 2], ss[j % 2], xs[j % 2], uT[j % 2]
                fw.dma("sp", x.t[:, :], src, x, writes=[x])
                fw.op("dve", lambda e: e.memset(s_.t[:, 0:2], 0.0), writes=[s_])
                fw.op("act", lambda e: e.activation(out=junk.t[:, :], in_=x.t[:, :], func=AF.Square,
                                                    accum_out=s_.t[:, 0:1]), reads=[x, s_], writes=[junk], join=[s_])
                fw.op("act", lambda e: e.activation(out=s_.t[:, 1:2], in_=s_.t[:, 0:1], func=AF.Ln, scale=1.0 / D,
                                                    bias=self.spc("eps6")), reads=[s_, self.smallp], join=[s_])
                fw.op("act", lambda e: e.activation(out=s_.t[:, 1:2], in_=s_.t[:, 1:2], func=AF.Exp, scale=-0.5),
                      reads=[s_], join=[s_])
                fw.op("dve", lambda e: e.tensor_scalar(out=xs_.t[:, :], in0=x.t[:, :], scalar1=s_.t[:, 1:2],
                                                       scalar2=None, op0=ALU.mult), reads=[x, s_], writes=[xs_])
                first = True
                for c4 in range(0, DC, 4):
                    p = pst[npt % 2]
                    npt += 1
                    nn = min(4, DC - c4)
                    for k in range(nn):
                        cc = c4 + k
                        fw.op("pe", lambda e: e.transpose(out=p.t[:, k, :], in_=xs_.t[:, cc * 128:(cc + 1) * 128],
                                                          identity=self.ident()), reads=[xs_, self.constb],
                              **({"writes": [p]} if k == 0 else {"join": [p]}))
                    for k in range(nn):
                        cc = c4 + k
                        fw.op("act", lambda e: e.activation(out=u.t[:, cc, :], in_=p.t[:, k, :], func=AF.Copy,
                                                            scale=self.spc(gname, cc)), reads=[p, self.smallp],
                              **({"writes": [u]} if first else {"join": [u]}))
                        first = False
                fw.dma("sp", dst[j].rearrange("p (c t) -> p c t", t=128), u.t[:, :, :], u, reads=[u])
            fw.barrier()

    def gemm(self, A_d, tiles, KC, Ws, col0, N, mode, epi, big=True, extra=None):
        c, fw = self.c, self.fw
        nw = len(Ws)
        ntile = len(tiles)
        TGt = 8 if (KC <= 32 and ntile % 8 == 0 and big) else 4
        if ntile % TGt:
            TGt = ntile
        TG = TGt * 128
        NTH = max(1, TG // 512)
        THW = min(TG, 512)
        CW = 128 if nw == 2 else 256
        KS = 32
        nks = (KC + KS - 1) // KS
        with contextlib.ExitStack() as st:
            A = [self.sb(st, [128, TGt, KC, 128], BF16, "A", dma=True) for _ in range(2 if TGt * KC <= 128 else 1)]
            nsl = 3
            slabs = [self.sb(st, [128, min(KS, KC), CW], BF16, "slab", dma=True) for _ in range(3 if nw == 1 else 4)]
            pss = [self.ps(st, [128, 2048], F32, "gps") for _ in range(2)]
            ectx = extra(st) if extra else None
            nsl_i = 0
            nblk = 0
            for gi in range(ntile // TGt):
                tl = tiles[gi * TGt:(gi + 1) * TGt]
                a = A[gi % len(A)]
                fw.dma("sp", a.t[:, :, :, :].rearrange("p j c t -> p j (c t)"),
                       A_d.ap()[tl[0]:tl[0] + TGt].rearrange("j p f -> p j f"), a, writes=[a])
                for c0 in range(0, N, CW):
                    cw = min(CW, N - c0)
                    psb = pss[nblk % 2]
                    nblk += 1
                    firstmm = True
                    for ks in range(nks):
                        kn = min(KS, KC - ks * KS)
                        sl = []
                        for wi in range(nw):
                            s_ = slabs[nsl_i % len(slabs)]
                            nsl_i += 1
                            wv = Ws[wi].ap()[ks * KS * 128:(ks * KS + kn) * 128, col0 + c0:col0 + c0 + cw]
                            fw.dma("pool", s_.t[:, 0:kn, 0:cw], wv.rearrange("(kc p) n -> p kc n", p=128), s_,
                                   writes=[s_])
                            sl.append(s_)
                        for kc in range(kn):
                            kk = ks * KS + kc
                            last = (ks == nks - 1 and kc == kn - 1)
                            first = (ks == 0 and kc == 0)
                            if mode == "F":
                                for wi in range(nw):
                                    for nci in range((cw + 127) // 128):
                                        mc = min(128, cw - nci * 128)
                                        for th in range(NTH):
                                            idx = ((wi if nw == 2 else nci) * NTH + th)
                                            o = psb.t[0:mc, idx * 512:idx * 512 + THW]
                                            l_ = sl[wi].t[:, kc, nci * 128:nci * 128 + mc]
                                            r_ = a.t[:, th * 4:th * 4 + THW // 128, kk, :]
                                            inc_ = (kc == kn - 1 and wi == nw - 1 and nci == (cw + 127) // 128 - 1
                                                    and th == NTH - 1)
                                            fw.op("pe", lambda e: e.matmul(o, l_, r_, start=first, stop=last),
                                                  reads=[a, sl[wi]], inc=inc_,
                                                  **({"writes": [psb]} if firstmm else {"join": [psb]}))
                                            firstmm = False
                            else:
                                for tt in range(TGt):
                                    o = psb.t[:, tt * CW:tt * CW + cw]
                                    l_ = a.t[:, tt, kk, :]
                                    r_ = sl[0].t[:, kc, 0:cw]
                                    st0 = first and ((tt * CW * 4) % 2048 == 0)
                                    fw.op("pe", lambda e: e.matmul(o, l_, r_, start=st0, stop=last, skip_group_check=True),
                                          reads=[a, sl[0]], inc=(kc == kn - 1 and tt == TGt - 1),
                                          **({"writes": [psb]} if firstmm else {"join": [psb]}))
                                    firstmm = False
                    epi(dict(tiles=tl, gi=gi, TGt=TGt, TG=TG, NTH=NTH, THW=THW, CW=CW, c0=c0, cw=cw, ps=psb, st=st,
                             ctx=ectx))
            fw.barrier()

    def ring(self, st, n, shape, dt, name, dma=True):
        bufs = [self.sb(st, shape, dt, name, dma=dma) for _ in range(n)]
        state = {"i": 0}

        def nxt():
            b = bufs[state["i"] % n]
            state["i"] += 1
            return b
        return nxt

    def in_proj(self):
        c, fw, s = self.c, self.fw, self.s
        D, T, PT, TT, DI, H, CONVC, QL = (c[k] for k in ("D", "T", "PT", "TT", "DI", "H", "CONVC", "QL"))
        DC = D // 128
        allt = list(range(TT // 128))
        own = list(range(PT // 128, TT // 128))
        npre = PT // 128
        Wz, Wx, Wr = self.i["w_in_z"], self.i["w_in_xbc"], self.i["w_in_r"]

        def ex_z(st):
            return self.ring(st, 2, [128, 8, 256], BF16, "stz")

        def epi_z(k):
            b = k["ctx"]()
            for tt in range(k["TGt"]):
                fw.op("act", lambda e: e.activation(out=b.t[:, tt, 0:k["cw"]],
                                                    in_=k["ps"].t[:, tt * k["CW"]:tt * k["CW"] + k["cw"]], func=AF.Silu),
                      reads=[k["ps"]], **({"writes": [b]} if tt == 0 else {"join": [b]}))
            r0 = (k["tiles"][0] - npre) * 128
            fw.dma("sp", s["sz"].ap()[r0:r0 + k["TG"], k["c0"]:k["c0"] + k["cw"]].rearrange("(j p) n -> p j n", p=128),
                   b.t[:, 0:k["TGt"], 0:k["cw"]], b, reads=[b])
        self.gemm(s["uT"], own, DC, [Wz], 0, DI, "T", epi_z, extra=ex_z)

        def mk_epiF(dst, tok_off, func=None, bias_name=None, ch_off=0):
            def ex(st):
                return self.ring(st, 2, [128, 4, 1024], BF16, "stF")

            def epi(k):
                b = k["ctx"]()
                nci_n = (k["cw"] + 127) // 128
                firstw = True
                for nci in range(nci_n):
                    mc = min(128, k["cw"] - nci * 128)
                    ch = (k["c0"] // 128) + nci
                    for th in range(k["NTH"]):
                        idx = nci * k["NTH"] + th
                        src = k["ps"].t[0:mc, idx * 512:idx * 512 + k["THW"]]
                        dstv = b.t[0:mc, nci, th * 512:th * 512 + k["THW"]]
                        if func is None:
                            eng = "dve" if (idx % 2) else "act"
                            if eng == "act":
                                fw.op("act", lambda e: e.activation(out=dstv, in_=src, func=AF.Copy), reads=[k["ps"]],
                                      **({"writes": [b]} if firstw else {"join": [b]}))
                            else:
                                fw.op("dve", lambda e: e.tensor_copy(out=dstv, in_=src), reads=[k["ps"]],
                                      **({"writes": [b]} if firstw else {"join": [b]}))
                        else:
                            fw.op("act", lambda e: e.activation(out=dstv, in_=src, func=func,
                                                                bias=self.spc(bias_name, ch)[0:mc, :]),
                                  reads=[k["ps"], self.smallp], **({"writes": [b]} if firstw else {"join": [b]}))
                        firstw = False
                t0 = k["tiles"][0] * 128 - tok_off
                for nci in range(nci_n):
                    mc = min(128, k["cw"] - nci * 128)
                    ch = (k["c0"] // 128) + nci + ch_off
                    fw.dma("sp", dst.ap()[ch, 0:mc, t0:t0 + k["TG"]], b.t[0:mc, nci, 0:k["TG"]], b, reads=[b])
            return ex, epi

        ex, epi = mk_epiF(s["xp"], 0)
        self.gemm(s["uT"], allt, DC, [Wx], 0, CONVC, "F", epi, extra=ex)
        ex, epi = mk_epiF(s["cqr"], PT)
        self.gemm(s["uT"], own, DC, [Wr], H, QL, "F", epi, extra=ex)
        ex, epi = mk_epiF(s["ckvr"], 0)
        self.gemm(s["uT"], allt, DC, [self.i["w_ckv"]], 0, 640, "F", epi, extra=ex)
        ex, epi = mk_epiF(s["gates"], PT, func=AF.Sigmoid, bias_name="gate_bias")
        self.gemm(s["uT"], own, DC, [Wr], H + QL, 2 * D, "F", epi, extra=ex)

        def ex_dt(st):
            return (self.ring(st, 2, [128, 8, H], F32, "stdt"), self.sb(st, [128, H], F32, "dtt"))

        def epi_dt(k):
            nxt, tmp = k["ctx"]
            b = nxt()
            for tt in range(k["TGt"]):
                src = k["ps"].t[:, tt * k["CW"]:tt * k["CW"] + H]
                fw.op("dve", lambda e: e.tensor_tensor(out=tmp.t[:, :], in0=src, in1=self.rowp.t[:, 0:H], op=ALU.add),
                      reads=[k["ps"], self.rowp], writes=[tmp])
                fw.op("act", lambda e: e.activation(out=tmp.t[:, :], in_=tmp.t[:, :], func=AF.Exp), reads=[tmp],
                      join=[tmp])
                fw.op("act", lambda e: e.activation(out=b.t[:, tt, :], in_=tmp.t[:, :], func=AF.Ln, bias=self.spc("one")),
                      reads=[tmp], **({"writes": [b]} if tt == 0 else {"join": [b]}))
                if k["tiles"][tt] < npre:
                    fw.op("dve", lambda e: e.tensor_scalar(out=b.t[:, tt, :], in0=b.t[:, tt, :],
                                                           scalar1=self.spc("flag"), scalar2=None, op0=ALU.mult),
                          reads=[b, self.smallp], join=[b])
            r0 = k["tiles"][0] * 128
            fw.dma("sp", s["dt"].ap()[r0:r0 + k["TG"], :].rearrange("(j p) n -> p j n", p=128),
                   b.t[:, 0:k["TGt"], :], b, reads=[b])
        self.gemm(s["uT"], allt, DC, [Wr], 0, H, "T", epi_dt, extra=ex_dt)

    def conv_phase(self):
        c, fw, s = self.c, self.fw, self.s
        TT, CONVC = c["TT"], c["CONVC"]
        CC = CONVC // 128
        NJ = TT // 128
        CG = 4
        with contextlib.ExitStack() as st:
            xin = [self.sb(st, [128, CG, TT + 4], BF16, "cin", dma=True) for _ in range(2)]
            acc = [self.sb(st, [128, TT], F32, "cacc") for _ in range(2)]
            xo = [self.sb(st, [128, NJ, CG, 128], BF16, "cout", dma=True) for _ in range(2)]
            for b in xin:
                fw.op("pool", lambda e: e.memset(b.t[:, :, 0:4], 0.0), writes=[b])
            for gi in range(CC // CG):
                xi, o = xin[gi % 2], xo[gi % 2]
                fw.dma("sp", xi.t[:, :, 4:4 + TT], s["xp"].ap()[gi * CG:(gi + 1) * CG].rearrange("c p t -> p c t"), xi,
                       join=[xi])
                for k in range(CG):
                    ch = gi * CG + k
                    a = acc[k % 2]
                    eng = "dve"
                    fw.op(eng, lambda e: e.tensor_scalar(out=a.t[:, :], in0=xi.t[:, k, 4:4 + TT],
                                                         scalar1=self.spc("conv_w", ch * 4 + 3),
                                                         scalar2=self.spc("conv_b", ch), op0=ALU.mult, op1=ALU.add),
                          reads=[xi, self.smallp], writes=[a])
                    for j in range(3):
                        sh = 3 - j
                        fw.op(eng, lambda e: e.scalar_tensor_tensor(out=a.t[:, :], in0=xi.t[:, k, 4 - sh:4 - sh + TT],
                                                                    scalar=self.spc("conv_w", ch * 4 + j), in1=a.t[:, :],
                                                                    op0=ALU.mult, op1=ALU.add),
                              reads=[xi, a, self.smallp], join=[a])
                    fw.op("act", lambda e: e.activation(out=o.t[:, :, k, :], in_=a.t[:, :].rearrange("p (j t) -> p j t", t=128),
                                                        func=AF.Silu), reads=[a],
                          **({"writes": [o]} if k == 0 else {"join": [o]}))
                fw.dma("sp", s["xc"].ap().rearrange("j p (c t) -> p j c t", t=128)[:, :, gi * CG:(gi + 1) * CG, :],
                       o.t[:, :, :, :], o, reads=[o])
            fw.barrier()

    def rope_tables(self):
        c, fw, s = self.c, self.fw, self.s
        TT = c["TT"]
        with contextlib.ExitStack() as st:
            pi_ = self.sb(st, [128, TT], I32, "posi", dma=True)
            ang = self.sb(st, [128, TT], F32, "ang")
            r = self.sb(st, [128, TT], F32, "rr")
            yv = self.sb(st, [128, TT], F32, "yv")
            tb = [self.sb(st, [128, TT], F32, "ropet", dma=True) for _ in range(2)]
            fw.dma("sp", pi_.t[:, :], self.i["pos"].ap()[0:1, :].broadcast_to([128, TT]), pi_, writes=[pi_])
            fw.op("dve", lambda e: e.tensor_copy(out=ang.t[:, :], in_=pi_.t[:, :]), reads=[pi_], writes=[ang])
            fw.op("dve", lambda e: e.tensor_scalar(out=ang.t[:, :], in0=ang.t[:, :], scalar1=self.spc("invf"),
                                                   scalar2=None, op0=ALU.mult), reads=[ang, self.smallp], join=[ang])
            for which, sh in ((0, 1.5 * PI), (1, PI)):
                fw.op("dve", lambda e: e.tensor_scalar(out=yv.t[:, :], in0=ang.t[:, :], scalar1=sh, scalar2=None,
                                                       op0=ALU.add), reads=[ang], writes=[yv])
                fw.op("dve", lambda e: e.tensor_scalar(out=r.t[:, :], in0=yv.t[:, :], scalar1=1.0 / (2 * PI), scalar2=None,
                                                       op0=ALU.mult), reads=[yv], writes=[r])
                fw.op("dve", lambda e: e.tensor_copy(out=pi_.t[:, :], in_=r.t[:, :]), reads=[r], writes=[pi_])
                fw.op("dve", lambda e: e.tensor_copy(out=r.t[:, :], in_=pi_.t[:, :]), reads=[pi_], writes=[r])
                fw.op("dve", lambda e: e.scalar_tensor_tensor(out=yv.t[:, :], in0=r.t[:, :], scalar=-2 * PI, in1=yv.t[:, :],
                                                              op0=ALU.mult, op1=ALU.add), reads=[r, yv], join=[yv])
                fw.op("dve", lambda e: e.tensor_scalar(out=r.t[:, :], in0=yv.t[:, :], scalar1=0.0, scalar2=2 * PI,
                                                       op0=ALU.is_lt, op1=ALU.mult), reads=[yv], writes=[r])
                fw.op("dve", lambda e: e.scalar_tensor_tensor(out=r.t[:, :], in0=yv.t[:, :], scalar=-PI, in1=r.t[:, :],
                                                              op0=ALU.add, op1=ALU.add), reads=[yv, r], join=[r])
                fw.op("dve", lambda e: e.tensor_scalar(out=r.t[:, :], in0=r.t[:, :], scalar1=-3.1415925, scalar2=3.1415925,
                                                       op0=ALU.max, op1=ALU.min), reads=[r], join=[r])
                if which == 0:
                    fw.op("act", lambda e: e.activation(out=tb[0].t[:, :], in_=r.t[:, :], func=AF.Sin), reads=[r],
                          writes=[tb[0]])
                else:
                    fw.op("act", lambda e: e.activation(out=tb[1].t[:, :], in_=r.t[:, :], func=AF.Sin,
                                                        scale=self.spc("sgn")), reads=[r, self.smallp], writes=[tb[1]])
                fw.dma("sp", s["rope"].ap()[which], tb[which].t[:, :], tb[which], reads=[tb[which]])
            fw.barrier()

    def apply_rope(self, st, src_sb, n, cos_sb, sin_sb, t0, out_sb, pr, tmp):
        fw = self.fw
        for o in range(0, n, 512):
            w = min(512, n - o)
            fw.op("pe", lambda e: e.matmul(pr.t[:, 0:w], self.perm(), src_sb.t[:, o:o + w], start=True, stop=True),
                  reads=[src_sb, self.constb], writes=[pr])
            fw.op("dve", lambda e: e.tensor_tensor(out=tmp.t[:, 0:w], in0=pr.t[:, 0:w], in1=sin_sb.t[:, t0 + o:t0 + o + w],
                                                   op=ALU.mult), reads=[pr, sin_sb], writes=[tmp])
            fw.op("pool", lambda e: e.tensor_tensor(out=out_sb.t[:, o:o + w], in0=src_sb.t[:, o:o + w],
                                                    in1=cos_sb.t[:, t0 + o:t0 + o + w], op=ALU.mult),
                  reads=[src_sb, cos_sb], **({"writes": [out_sb]} if o == 0 else {"join": [out_sb]}))
            fw.op("pool", lambda e: e.tensor_tensor(out=out_sb.t[:, o:o + w], in0=out_sb.t[:, o:o + w], in1=tmp.t[:, 0:w],
                                                    op=ALU.add), reads=[out_sb, tmp], join=[out_sb])

    def latent_norm(self, raw, nch, ntok, gname, dst, feat, rope_extra):
        c, fw, s = self.c, self.fw, self.s
        with contextlib.ExitStack() as st:
            xin = [self.sb(st, [128, nch, 512], BF16, "lin", dma=True) for _ in range(2)]
            sq = self.sb(st, [128, nch, 512], BF16, "lsq")
            rs = self.sb(st, [128, 512], F32, "lrs")
            xo = [self.sb(st, [128, 4, nch, 128], BF16, "lout", dma=True) for _ in range(2)]
            pss = self.ps(st, [128, 512], F32, "lps")
            if rope_extra:
                cos_sb = self.sb(st, [128, ntok], F32, "cos", dma=True)
                sin_sb = self.sb(st, [128, ntok], F32, "sin", dma=True)
                fw.dma("sp", cos_sb.t[:, :], s["rope"].ap()[0], cos_sb, writes=[cos_sb])
                fw.dma("sp", sin_sb.t[:, :], s["rope"].ap()[1], sin_sb, writes=[sin_sb])
                krin = [self.sb(st, [128, 512], BF16, "krin", dma=True) for _ in range(2)]
                krout = [self.sb(st, [128, 512], BF16, "krout", dma=True) for _ in range(2)]
                pr = self.ps(st, [128, 512], F32, "prp")
                rtmp = self.sb(st, [128, 512], F32, "rtmp")
            for gi in range(ntok // 512):
                xi, o = xin[gi % 2], xo[gi % 2]
                fw.dma("sp", xi.t[:, :, :], raw.ap()[0:nch, :, gi * 512:(gi + 1) * 512].rearrange("c p t -> p c t"), xi,
                       writes=[xi])
                fw.op("dve", lambda e: e.tensor_tensor(out=sq.t[:, :, :], in0=xi.t[:, :, :], in1=xi.t[:, :, :], op=ALU.mult),
                      reads=[xi], writes=[sq])
                for k in range(nch):
                    fw.op("pe", lambda e: e.matmul(pss.t[:, :], self.ones_bf(), sq.t[:, k, :], start=(k == 0),
                                                   stop=(k == nch - 1)), reads=[sq, self.constb],
                          **({"writes": [pss]} if k == 0 else {"join": [pss]}))
                fw.op("act", lambda e: e.activation(out=rs.t[:, :], in_=pss.t[:, :], func=AF.Ln, scale=1.0 / feat,
                                                    bias=self.spc("eps6")), reads=[pss, self.smallp], writes=[rs])
                fw.op("act", lambda e: e.activation(out=rs.t[:, :], in_=rs.t[:, :], func=AF.Exp, scale=-0.5), reads=[rs],
                      join=[rs])
                for k in range(nch):
                    eng = "dve"
                    fw.op(eng, lambda e: e.scalar_tensor_tensor(out=o.t[:, :, k, :],
                                                                in0=xi.t[:, k, :].rearrange("p (j t) -> p j t", t=128),
                                                                scalar=self.spc(gname, k),
                                                                in1=rs.t[:, :].rearrange("p (j t) -> p j t", t=128),
                                                                op0=ALU.mult, op1=ALU.mult),
                          reads=[xi, rs, self.smallp], **({"writes": [o]} if k == 0 else {"join": [o]}))
                fw.dma("sp", dst.ap()[gi * 4:(gi + 1) * 4].rearrange("j p (c t) -> p j c t", t=128), o.t[:, :, :, :], o,
                       reads=[o])
                if rope_extra:
                    ki, ko = krin[gi % 2], krout[gi % 2]
                    fw.dma("sp", ki.t[:, :], raw.ap()[4, :, gi * 512:(gi + 1) * 512], ki, writes=[ki])
                    self.apply_rope(st, ki, 512, cos_sb, sin_sb, gi * 512, ko, pr, rtmp)
                    fw.dma("sp", s["kr"].ap()[:, gi * 512:(gi + 1) * 512], ko.t[:, :], ko, reads=[ko])
            fw.barrier()

    def ssd(self):
        c, fw, s = self.c, self.fw, self.s
        T, PT, TT, DI, H, R, CONVC = (c[k] for k in ("T", "PT", "TT", "DI", "H", "R", "CONVC"))
        CC = CONVC // 128
        XC = DI // 128
        GW = R * 64
        GCH = GW // 128
        PW = min(512, GW)
        NPC = GW // PW
        HS = min(4, R)
        npre = PT // 128
        tri = lambda: self.spc("tri", 0, 128)
        ntri = lambda: self.spc("ntri", 0, 128)
        onesf = lambda: self.spc("onesf", 0, 128)
        Abc = lambda: self.derived.t[:, 0:H]
        dskip = lambda: self.rowp.t[:, 2 * H:3 * H]
        with contextlib.ExitStack() as st:
            sb, ps = (lambda sh, dt, nm, dma=False: self.sb(st, sh, dt, nm, dma=dma)), (lambda sh, dt, nm: self.ps(st, sh, dt, nm))
            xc = [sb([128, CC, 128], BF16, "xc", True) for _ in range(2)]
            dtb = [sb([128, H], F32, "dt", True) for _ in range(2)]
            szb = [sb([128, GW], BF16, "sz", True) for _ in range(2)]
            ynT = [sb([128, XC, 128], BF16, "ynT", True) for _ in range(2)]
            state = sb([128, DI], F32, "state")
            stbf = sb([128, DI], BF16, "stbf")
            a_sb = sb([128, H], F32, "a")
            acs_sb = sb([128, H], F32, "acs")
            eacs = sb([128, H], F32, "eacs")
            eL = sb([128, H], F32, "eL")
            wd = sb([128, H], F32, "wd")
            dtw = sb([128, H], F32, "dtw")
            xtok = sb([128, GW], BF16, "xtok")
            xdt = sb([128, GW], BF16, "xdt")
            xdtw = sb([128, GW], BF16, "xdtw")
            btok = sb([128, 128], BF16, "btok")
            cbm = sb([128, 128], BF16, "cbm")
            Yg = sb([128, HS, 128], F32, "Yg")
            Eg = sb([128, HS, 128], BF16, "Eg")
            Mg = sb([128, HS, 128], BF16, "Mg")
            t1 = sb([128, GW], F32, "t1")
            t2 = sb([128, GW], F32, "t2")
            yn = sb([128, GW], BF16, "yn")
            junk = sb([128, GW], BF16, "junk")
            ssq = sb([128, 2], F32, "ssq")
            psA = ps([128, 2 * H], F32, "psA")
            psX = ps([128, GW], BF16, "psX")
            psBC = ps([128, 512], F32, "psBC")
            psBt = ps([128, 128], BF16, "psBt")
            psY2 = ps([128, PW], F32, "psY2")
            psS = ps([128, PW], F32, "psS")
            psD = ps([128, HS * 128], F32, "psD")
            psY1 = ps([128, PW], F32, "psY1")
            fw.op("dve", lambda e: e.memset(state.t[:, :], 0.0), writes=[state])
            fw.op("pool", lambda e: e.memset(stbf.t[:, :], 0.0), writes=[stbf])
            for j in range(TT // 128):
                own = j >= npre
                jo = j - npre
                x, d_, yo = xc[j % 2], dtb[j % 2], ynT[j % 2]
                fw.dma("sp", x.t[:, :, :], s["xc"].ap()[j].rearrange("p (c t) -> p c t", t=128), x, writes=[x])
                fw.dma("sp", d_.t[:, :], s["dt"].ap()[j * 128:(j + 1) * 128, :], d_, writes=[d_])
                fw.op("dve", lambda e: e.tensor_tensor(out=a_sb.t[:, :], in0=d_.t[:, :], in1=Abc(), op=ALU.mult),
                      reads=[d_, self.derived], writes=[a_sb])
                fw.op("pe", lambda e: e.matmul(psA.t[:, 0:H], tri(), a_sb.t[:, :], start=True, stop=True),
                      reads=[a_sb, self.smallp], writes=[psA])
                fw.op("pe", lambda e: e.matmul(psA.t[:, H:2 * H], onesf(), a_sb.t[:, :], start=True, stop=True),
                      reads=[a_sb, self.smallp], join=[psA])
                fw.op("act", lambda e: e.activation(out=acs_sb.t[:, :], in_=psA.t[:, 0:H], func=AF.Copy), reads=[psA],
                      writes=[acs_sb])
                fw.op("act", lambda e: e.activation(out=eL.t[:, :], in_=psA.t[:, H:2 * H], func=AF.Exp), reads=[psA],
                      writes=[eL])
                fw.op("dve", lambda e: e.tensor_tensor(out=wd.t[:, :], in0=psA.t[:, H:2 * H], in1=acs_sb.t[:, :],
                                                       op=ALU.subtract), reads=[psA, acs_sb], writes=[wd])
                fw.op("act", lambda e: e.activation(out=wd.t[:, :], in_=wd.t[:, :], func=AF.Exp), reads=[wd], join=[wd])
                fw.op("dve", lambda e: e.tensor_tensor(out=dtw.t[:, :], in0=wd.t[:, :], in1=d_.t[:, :], op=ALU.mult),
                      reads=[wd, d_], writes=[dtw])
                if own:
                    fw.op("act", lambda e: e.activation(out=eacs.t[:, :], in_=acs_sb.t[:, :], func=AF.Exp),
                          reads=[acs_sb], writes=[eacs])
                for g in range(8):
                    h0 = g * R
                    gc0 = g * GW
                    bch = XC + g
                    cch = XC + 8 + g
                    for i in range(GCH):
                        fw.op("pe", lambda e: e.transpose(out=psX.t[:, i * 128:(i + 1) * 128], in_=x.t[:, g * GCH + i, :],
                                                          identity=self.ident()), reads=[x, self.constb],
                              **({"writes": [psX]} if i == 0 else {"join": [psX]}))
                    fw.op("pe", lambda e: e.transpose(out=psBt.t[:, :], in_=x.t[:, bch, :], identity=self.ident()),
                          reads=[x, self.constb], writes=[psBt])
                    fw.op("act", lambda e: e.activation(out=xtok.t[:, :], in_=psX.t[:, :], func=AF.Copy), reads=[psX],
                          writes=[xtok])
                    fw.op("act", lambda e: e.activation(out=btok.t[:, :], in_=psBt.t[:, :], func=AF.Copy), reads=[psBt],
                          writes=[btok])
                    fw.op("pool", lambda e: e.tensor_tensor(out=xdtw.t[:, :].rearrange("p (h d) -> p h d", d=64),
                                                            in0=xtok.t[:, :].rearrange("p (h d) -> p h d", d=64),
                                                            in1=bc_last(dtw.t[:, h0:h0 + R], 64), op=ALU.mult),
                          reads=[xtok, dtw], writes=[xdtw])
                    if own:
                        fw.op("dve", lambda e: e.tensor_tensor(out=xdt.t[:, :].rearrange("p (h d) -> p h d", d=64),
                                                               in0=xtok.t[:, :].rearrange("p (h d) -> p h d", d=64),
                                                               in1=bc_last(d_.t[:, h0:h0 + R], 64), op=ALU.mult),
                              reads=[xtok, d_], writes=[xdt])
                        fw.op("pe", lambda e: e.matmul(psBC.t[:, 0:128], x.t[:, bch, :], x.t[:, cch, :], start=True,
                                                       stop=True), reads=[x], writes=[psBC])
                        fw.op("dve", lambda e: e.tensor_tensor(out=cbm.t[:, :], in0=psBC.t[:, 0:128], in1=tri(),
                                                               op=ALU.mult), reads=[psBC, self.smallp], writes=[cbm])
                        fw.op("pool", lambda e: e.memset(ssq.t[:, :], 0.0), writes=[ssq])
                        fw.dma("sp", szb[g % 2].t[:, :], s["sz"].ap()[jo * 128:(jo + 1) * 128, gc0:gc0 + GW], szb[g % 2],
                               writes=[szb[g % 2]])
                    for pc in range(NPC):
                        c0 = pc * PW
                        hp0 = h0 + c0 // 64
                        nhp = PW // 64
                        if own:
                            fw.op("pe", lambda e: e.matmul(psY2.t[:, :], x.t[:, cch, :], stbf.t[:, gc0 + c0:gc0 + c0 + PW],
                                                           start=True, stop=True), reads=[x, stbf], writes=[psY2])
                            for sg in range(nhp // HS):
                                hh0 = hp0 + sg * HS
                                fw.op("dve", lambda e: e.tensor_tensor(out=Yg.t[:, :, :], in0=bc_mid(tri(), HS),
                                                                       in1=bc_last(a_sb.t[:, hh0:hh0 + HS], 128),
                                                                       op=ALU.mult), reads=[a_sb, self.smallp],
                                      writes=[Yg])
                                fw.op("pe", lambda e: e.matmul(psD.t[:, :], ntri(),
                                                               Yg.t[:, :, :].rearrange("p h l -> p (h l)"), start=True,
                                                               stop=True), reads=[Yg, self.smallp], writes=[psD])
                                fw.op("act", lambda e: e.activation(out=Eg.t[:, :, :].rearrange("p h l -> p (h l)"),
                                                                    in_=psD.t[:, :], func=AF.Exp), reads=[psD],
                                      writes=[Eg])
                                fw.op("pool", lambda e: e.tensor_tensor(out=Mg.t[:, :, :], in0=Eg.t[:, :, :],
                                                                        in1=bc_mid(cbm.t[:, :], HS), op=ALU.mult),
                                      reads=[Eg, cbm], writes=[Mg])
                                for hi in range(HS):
                                    lc = (sg * HS + hi) * 64
                                    fw.op("pe", lambda e: e.matmul(psY1.t[:, lc:lc + 64], Mg.t[:, hi, :],
                                                                   xdt.t[:, c0 + lc:c0 + lc + 64], start=True, stop=True),
                                          reads=[Mg, xdt], **({"writes": [psY1]} if (sg == 0 and hi == 0) else {"join": [psY1]}))
                            v3 = lambda ap: ap.rearrange("p (h d) -> p h d", d=64)
                            fw.op("dve", lambda e: e.tensor_tensor(out=v3(t1.t[:, c0:c0 + PW]), in0=v3(psY2.t[:, :]),
                                                                   in1=bc_last(eacs.t[:, hp0:hp0 + nhp], 64), op=ALU.mult),
                                  reads=[psY2, eacs], **({"writes": [t1]} if pc == 0 else {"join": [t1]}))
                            fw.op("dve", lambda e: e.tensor_tensor(out=t1.t[:, c0:c0 + PW], in0=t1.t[:, c0:c0 + PW],
                                                                   in1=psY1.t[:, :], op=ALU.add), reads=[t1, psY1],
                                  join=[t1])
                        fw.op("pe", lambda e: e.matmul(psS.t[:, :], btok.t[:, :], xdtw.t[:, c0:c0 + PW], start=True,
                                                       stop=True), reads=[btok, xdtw], writes=[psS])
                        sv = state.t[:, gc0 + c0:gc0 + c0 + PW]
                        fw.op("pool", lambda e: e.tensor_tensor(out=sv.rearrange("p (h d) -> p h d", d=64),
                                                                in0=sv.rearrange("p (h d) -> p h d", d=64),
                                                                in1=bc_last(eL.t[:, hp0:hp0 + nhp], 64), op=ALU.mult),
                              reads=[state, eL, stbf], join=[state])
                        fw.op("dve", lambda e: e.tensor_tensor(out=sv, in0=sv, in1=psS.t[:, :], op=ALU.add),
                              reads=[state, psS], join=[state])
                        fw.op("act", lambda e: e.activation(out=stbf.t[:, gc0 + c0:gc0 + c0 + PW], in_=sv, func=AF.Copy),
                              reads=[state], join=[stbf])
                    if own:
                        z = szb[g % 2]
                        fw.op("pool", lambda e: e.tensor_tensor(out=t2.t[:, :].rearrange("p (h d) -> p h d", d=64),
                                                                in0=xtok.t[:, :].rearrange("p (h d) -> p h d", d=64),
                                                                in1=bc_last(dskip()[:, h0:h0 + R], 64), op=ALU.mult),
                              reads=[xtok, self.rowp], writes=[t2])
                        fw.op("dve", lambda e: e.tensor_tensor(out=t1.t[:, :], in0=t1.t[:, :], in1=t2.t[:, :], op=ALU.add),
                              reads=[t1, t2], join=[t1])
                        fw.op("dve", lambda e: e.tensor_tensor(out=t1.t[:, :], in0=t1.t[:, :], in1=z.t[:, :], op=ALU.mult),
                              reads=[t1, z], join=[t1])
                        fw.op("act", lambda e: e.activation(out=junk.t[:, :], in_=t1.t[:, :], func=AF.Square,
                                                            accum_out=ssq.t[:, 0:1]), reads=[t1, ssq], writes=[junk],
                              join=[ssq])
                        fw.op("act", lambda e: e.activation(out=ssq.t[:, 1:2], in_=ssq.t[:, 0:1], func=AF.Ln, scale=1.0 / GW,
                                                            bias=self.spc("eps5")), reads=[ssq, self.smallp], join=[ssq])
                        fw.op("act", lambda e: e.activation(out=ssq.t[:, 1:2], in_=ssq.t[:, 1:2], func=AF.Exp, scale=-0.5),
                              reads=[ssq], join=[ssq])
                        fw.op("dve", lambda e: e.tensor_scalar(out=yn.t[:, :], in0=t1.t[:, :], scalar1=ssq.t[:, 1:2],
                                                               scalar2=None, op0=ALU.mult), reads=[t1, ssq], writes=[yn])
                        for i in range(GCH):
                            ch = g * GCH + i
                            fw.op("pe", lambda e: e.transpose(out=psX.t[:, i * 128:(i + 1) * 128],
                                                              in_=yn.t[:, i * 128:(i + 1) * 128], identity=self.ident()),
                                  reads=[yn, self.constb], **({"writes": [psX]} if i == 0 else {"join": [psX]}))
                        for i in range(GCH):
                            ch = g * GCH + i
                            fw.op("act", lambda e: e.activation(out=yo.t[:, ch, :], in_=psX.t[:, i * 128:(i + 1) * 128],
                                                                func=AF.Copy, scale=self.spc("ssm_g", ch)),
                                  reads=[psX, self.smallp], **({"writes": [yo]} if (g == 0 and i == 0) else {"join": [yo]}))
                if own:
                    fw.dma("sp", s["ynT"].ap()[jo].rearrange("p (c t) -> p c t", t=128), yo.t[:, :, :], yo, reads=[yo])
            fw.barrier()

    def mla(self):
        c, fw, s = self.c, self.fw, self.s
        T, PT, TT, MH, QL, D = (c[k] for k in ("T", "PT", "TT", "MH", "QL", "D"))
        QC = QL // 128
        npre = PT // 128
        allt = list(range(TT // 128))
        ownq = list(range(T // 128))

        def mkF(dst, tok_off):
            def ex(st):
                return self.ring(st, 2, [128, 4, 1024], BF16, "stF")

            def epi(k):
                b = k["ctx"]()
                nci_n = k["cw"] // 128
                fz = True
                for nci in range(nci_n):
                    for th in range(k["NTH"]):
                        idx = nci * k["NTH"] + th
                        src = k["ps"].t[:, idx * 512:idx * 512 + k["THW"]]
                        dv = b.t[:, nci, th * 512:th * 512 + k["THW"]]
                        if idx % 2:
                            fw.op("dve", lambda e: e.tensor_copy(out=dv, in_=src), reads=[k["ps"]],
                                  **({"writes": [b]} if fz else {"join": [b]}))
                        else:
                            fw.op("act", lambda e: e.activation(out=dv, in_=src, func=AF.Copy), reads=[k["ps"]],
                                  **({"writes": [b]} if fz else {"join": [b]}))
                        fz = False
                t0 = k["tiles"][0] * 128 - tok_off
                for nci in range(nci_n):
                    fw.dma("sp", dst.ap()[k["c0"] // 128 + nci, :, t0:t0 + k["TG"]], b.t[:, nci, 0:k["TG"]], b, reads=[b])
            return ex, epi

        ex, epi = mkF(s["kn"], 0)
        self.gemm(s["ckv"], allt, 4, [self.i["w_kn"]], 0, MH * 128, "F", epi, extra=ex)

        def ex_v(st):
            return self.ring(st, 2, [128, 8, 256], BF16, "stv")

        def epi_v(k):
            b = k["ctx"]()
            for tt in range(k["TGt"]):
                src = k["ps"].t[:, tt * k["CW"]:tt * k["CW"] + k["cw"]]
                if tt % 2:
                    fw.op("dve", lambda e: e.tensor_copy(out=b.t[:, tt, 0:k["cw"]], in_=src), reads=[k["ps"]],
                          **({"writes": [b]} if tt == 0 else {"join": [b]}))
                else:
                    fw.op("act", lambda e: e.activation(out=b.t[:, tt, 0:k["cw"]], in_=src, func=AF.Copy),
                          reads=[k["ps"]], **({"writes": [b]} if tt == 0 else {"join": [b]}))
            r0 = k["tiles"][0] * 128
            fw.dma("sp", s["v"].ap()[r0:r0 + k["TG"], k["c0"]:k["c0"] + k["cw"]].rearrange("(j p) n -> p j n", p=128),
                   b.t[:, 0:k["TGt"], 0:k["cw"]], b, reads=[b])
        self.gemm(s["ckv"], allt, 4, [self.i["w_v"]], 0, MH * 128, "T", epi_v, extra=ex_v)
        ex, epi = mkF(s["qn"], 0)
        self.gemm(s["cq"], ownq, QC, [self.i["w_qn"]], 0, MH * 128, "F", epi, extra=ex)
        ex, epi = mkF(s["qrr"], 0)
        self.gemm(s["cq"], ownq, QC, [self.i["w_qr"]], 0, MH * 64, "F", epi, extra=ex)
        with contextlib.ExitStack() as st:
            cos_sb = self.sb(st, [128, T], F32, "cos", dma=True)
            sin_sb = self.sb(st, [128, T], F32, "sin", dma=True)
            fw.dma("sp", cos_sb.t[:, :], s["rope"].ap()[0, :, PT:TT], cos_sb, writes=[cos_sb])
            fw.dma("sp", sin_sb.t[:, :], s["rope"].ap()[1, :, PT:TT], sin_sb, writes=[sin_sb])
            qi = [self.sb(st, [128, T], BF16, "qri", dma=True) for _ in range(2)]
            qo = [self.sb(st, [128, T], BF16, "qro", dma=True) for _ in range(2)]
            pr = self.ps(st, [128, 512], F32, "prq")
            rtmp = self.sb(st, [128, 512], F32, "rtmpq")
            for hp in range(MH // 2):
                a, o = qi[hp % 2], qo[hp % 2]
                fw.dma("sp", a.t[:, :], s["qrr"].ap()[hp], a, writes=[a])
                self.apply_rope(st, a, T, cos_sb, sin_sb, 0, o, pr, rtmp)
                fw.dma("sp", s["qr"].ap()[hp], o.t[:, :], o, reads=[o])
            fw.barrier()
        scale = float((128 + 64) ** -0.5)
        NK = TT // 128
        NQS = T // 512
        with contextlib.ExitStack() as st:
            kr = self.sb(st, [128, TT], BF16, "kr", dma=True)
            fw.dma("sp", kr.t[:, :], s["kr"].ap(), kr, writes=[kr])
            kn = [self.sb(st, [128, TT], BF16, "kn", dma=True) for _ in range(2)]
            qn = [self.sb(st, [128, T], BF16, "qn", dma=True) for _ in range(2)]
            qr = [self.sb(st, [128, T], BF16, "qr", dma=True) for _ in range(2)]
            vv = [self.sb(st, [128, NK, 128], BF16, "vv", dma=True) for _ in range(2)]
            pt = [self.sb(st, [128, 512], BF16, "pt") for _ in range(3)]
            psS = [self.ps(st, [128, 512], F32, "psS") for _ in range(2)]
            acc = [self.ps(st, [128, 4, 256], F32, "acc") for _ in range(2)]
            psT = self.ps(st, [128, 4, 128], BF16, "psT")
            rden = self.sb(st, [128, 4], F32, "rden")
            abf = self.sb(st, [128, 4, 128], BF16, "abf")
            aT = [self.sb(st, [128, 4, 128], BF16, "aT", dma=True) for _ in range(2)]
            npt = 0
            nsb = 0
            nacc = 0
            for h in range(MH):
                k_, q_, v_ = kn[h % 2], qn[h % 2], vv[h % 2]
                half = (h % 2) * 64
                fw.dma("sp", k_.t[:, :], s["kn"].ap()[h], k_, writes=[k_])
                fw.dma("sp", q_.t[:, :], s["qn"].ap()[h], q_, writes=[q_])
                fw.dma("sp", v_.t[:, :, :], s["v"].ap()[:, h * 128:(h + 1) * 128].rearrange("(j p) d -> p j d", p=128), v_,
                       writes=[v_])
                if h % 2 == 0:
                    qrb = qr[(h // 2) % 2]
                    fw.dma("sp", qrb.t[:, :], s["qr"].ap()[h // 2], qrb, writes=[qrb])
                for qs in range(NQS):
                    ac = acc[nacc % 2]
                    nacc += 1
                    nkt = npre + 4 * qs + 4
                    for kt in range(nkt):
                        o = kt - npre
                        qi0 = 0 if (kt < npre or o < 4 * qs) else (o - 4 * qs)
                        q0 = qi0 * 128
                        diag = (kt >= npre and o >= 4 * qs)
                        pS = psS[nsb % 2]
                        nsb += 1
                        p_ = pt[npt % 3]
                        npt += 1
                        qa, qb = qs * 512 + q0, (qs + 1) * 512
                        fw.op("pe", lambda e: e.matmul(pS.t[:, q0:512], k_.t[:, kt * 128:(kt + 1) * 128], q_.t[:, qa:qb],
                                                       start=True, stop=False), reads=[k_, q_], writes=[pS], inc=False)
                        fw.op("pe", lambda e: e.matmul(pS.t[:, q0:512], kr.t[half:half + 64, kt * 128:(kt + 1) * 128],
                                                       qrb.t[half:half + 64, qa:qb], start=False, stop=True),
                              reads=[kr, qrb], join=[pS])
                        if not diag:
                            bias = self.derived.t[:, c["H"]:c["H"] + 1] if kt < npre else 0.0
                            fw.op("act", lambda e: e.activation(out=p_.t[:, 0:512], in_=pS.t[:, 0:512], func=AF.Exp,
                                                                bias=bias, scale=scale), reads=[pS, self.derived],
                                  writes=[p_])
                        else:
                            fw.op("act", lambda e: e.activation(out=p_.t[0:64, q0:512], in_=pS.t[0:64, q0:512],
                                                                func=AF.Exp, scale=scale), reads=[pS], writes=[p_])
                            if q0 + 64 < 512 or True:
                                fw.op("act", lambda e: e.activation(out=p_.t[64:128, q0 + 64:512],
                                                                    in_=pS.t[64:128, q0 + 64:512], func=AF.Exp,
                                                                    scale=scale), reads=[pS], join=[p_])
                            fw.op("pool", lambda e: e.memset(p_.t[64:128, q0:q0 + 64], 0.0), join=[p_])
                        for qi in range(qi0, 4):
                            own_tile = 4 * qs + qi
                            st_ = (kt == 0)
                            sp_ = (kt == npre + own_tile)
                            fw.op("pe", lambda e: e.matmul(ac.t[:, qi, 0:128], p_.t[:, qi * 128:(qi + 1) * 128],
                                                           v_.t[:, kt, :], start=(st_ and qi % 2 == 0), stop=sp_,
                                                           skip_group_check=True), reads=[p_, v_], inc=False,
                                  **({"writes": [ac]} if (kt == 0 and qi == qi0) else {"join": [ac]}))
                            fw.op("pe", lambda e: e.matmul(ac.t[:, qi, 128:129], p_.t[:, qi * 128:(qi + 1) * 128],
                                                           self.ones_bf()[:, 0:1], start=False, stop=sp_,
                                                           skip_group_check=True),
                                  reads=[p_, self.constb], join=[ac], inc=(qi == 3))
                    fw.op("dve", lambda e: e.reciprocal(out=rden.t[:, :], in_=ac.t[:, :, 128]), reads=[ac], writes=[rden])
                    for qi in range(4):
                        fw.op("dve", lambda e: e.tensor_scalar(out=abf.t[:, qi, :], in0=ac.t[:, qi, 0:128],
                                                               scalar1=rden.t[:, qi:qi + 1], scalar2=None, op0=ALU.mult),
                              reads=[ac, rden], **({"writes": [abf]} if qi == 0 else {"join": [abf]}))
                    for qi in range(4):
                        fw.op("pe", lambda e: e.transpose(out=psT.t[:, qi, :], in_=abf.t[:, qi, :], identity=self.ident()),
                              reads=[abf, self.constb], **({"writes": [psT]} if qi == 0 else {"join": [psT]}))
                    at = aT[(h * NQS + qs) % 2]
                    fw.op("act", lambda e: e.activation(out=at.t[:, :, :], in_=psT.t[:, :, :], func=AF.Copy), reads=[psT],
                          writes=[at])
                    fw.dma("sp", s["attnT"].ap()[qs * 4:(qs + 1) * 4, :, h * 128:(h + 1) * 128].rearrange("j p t -> p j t"),
                           at.t[:, :, :], at, reads=[at])
            fw.barrier()

    def merge_out(self):
        c, fw, s = self.c, self.fw, self.s
        T, D, DI, MH = c["T"], c["D"], c["DI"], c["MH"]
        DC = D // 128
        own = list(range(T // 128))

        def mk(first):
            def ex(st):
                return (self.ring(st, 2, [128, 4, 1024], BF16, "gt"), self.ring(st, 2, [128, 4, 1024], BF16, "yg"),
                        self.ring(st, 2, [128, 8, 4, 128], BF16, "mo"))

            def epi(k):
                gring, yring, oring = k["ctx"]
                gt = gring()
                nci_n = k["cw"] // 128
                t0 = k["tiles"][0] * 128
                TG = k["TG"]
                for nci in range(nci_n):
                    ch = k["c0"] // 128 + nci + (0 if first else DC)
                    fw.dma("sp", gt.t[:, nci, 0:TG], s["gates"].ap()[ch, :, t0:t0 + TG], gt,
                           **({"writes": [gt]} if nci == 0 else {"join": [gt]}))
                if first:
                    o = yring()
                else:
                    yg = yring()
                    for nci in range(nci_n):
                        ch = k["c0"] // 128 + nci
                        fw.dma("sp", yg.t[:, nci, 0:TG], s["yg"].ap()[ch, :, t0:t0 + TG], yg,
                               **({"writes": [yg]} if nci == 0 else {"join": [yg]}))
                    o = oring()
                fz = True
                for nci in range(nci_n):
                    for th in range(k["NTH"]):
                        idx = nci * k["NTH"] + th
                        w = k["THW"]
                        src = k["ps"].t[:, idx * 512:idx * 512 + w]
                        gv = gt.t[:, nci, th * 512:th * 512 + w]
                        if first:
                            fw.op("dve", lambda e: e.tensor_tensor(out=o.t[:, nci, th * 512:th * 512 + w], in0=src, in1=gv,
                                                                   op=ALU.mult), reads=[k["ps"], gt],
                                  **({"writes": [o]} if fz else {"join": [o]}))
                        else:
                            yv = yg.t[:, nci, th * 512:th * 512 + w]
                            fw.op("dve", lambda e: e.tensor_tensor(out=gv, in0=src, in1=gv, op=ALU.mult),
                                  reads=[k["ps"], gt], join=[gt])
                            fw.op("pool", lambda e: e.tensor_tensor(
                                out=o.t[:, th * 4:th * 4 + w // 128, nci, :],
                                in0=gv.rearrange("p (j t) -> p j t", t=128), in1=yv.rearrange("p (j t) -> p j t", t=128),
                                op=ALU.add), reads=[gt, yg], **({"writes": [o]} if fz else {"join": [o]}))
                        fz = False
                if first:
                    for nci in range(nci_n):
                        ch = k["c0"] // 128 + nci
                        fw.dma("sp", s["yg"].ap()[ch, :, t0:t0 + TG], o.t[:, nci, 0:TG], o, reads=[o])
                else:
                    cb = k["c0"] // 128
                    fw.dma("sp", s["mg"].ap()[k["tiles"][0]:k["tiles"][0] + k["TGt"]].rearrange(
                        "j p (c t) -> p j c t", t=128)[:, :, cb:cb + nci_n, :], o.t[:, 0:k["TGt"], 0:nci_n, :], o, reads=[o])
            return ex, epi

        ex, epi = mk(True)
        self.gemm(s["ynT"], own, DI // 128, [self.i["w_ssm_out"]], 0, D, "F", epi, extra=ex)
        ex, epi = mk(False)
        self.gemm(s["attnT"], own, MH, [self.i["w_mla_out"]], 0, D, "F", epi, extra=ex)
        self.resid_gemm(s["mg"], DC, self.i["w_out"], self.i["x_own"], s["h"])

    def resid_gemm(self, A_d, KC, W, res_d, dst_d):
        c, fw = self.c, self.fw
        T, D = c["T"], c["D"]
        own = list(range(T // 128))

        def ex(st):
            return self.ring(st, 2, [128, 8, 256], F32, "rs")

        def epi(k):
            b = k["ctx"]()
            r0 = k["tiles"][0] * 128
            rv = lambda d_: d_.ap()[r0:r0 + k["TG"], k["c0"]:k["c0"] + k["cw"]].rearrange("(j p) n -> p j n", p=128)
            fw.dma("sp", b.t[:, 0:k["TGt"], 0:k["cw"]], rv(res_d), b, writes=[b])
            for tt in range(k["TGt"]):
                fw.op("dve", lambda e: e.tensor_tensor(out=b.t[:, tt, 0:k["cw"]], in0=b.t[:, tt, 0:k["cw"]],
                                                       in1=k["ps"].t[:, tt * k["CW"]:tt * k["CW"] + k["cw"]], op=ALU.add),
                      reads=[b, k["ps"]], join=[b])
            fw.dma("sp", rv(dst_d), b.t[:, 0:k["TGt"], 0:k["cw"]], b, reads=[b])
        self.gemm(A_d, own, KC, [W], 0, D, "T", epi, extra=ex)

    def ffn(self):
        c, fw, s = self.c, self.fw, self.s
        T, D, DFF = c["T"], c["D"], c["DFF"]
        DC, FC = D // 128, DFF // 128
        own = list(range(T // 128))

        def ex(st):
            return (self.sb(st, [128, 1024], F32, "sg"), self.ring(st, 2, [128, 8, 128], BF16, "ao"))

        def epi(k):
            sg, oring = k["ctx"]
            o = oring()
            for th in range(k["NTH"]):
                w = k["THW"]
                fw.op("act", lambda e: e.activation(out=sg.t[:, th * 512:th * 512 + w], in_=k["ps"].t[:, th * 512:th * 512 + w],
                                                    func=AF.Silu), reads=[k["ps"]],
                      **({"writes": [sg]} if th == 0 else {"join": [sg]}))
                fw.op("dve", lambda e: e.tensor_tensor(
                    out=o.t[:, th * 4:th * 4 + w // 128, :],
                    in0=sg.t[:, th * 512:th * 512 + w].rearrange("p (j t) -> p j t", t=128),
                    in1=k["ps"].t[:, (k["NTH"] + th) * 512:(k["NTH"] + th) * 512 + w].rearrange("p (j t) -> p j t", t=128),
                    op=ALU.mult), reads=[sg, k["ps"]], **({"writes": [o]} if th == 0 else {"join": [o]}))
            ch = k["c0"] // 128
            fw.dma("sp", s["act"].ap()[k["tiles"][0]:k["tiles"][0] + k["TGt"], :, ch * 128:(ch + 1) * 128].rearrange(
                "j p t -> p j t"), o.t[:, 0:k["TGt"], :], o, reads=[o])
        self.gemm(s["nT"], own, DC, [self.i["w_gate"], self.i["w_up"]], 0, DFF, "F", epi, extra=ex)
        self.resid_gemm(s["act"], FC, self.i["w_down"], s["h"], s["h2"])

    def final_norm(self):
        c, fw, s = self.c, self.fw, self.s
        T, D, H = c["T"], c["D"], c["H"]
        with contextlib.ExitStack() as st:
            gb = self.sb(st, [128, D], F32, "gfin", dma=True)
            fw.dma("sp", gb.t[:, :], self.i["rowp"].ap()[0:1, 3 * H:3 * H + D].broadcast_to([128, D]), gb, writes=[gb])
            xb = [self.sb(st, [128, D], F32, "fx", dma=True) for _ in range(2)]
            junk = self.sb(st, [128, D], BF16, "fj")
            ss = [self.sb(st, [128, 2], F32, "fs") for _ in range(2)]
            for j in range(T // 128):
                x, s_ = xb[j % 2], ss[j % 2]
                fw.dma("sp", x.t[:, :], s["h2"].ap()[j * 128:(j + 1) * 128, :], x, writes=[x])
                fw.op("dve", lambda e: e.memset(s_.t[:, 0:2], 0.0), writes=[s_])
                fw.op("act", lambda e: e.activation(out=junk.t[:, :], in_=x.t[:, :], func=AF.Square, accum_out=s_.t[:, 0:1]),
                      reads=[x, s_], writes=[junk], join=[s_])
                fw.op("act", lambda e: e.activation(out=s_.t[:, 1:2], in_=s_.t[:, 0:1], func=AF.Ln, scale=1.0 / D,
                                                    bias=self.spc("eps6")), reads=[s_, self.smallp], join=[s_])
                fw.op("act", lambda e: e.activation(out=s_.t[:, 1:2], in_=s_.t[:, 1:2], func=AF.Exp, scale=-0.5),
                      reads=[s_], join=[s_])
                fw.op("dve", lambda e: e.scalar_tensor_tensor(out=x.t[:, :], in0=x.t[:, :], scalar=s_.t[:, 1:2],
                                                              in1=gb.t[:, :], op0=ALU.mult, op1=ALU.mult),
                      reads=[x, s_, gb], join=[x])
                fw.dma("sp", self.out.ap()[j * 128:(j + 1) * 128, :], x.t[:, :], x, reads=[x])
            fw.barrier()

    def build(self, upto=99):
        c = self.c
        self.declare()
        self.fw = FW(self.nc, self.top)
        self.load_consts()
        s, i = self.s, self.i
        NP, NO = c["PT"] // 128, c["T"] // 128
        srcs = [i["x_pre"].ap()[j * 128:(j + 1) * 128, :] for j in range(NP)] + \
               [i["x_own"].ap()[j * 128:(j + 1) * 128, :] for j in range(NO)]
        phases = [
            lambda: self.norm_transpose(srcs, "g_mix", s["uT"].ap()),
            self.in_proj,
            self.conv_phase,
            self.rope_tables,
            lambda: self.latent_norm(s["cqr"], c["QL"] // 128, c["T"], "q_g", s["cq"], c["QL"], False),
            lambda: self.latent_norm(s["ckvr"], 4, c["TT"], "kv_g", s["ckv"], c["KVL"], True),
            self.ssd,
            self.mla,
            self.merge_out,
            lambda: self.norm_transpose([s["h"].ap()[j * 128:(j + 1) * 128, :] for j in range(NO)], "g_ffn", s["nT"].ap()),
            self.ffn,
            self.final_norm,
        ]
        for pi, ph in enumerate(phases):
            if pi < upto:
                ph()
        self.top.close()
        return self.nc


def small_layout(c):
    D, DI, H, CONVC, QL = c["D"], c["DI"], c["H"], c["CONVC"], c["QL"]
    cols = {}
    n = 0
    for name, w in (("g_mix", D // 128), ("g_ffn", D // 128), ("conv_w", CONVC // 128 * 4), ("conv_b", CONVC // 128),
                    ("ssm_g", DI // 128), ("q_g", QL // 128), ("kv_g", 4), ("gate_bias", 2 * D // 128), ("flag", 1),
                    ("invf", 1), ("sgn", 1), ("eps6", 1), ("eps5", 1), ("one", 1), ("tri", 128), ("ntri", 128), ("onesf", 128)):
        cols[name] = n
        n += w
    cols["_n"] = n
    return cols


def host_prep(c, inp):
    D, T, PT, TT, DI, H, CONVC, QL, KVL, MH, SEQ = (c[k] for k in
                                                     ("D", "T", "PT", "TT", "DI", "H", "CONVC", "QL", "KVL", "MH", "SEQ"))
    f = lambda a: np.ascontiguousarray(np.asarray(a), dtype=np.float32)
    pc = lambda v: f(v).reshape(-1, 128).T
    cols = small_layout(c)
    sp = np.zeros((128, cols["_n"]), np.float32)

    def put(name, a):
        a = np.asarray(a, np.float32)
        sp[:, cols[name]:cols[name] + a.shape[1]] = a
    put("g_mix", pc(inp["g_mix"][0]))
    put("g_ffn", pc(inp["g_ffn"][0]))
    cw = f(inp["conv_w"][0])
    put("conv_w", cw.T.reshape(CONVC // 128, 128, 4).transpose(1, 0, 2).reshape(128, -1))
    put("conv_b", pc(inp["conv_b"][0]))
    put("ssm_g", pc(inp["ssm_norm_g"][0]))
    put("q_g", pc(inp["q_norm_g"][0]))
    put("kv_g", pc(inp["kv_norm_g"][0]))
    put("gate_bias", pc(f(inp["gate_bias"][0]).reshape(-1)))
    half = 32
    invf = (np.float32(10000.0) ** (-np.arange(half, dtype=np.float32) / np.float32(half))).astype(np.float32)
    put("invf", np.tile(invf, 4)[:, None])
    put("sgn", np.tile(np.concatenate([-np.ones(32), np.ones(32)]), 2)[:, None])
    put("eps6", np.full((128, 1), 1e-6))
    put("eps5", np.full((128, 1), 1e-5))
    put("one", np.ones((128, 1)))
    k = np.arange(128)
    put("tri", (k[:, None] <= k[None, :]).astype(np.float32))
    put("ntri", (k[:, None] > k[None, :]).astype(np.float32))
    put("onesf", np.ones((128, 128), np.float32))
    constb = np.zeros((128, 384), np.float32)
    constb[:, 0:128] = np.eye(128)
    constb[:, 128:256] = 1.0
    pm = np.zeros((128, 128), np.float32)
    for m in range(128):
        pm[(m // 64) * 64 + ((m % 64) + 32) % 64, m] = 1.0
    constb[:, 256:384] = pm
    constb = constb.astype(ml_dtypes.bfloat16)
    rowp = np.concatenate([f(inp["dt_bias"][0]), f(inp["a_log"][0]), f(inp["d_skip"][0]), f(inp["g_final"])])[None, :]
    w_in = f(inp["w_in"][0])
    o = DI + CONVC + H + QL
    w_ckv = np.ascontiguousarray(np.concatenate([w_in[:, o:o + 512], w_in[:, o + 512:o + 576], w_in[:, o + 512:o + 576]], 1))
    wq = f(inp["w_q_up"][0]).reshape(QL, MH, 192)
    wkv = f(inp["w_kv_up"][0]).reshape(KVL, MH, 256)
    shared = {
        "w_in_z": np.ascontiguousarray(w_in[:, 0:DI]), "w_in_xbc": np.ascontiguousarray(w_in[:, DI:DI + CONVC]),
        "w_in_r": np.ascontiguousarray(np.concatenate([w_in[:, DI + CONVC:o], w_in[:, o + 576:]], 1)), "w_ckv": w_ckv, "w_ssm_out": f(inp["w_ssm_out"][0]),
        "w_qn": np.ascontiguousarray(wq[:, :, :128].reshape(QL, -1)),
        "w_qr": np.ascontiguousarray(wq[:, :, 128:].reshape(QL, -1)),
        "w_kn": np.ascontiguousarray(wkv[:, :, :128].reshape(KVL, -1)),
        "w_v": np.ascontiguousarray(wkv[:, :, 128:].reshape(KVL, -1)),
        "w_mla_out": f(inp["w_mla_out"][0]), "w_out": f(inp["w_out"][0]), "w_gate": f(inp["w_ffn_gate"][0]),
        "w_up": f(inp["w_ffn_up"][0]), "w_down": f(inp["w_ffn_down"][0]), "constb": constb, "rowp": rowp,
    }
    x = np.asarray(inp["x"], np.float32)
    pos = np.asarray(inp["positions"], np.int32)
    maps = []
    for core in range(8):
        b, hf = core // 2, core % 2
        m = dict(shared)
        m["x_own"] = np.ascontiguousarray(x[b, hf * T:(hf + 1) * T])
        m["x_pre"] = np.ascontiguousarray(x[b, 0:PT]) if hf else np.zeros((PT, D), np.float32)
        p_own = pos[b, hf * T:(hf + 1) * T]
        p_pre = pos[b, 0:PT] if hf else np.zeros(PT, np.int32)
        m["pos"] = np.ascontiguousarray(np.concatenate([p_pre, p_own])[None, :].astype(np.int32))
        spc = sp.copy()
        spc[:, cols["flag"]] = float(hf)
        m["smallp"] = spc
        maps.append(m)
    return maps


_CACHE = {}


def run(cfg, inp):
    key = (cfg["D"], cfg["SEQ"])
    if key not in _CACHE:
        _CACHE[key] = Prog(cfg).build()
    nc = _CACHE[key]
    maps = host_prep(cfg, inp)
    res = run_bass_kernel_spmd(nc, maps, core_ids=list(range(8)))
    T, D = cfg["T"], cfg["D"]
    out = np.zeros((4, cfg["SEQ"], D), np.float32)
    for core in range(8):
        b, hf = core // 2, core % 2
        out[b, hf * T:(hf + 1) * T] = res.results[core]["out"]
    return out


def kernel(**inputs):
    cfg = make_cfg(4096, 4096)
    return run(cfg, inputs)
```

```python
import contextlib
import numpy as np
import ml_dtypes
import concourse.bass as bass
import concourse.mybir as mybir
from concourse.bass_utils import run_bass_kernel_spmd

F32 = mybir.dt.float32
BF16 = mybir.dt.bfloat16
I32 = mybir.dt.int32
ALU = mybir.AluOpType
AF = mybir.ActivationFunctionType
PI = float(np.pi)


def make_cfg(D, SEQ):
    c = dict(D=D, SEQ=SEQ, T=SEQ // 2, PT=SEQ // 2)
    c["TT"] = c["T"] + c["PT"]
    c["DI"] = 2 * D
    c["H"] = c["DI"] // 64
    c["G"] = 8
    c["R"] = c["H"] // 8
    c["NST"] = 128
    c["CONVC"] = c["DI"] + 2 * 8 * 128
    c["QL"] = D // 4
    c["KVL"] = 512
    c["MH"] = D // 128
    c["DFF"] = ((8 * D // 3 + 255) // 256) * 256
    c["INW"] = c["DI"] + c["CONVC"] + c["H"] + c["QL"] + 576 + 2 * D
    return c


class Buf:
    __slots__ = ("t", "w", "r", "sem", "name")

    def __init__(self, t, name):
        self.t = t
        self.w = {}
        self.r = {}
        self.sem = None
        self.name = name


class DSem:
    def __init__(self, sem):
        self.sem = sem
        self.total = 0


class FW:
    def __init__(self, nc, stack, n_dsem=84):
        self.nc = nc
        self.eng = {"pe": nc.tensor, "act": nc.scalar, "dve": nc.vector, "pool": nc.gpsimd, "sp": nc.sync}
        self.psem = {}
        self.cnt = {}
        for e in ("pe", "act", "dve", "pool"):
            self.psem[e] = stack.enter_context(nc.semaphore("p_" + e))
            self.cnt[e] = 0
        self.seen = {e: {} for e in self.eng}
        self.dsems = [DSem(stack.enter_context(nc.semaphore("d%d" % i))) for i in range(n_dsem)]
        self.free = list(range(n_dsem))
        self.phase_sems = []

    def buf(self, t, name="", dma=False):
        b = Buf(t, name)
        if dma:
            b.sem = self.free.pop(0)
            self.phase_sems.append(b.sem)
        return b

    def _wait(self, e, key, val):
        if key == ("e", "pe") and e == "pe":
            return
        if self.seen[e].get(key, 0) >= val:
            return
        if key[0] == "e":
            assert val <= self.cnt[key[1]], ("wait on a not-yet-signalled instruction", e, key, val)
        self.seen[e][key] = val
        sem = self.psem[key[1]] if key[0] == "e" else self.dsems[key[1]].sem
        self.eng[e].wait_ge(sem, val)

    def _deps(self, e, reads, writes, join):
        me = ("e", e)
        for b in reads:
            for k, v in b.w.items():
                self._wait(e, k, v)
        for b in writes:
            for k, v in b.w.items():
                if k != me:
                    self._wait(e, k, v)
            for k, v in b.r.items():
                if k != me:
                    self._wait(e, k, v)
        for b in join:
            for k, v in b.w.items():
                if k != me:
                    self._wait(e, k, v)
            for k, v in b.r.items():
                if k != me:
                    self._wait(e, k, v)

    def _upd(self, key, val, reads, writes, join):
        for b in reads:
            b.r[key] = val
        for b in writes:
            b.w = {key: val}
            b.r = {}
        for b in join:
            b.w[key] = val

    def op(self, e, fn, reads=(), writes=(), join=(), inc=True):
        self._deps(e, reads, writes, join)
        ins = fn(self.eng[e])
        if inc:
            self.cnt[e] += 1
            ins.then_inc(self.psem[e], 1)
            self._upd(("e", e), self.cnt[e], reads, writes, join)
        else:
            self._upd(("e", e), self.cnt[e] + 1, reads, writes, join)

    def dma(self, q, out, in_, semb, reads=(), writes=(), join=()):
        self._deps(q, reads, writes, join)
        d = self.dsems[semb.sem]
        d.total += 16
        self.eng[q].dma_start(out=out, in_=in_).then_inc(d.sem, 16)
        self._upd(("d", semb.sem), d.total, reads, writes, join)

    def barrier(self):
        for e in self.eng:
            for e2 in self.psem:
                if self.cnt[e2]:
                    self._wait(e, ("e", e2), self.cnt[e2])
            for i, d in enumerate(self.dsems):
                if d.total:
                    self._wait(e, ("d", i), d.total)
        self.free = self.free + self.phase_sems
        self.phase_sems = []


def bc_last(ap, n):
    s = list(ap.shape)
    return ap.unsqueeze(len(s)).broadcast_to(s + [n])


def bc_mid(ap, n):
    s = list(ap.shape)
    return ap.unsqueeze(1).broadcast_to([s[0], n] + s[1:])


class Prog:
    def __init__(self, cfg, dbg=()):
        self.c = cfg
        self.dbg = set(dbg)
        self.nc = bass.Bass("TRN2", target_bir_lowering=False)
        self.top = contextlib.ExitStack()
        self.fw = None
        self.uid = 0

    def din(self, name, shape, dt=F32):
        return self.nc.dram_tensor(name, list(shape), dt, kind="ExternalInput")

    def dscr(self, name, shape, dt=BF16):
        kind = "ExternalOutput" if name in self.dbg else "Internal"
        return self.nc.dram_tensor(name, list(shape), dt, kind=kind)

    def sb(self, st, shape, dt, name=None, dma=False):
        self.uid += 1
        nm = "%s_%d" % (name or "sb", self.uid)
        t = st.enter_context(self.nc.sbuf_tensor(nm, list(shape), dt))
        return self.fw.buf(t, nm, dma=dma)

    def ps(self, st, shape, dt, name=None):
        self.uid += 1
        nm = "%s_%d" % (name or "ps", self.uid)
        t = st.enter_context(self.nc.psum_tensor(nm, list(shape), dt))
        return self.fw.buf(t, nm)

    def declare(self):
        c = self.c
        D, T, PT, TT, DI, H, CONVC, QL, KVL, MH, DFF = (c[k] for k in
                                                          ("D", "T", "PT", "TT", "DI", "H", "CONVC", "QL", "KVL", "MH", "DFF"))
        DC, CC, QC, FC = D // 128, CONVC // 128, QL // 128, DFF // 128
        i = {}
        i["x_own"] = self.din("x_own", [T, D])
        i["x_pre"] = self.din("x_pre", [PT, D])
        i["pos"] = self.din("pos", [1, TT], I32)
        i["w_in_z"] = self.din("w_in_z", [D, DI])
        i["w_in_xbc"] = self.din("w_in_xbc", [D, CONVC])
        i["w_in_r"] = self.din("w_in_r", [D, H + QL + 2 * D])
        i["w_ckv"] = self.din("w_ckv", [D, KVL + 128])
        i["w_ssm_out"] = self.din("w_ssm_out", [DI, D])
        i["w_qn"] = self.din("w_qn", [QL, MH * 128])
        i["w_qr"] = self.din("w_qr", [QL, MH * 64])
        i["w_kn"] = self.din("w_kn", [KVL, MH * 128])
        i["w_v"] = self.din("w_v", [KVL, MH * 128])
        i["w_mla_out"] = self.din("w_mla_out", [MH * 128, D])
        i["w_out"] = self.din("w_out", [D, D])
        i["w_gate"] = self.din("w_gate", [D, DFF])
        i["w_up"] = self.din("w_up", [D, DFF])
        i["w_down"] = self.din("w_down", [DFF, D])
        self.sp_cols = small_layout(c)
        i["smallp"] = self.din("smallp", [128, self.sp_cols["_n"]])
        i["constb"] = self.din("constb", [128, 3 * 128], BF16)
        i["rowp"] = self.din("rowp", [1, 3 * H + D])
        self.i = i
        self.out = self.nc.dram_tensor("out", [T, D], F32, kind="ExternalOutput")
        s = {}

        class View:
            def __init__(self, a):
                self._a = a

            def ap(self):
                return self._a

        def arena(name, nelem):
            return self.nc.dram_tensor(name, [int(nelem)], BF16, kind="Internal")

        def carve(ar, off, shape, dt=BF16, pat=None):
            n = int(np.prod(shape)) * (2 if dt == F32 else 1)
            a = ar.ap()[off:off + n]
            if dt == F32:
                a = a.bitcast(F32)
            names = " ".join("d%d" % k for k in range(len(shape)))
            kw = {"d%d" % k: int(shape[k]) for k in range(len(shape) - 1)}
            return View(a.rearrange("(%s) -> %s" % (names, names), **kw)), off + n

        MHd = MH * 128
        n_xp = CC * 128 * TT
        n_p4a = MHd * TT * 2 + MHd * T
        n_act = T * FC * 128
        R1 = arena("R1", max(n_xp, n_p4a, n_act))
        n_p45 = (MH // 2) * 128 * T * 2 + T * MHd + D * T * 2
        R2 = arena("R2", max(n_xp, n_p45, T * D + 2 * T * D))
        R3 = arena("R3", max(TT * D, T * DI))
        R4 = arena("R4", max(T * DI, 2 * T * D))
        R5 = arena("R5", 2 * D * T)
        s["xp"], _ = carve(R1, 0, [CC, 128, TT])
        s["kn"], o = carve(R1, 0, [MH, 128, TT])
        s["v"], o = carve(R1, o, [TT, MHd])
        s["qn"], o = carve(R1, o, [MH, 128, T])
        s["act"], _ = carve(R1, 0, [T // 128, 128, FC * 128])
        s["xc"], _ = carve(R2, 0, [TT // 128, 128, CC * 128])
        s["qrr"], o = carve(R2, 0, [MH // 2, 128, T])
        s["qr"], o = carve(R2, o, [MH // 2, 128, T])
        s["attnT"], o = carve(R2, o, [T // 128, 128, MHd])
        s["yg"], o = carve(R2, o, [DC, 128, T])
        s["mg"], o = carve(R2, o, [T // 128, 128, DC * 128])
        s["nT"], o = carve(R2, 0, [T // 128, 128, DC * 128])
        s["h2"], o = carve(R2, o, [T, D], F32)
        s["uT"], _ = carve(R3, 0, [TT // 128, 128, DC * 128])
        s["ynT"], _ = carve(R3, 0, [T // 128, 128, DI])
        s["sz"], _ = carve(R4, 0, [T, DI])
        s["h"], _ = carve(R4, 0, [T, D], F32)
        s["gates"], _ = carve(R5, 0, [2 * DC, 128, T])
        s["dt"] = self.dscr("dt_d", [TT, H], F32)
        s["cqr"] = self.dscr("cqr_d", [QC, 128, T])
        s["cq"] = self.dscr("cq_d", [T // 128, 128, QC * 128])
        s["ckvr"] = self.dscr("ckvr_d", [5, 128, TT])
        s["ckv"] = self.dscr("ckv_d", [TT // 128, 128, 4 * 128])
        s["kr"] = self.dscr("kr_d", [128, TT])
        s["rope"] = self.dscr("rope_d", [2, 128, TT], F32)
        self.s = s

    def load_consts(self):
        fw, st = self.fw, self.top
        n = self.sp_cols["_n"]
        H, D = self.c["H"], self.c["D"]
        self.smallp = self.sb(st, [128, n], F32, "smallp", dma=True)
        self.constb = self.sb(st, [128, 384], BF16, "constb", dma=True)
        self.rowp = self.sb(st, [128, 3 * H], F32, "rowp", dma=True)
        fw.dma("sp", self.smallp.t[:, :], self.i["smallp"].ap(), self.smallp, writes=[self.smallp])
        fw.dma("sp", self.constb.t[:, :], self.i["constb"].ap(), self.constb, writes=[self.constb])
        fw.dma("sp", self.rowp.t[:, :], self.i["rowp"].ap()[0:1, 0:3 * H].broadcast_to([128, 3 * H]), self.rowp,
               writes=[self.rowp])
        self.derived = self.sb(st, [128, H + 1], F32, "derived")
        fw.op("act", lambda e: e.activation(out=self.derived.t[:, 0:H], in_=self.rowp.t[:, H:2 * H], func=AF.Exp),
              reads=[self.rowp], writes=[self.derived])
        fw.op("dve", lambda e: e.tensor_scalar(out=self.derived.t[:, 0:H], in0=self.derived.t[:, 0:H], scalar1=-1.0,
                                               scalar2=None, op0=ALU.mult), reads=[self.derived], join=[self.derived])
        fc = self.sp_cols["flag"]
        fw.op("dve", lambda e: e.tensor_scalar(out=self.derived.t[:, H:H + 1], in0=self.smallp.t[:, fc:fc + 1],
                                               scalar1=-1.0, scalar2=30000.0, op0=ALU.add, op1=ALU.mult),
              reads=[self.smallp, self.derived], join=[self.derived])

    def spc(self, name, j=0, n=1):
        o = self.sp_cols[name] + j
        return self.smallp.t[:, o:o + n]

    def ident(self):
        return self.constb.t[:, 0:128]

    def ones_bf(self):
        return self.constb.t[:, 128:256]

    def perm(self):
        return self.constb.t[:, 256:384]

    def norm_transpose(self, srcs, gname, dst):
        c, fw = self.c, self.fw
        D = c["D"]
        DC = D // 128
        with contextlib.ExitStack() as st:
            xb = [self.sb(st, [128, D], F32, "xb", dma=True) for _ in range(2)]
            junk = self.sb(st, [128, D], BF16, "junk")
            xs = [self.sb(st, [128, D], BF16, "xs") for _ in range(2)]
            ss = [self.sb(st, [128, 2], F32, "ss") for _ in range(2)]
            uT = [self.sb(st, [128, DC, 128], BF16, "uT", dma=True) for _ in range(2)]
            pst = [self.ps(st, [128, 4, 128], BF16, "pst") for _ in range(2)]
            npt = 0
            for j, src in enumerate(srcs):
                x, s_, xs_, u = xb[j % 2], ss[j % 2], xs[j % 2], uT[j % 2]
                fw.dma("sp", x.t[:, :], src, x, writes=[x])
                fw.op("dve", lambda e: e.memset(s_.t[:, 0:2], 0.0), writes=[s_])
                fw.op("act", lambda e: e.activation(out=junk.t[:, :], in_=x.t[:, :], func=AF.Square,
                                                    accum_out=s_.t[:, 0:1]), reads=[x, s_], writes=[junk], join=[s_])
                fw.op("act", lambda e: e.activation(out=s_.t[:, 1:2], in_=s_.t[:, 0:1], func=AF.Ln, scale=1.0 / D,
                                                    bias=self.spc("eps6")), reads=[s_, self.smallp], join=[s_])
                fw.op("act", lambda e: e.activation(out=s_.t[:, 1:2], in_=s_.t[:, 1:2], func=AF.Exp, scale=-0.5),
                      reads=[s_], join=[s_])
                fw.op("dve", lambda e: e.tensor_scalar(out=xs_.t[:, :], in0=x.t[:, :], scalar1=s_.t[:, 1:2],
                                                       scalar2=None, op0=ALU.mult), reads=[x, s_], writes=[xs_])
                first = True
                for c4 in range(0, DC, 4):
                    p = pst[npt % 2]
                    npt += 1
                    nn = min(4, DC - c4)
                    for k in range(nn):
                        cc = c4 + k
                        fw.op("pe", lambda e: e.transpose(out=p.t[:, k, :], in_=xs_.t[:, cc * 128:(cc + 1) * 128],
                                                          identity=self.ident()), reads=[xs_, self.constb],
                              **({"writes": [p]} if k == 0 else {"join": [p]}))
                    for k in range(nn):
                        cc = c4 + k
                        fw.op("act", lambda e: e.activation(out=u.t[:, cc, :], in_=p.t[:, k, :], func=AF.Copy,
                                                            scale=self.spc(gname, cc)), reads=[p, self.smallp],
                              **({"writes": [u]} if first else {"join": [u]}))
                        first = False
                fw.dma("sp", dst[j].rearrange("p (c t) -> p c t", t=128), u.t[:, :, :], u, reads=[u])
            fw.barrier()

    def gemm(self, A_d, tiles, KC, Ws, col0, N, mode, epi, big=True, extra=None):
        c, fw = self.c, self.fw
        nw = len(Ws)
        ntile = len(tiles)
        TGt = 8 if (KC <= 32 and ntile % 8 == 0 and big) else 4
        if ntile % TGt:
            TGt = ntile
        TG = TGt * 128
        NTH = max(1, TG // 512)
        THW = min(TG, 512)
        CW = 128 if nw == 2 else 256
        KS = 32
        nks = (KC + KS - 1) // KS
        with contextlib.ExitStack() as st:
            A = [self.sb(st, [128, TGt, KC, 128], BF16, "A", dma=True)]
            nsl = 3
            slabs = [self.sb(st, [128, min(KS, KC), CW], BF16, "slab", dma=True) for _ in range(3 if nw == 1 else 4)]
            pss = [self.ps(st, [128, 2048], F32, "gps") for _ in range(2)]
            ectx = extra(st) if extra else None
            if ntile // TGt > 1 and self.nc.sbuf_bytes_remaining >= TGt * KC * 256 + 6144:
                A.append(self.sb(st, [128, TGt, KC, 128], BF16, "A", dma=True))
            nsl_i = 0
            nblk = 0
            for gi in range(ntile // TGt):
                tl = tiles[gi * TGt:(gi + 1) * TGt]
                a = A[gi % len(A)]
                fw.dma("sp", a.t[:, :, :, :].rearrange("p j c t -> p j (c t)"),
                       A_d.ap()[tl[0]:tl[0] + TGt].rearrange("j p f -> p j f"), a, writes=[a])
                for c0 in range(0, N, CW):
                    cw = min(CW, N - c0)
                    psb = pss[nblk % 2]
                    nblk += 1
                    firstmm = True
                    for ks in range(nks):
                        kn = min(KS, KC - ks * KS)
                        sl = []
                        for wi in range(nw):
                            s_ = slabs[nsl_i % len(slabs)]
                            nsl_i += 1
                            wv = Ws[wi].ap()[ks * KS * 128:(ks * KS + kn) * 128, col0 + c0:col0 + c0 + cw]
                            fw.dma("pool", s_.t[:, 0:kn, 0:cw], wv.rearrange("(kc p) n -> p kc n", p=128), s_,
                                   writes=[s_])
                            sl.append(s_)
                        for kc in range(kn):
                            kk = ks * KS + kc
                            last = (ks == nks - 1 and kc == kn - 1)
                            first = (ks == 0 and kc == 0)
                            if mode == "F":
                                for wi in range(nw):
                                    for nci in range((cw + 127) // 128):
                                        mc = min(128, cw - nci * 128)
                                        for th in range(NTH):
                                            idx = ((wi if nw == 2 else nci) * NTH + th)
                                            o = psb.t[0:mc, idx * 512:idx * 512 + THW]
                                            l_ = sl[wi].t[:, kc, nci * 128:nci * 128 + mc]
                                            r_ = a.t[:, th * 4:th * 4 + THW // 128, kk, :]
                                            inc_ = (kc == kn - 1 and wi == nw - 1 and nci == (cw + 127) // 128 - 1
                                                    and th == NTH - 1)
                                            fw.op("pe", lambda e: e.matmul(o, l_, r_, start=first, stop=last),
                                                  reads=[a, sl[wi]], inc=inc_,
                                                  **({"writes": [psb]} if firstmm else {"join": [psb]}))
                                            firstmm = False
                            else:
                                for tt in range(TGt):
                                    o = psb.t[:, tt * CW:tt * CW + cw]
                                    l_ = a.t[:, tt, kk, :]
                                    r_ = sl[0].t[:, kc, 0:cw]
                                    st0 = first and ((tt * CW * 4) % 2048 == 0)
                                    fw.op("pe", lambda e: e.matmul(o, l_, r_, start=st0, stop=last, skip_group_check=True),
                                          reads=[a, sl[0]], inc=(kc == kn - 1 and tt == TGt - 1),
                                          **({"writes": [psb]} if firstmm else {"join": [psb]}))
                                    firstmm = False
                    epi(dict(tiles=tl, gi=gi, TGt=TGt, TG=TG, NTH=NTH, THW=THW, CW=CW, c0=c0, cw=cw, ps=psb, st=st,
                             ctx=ectx))
            fw.barrier()

    def ring(self, st, n, shape, dt, name, dma=True):
        bufs = [self.sb(st, shape, dt, name, dma=dma) for _ in range(n)]
        state = {"i": 0}

        def nxt():
            b = bufs[state["i"] % n]
            state["i"] += 1
            return b
        return nxt

    def in_proj(self):
        c, fw, s = self.c, self.fw, self.s
        D, T, PT, TT, DI, H, CONVC, QL = (c[k] for k in ("D", "T", "PT", "TT", "DI", "H", "CONVC", "QL"))
        DC = D // 128
        allt = list(range(TT // 128))
        own = list(range(PT // 128, TT // 128))
        npre = PT // 128
        Wz, Wx, Wr = self.i["w_in_z"], self.i["w_in_xbc"], self.i["w_in_r"]

        def ex_z(st):
            return self.ring(st, 2, [128, 8, 256], BF16, "stz")

        def epi_z(k):
            b = k["ctx"]()
            for tt in range(k["TGt"]):
                fw.op("act", lambda e: e.activation(out=b.t[:, tt, 0:k["cw"]],
                                                    in_=k["ps"].t[:, tt * k["CW"]:tt * k["CW"] + k["cw"]], func=AF.Silu),
                      reads=[k["ps"]], **({"writes": [b]} if tt == 0 else {"join": [b]}))
            r0 = (k["tiles"][0] - npre) * 128
            fw.dma("sp", s["sz"].ap()[r0:r0 + k["TG"], k["c0"]:k["c0"] + k["cw"]].rearrange("(j p) n -> p j n", p=128),
                   b.t[:, 0:k["TGt"], 0:k["cw"]], b, reads=[b])
        self.gemm(s["uT"], own, DC, [Wz], 0, DI, "T", epi_z, extra=ex_z)

        def mk_epiF(dst, tok_off, func=None, bias_name=None, ch_off=0):
            def ex(st):
                return self.ring(st, 2, [128, 4, 1024], BF16, "stF")

            def epi(k):
                b = k["ctx"]()
                nci_n = (k["cw"] + 127) // 128
                firstw = True
                for nci in range(nci_n):
                    mc = min(128, k["cw"] - nci * 128)
                    ch = (k["c0"] // 128) + nci
                    for th in range(k["NTH"]):
                        idx = nci * k["NTH"] + th
                        src = k["ps"].t[0:mc, idx * 512:idx * 512 + k["THW"]]
                        dstv = b.t[0:mc, nci, th * 512:th * 512 + k["THW"]]
                        if func is None:
                            eng = "dve" if (idx % 2) else "act"
                            if eng == "act":
                                fw.op("act", lambda e: e.activation(out=dstv, in_=src, func=AF.Copy), reads=[k["ps"]],
                                      **({"writes": [b]} if firstw else {"join": [b]}))
                            else:
                                fw.op("dve", lambda e: e.tensor_copy(out=dstv, in_=src), reads=[k["ps"]],
                                      **({"writes": [b]} if firstw else {"join": [b]}))
                        else:
                            fw.op("act", lambda e: e.activation(out=dstv, in_=src, func=func,
                                                                bias=self.spc(bias_name, ch)[0:mc, :]),
                                  reads=[k["ps"], self.smallp], **({"writes": [b]} if firstw else {"join": [b]}))
                        firstw = False
                t0 = k["tiles"][0] * 128 - tok_off
                for nci in range(nci_n):
                    mc = min(128, k["cw"] - nci * 128)
                    ch = (k["c0"] // 128) + nci + ch_off
                    fw.dma("sp", dst.ap()[ch, 0:mc, t0:t0 + k["TG"]], b.t[0:mc, nci, 0:k["TG"]], b, reads=[b])
            return ex, epi

        ex, epi = mk_epiF(s["xp"], 0)
        self.gemm(s["uT"], allt, DC, [Wx], 0, CONVC, "F", epi, extra=ex)
        ex, epi = mk_epiF(s["cqr"], PT)
        self.gemm(s["uT"], own, DC, [Wr], H, QL, "F", epi, extra=ex)
        ex, epi = mk_epiF(s["ckvr"], 0)
        self.gemm(s["uT"], allt, DC, [self.i["w_ckv"]], 0, 640, "F", epi, extra=ex)
        ex, epi = mk_epiF(s["gates"], PT, func=AF.Sigmoid, bias_name="gate_bias")
        self.gemm(s["uT"], own, DC, [Wr], H + QL, 2 * D, "F", epi, extra=ex)

        def ex_dt(st):
            return (self.ring(st, 2, [128, 8, H], F32, "stdt"), self.sb(st, [128, H], F32, "dtt"))

        def epi_dt(k):
            nxt, tmp = k["ctx"]
            b = nxt()
            for tt in range(k["TGt"]):
                src = k["ps"].t[:, tt * k["CW"]:tt * k["CW"] + H]
                fw.op("dve", lambda e: e.tensor_tensor(out=tmp.t[:, :], in0=src, in1=self.rowp.t[:, 0:H], op=ALU.add),
                      reads=[k["ps"], self.rowp], writes=[tmp])
                fw.op("act", lambda e: e.activation(out=tmp.t[:, :], in_=tmp.t[:, :], func=AF.Exp), reads=[tmp],
                      join=[tmp])
                fw.op("act", lambda e: e.activation(out=b.t[:, tt, :], in_=tmp.t[:, :], func=AF.Ln, bias=self.spc("one")),
                      reads=[tmp], **({"writes": [b]} if tt == 0 else {"join": [b]}))
                if k["tiles"][tt] < npre:
                    fw.op("dve", lambda e: e.tensor_scalar(out=b.t[:, tt, :], in0=b.t[:, tt, :],
                                                           scalar1=self.spc("flag"), scalar2=None, op0=ALU.mult),
                          reads=[b, self.smallp], join=[b])
            r0 = k["tiles"][0] * 128
            fw.dma("sp", s["dt"].ap()[r0:r0 + k["TG"], :].rearrange("(j p) n -> p j n", p=128),
                   b.t[:, 0:k["TGt"], :], b, reads=[b])
        self.gemm(s["uT"], allt, DC, [Wr], 0, H, "T", epi_dt, extra=ex_dt)

    def conv_phase(self):
        c, fw, s = self.c, self.fw, self.s
        TT, CONVC = c["TT"], c["CONVC"]
        CC = CONVC // 128
        NJ = TT // 128
        CG = 4
        with contextlib.ExitStack() as st:
            xin = [self.sb(st, [128, CG, TT + 4], BF16, "cin", dma=True) for _ in range(2)]
            acc = [self.sb(st, [128, TT], F32, "cacc") for _ in range(2)]
            xo = [self.sb(st, [128, NJ, CG, 128], BF16, "cout", dma=True) for _ in range(2)]
            for b in xin:
                fw.op("pool", lambda e: e.memset(b.t[:, :, 0:4], 0.0), writes=[b])
            for gi in range(CC // CG):
                xi, o = xin[gi % 2], xo[gi % 2]
                fw.dma("sp", xi.t[:, :, 4:4 + TT], s["xp"].ap()[gi * CG:(gi + 1) * CG].rearrange("c p t -> p c t"), xi,
                       join=[xi])
                for k in range(CG):
                    ch = gi * CG + k
                    a = acc[k % 2]
                    eng = "dve"
                    fw.op(eng, lambda e: e.tensor_scalar(out=a.t[:, :], in0=xi.t[:, k, 4:4 + TT],
                                                         scalar1=self.spc("conv_w", ch * 4 + 3),
                                                         scalar2=self.spc("conv_b", ch), op0=ALU.mult, op1=ALU.add),
                          reads=[xi, self.smallp], writes=[a])
                    for j in range(3):
                        sh = 3 - j
                        fw.op(eng, lambda e: e.scalar_tensor_tensor(out=a.t[:, :], in0=xi.t[:, k, 4 - sh:4 - sh + TT],
                                                                    scalar=self.spc("conv_w", ch * 4 + j), in1=a.t[:, :],
                                                                    op0=ALU.mult, op1=ALU.add),
                              reads=[xi, a, self.smallp], join=[a])
                    fw.op("act", lambda e: e.activation(out=o.t[:, :, k, :], in_=a.t[:, :].rearrange("p (j t) -> p j t", t=128),
                                                        func=AF.Silu), reads=[a],
                          **({"writes": [o]} if k == 0 else {"join": [o]}))
                fw.dma("sp", s["xc"].ap().rearrange("j p (c t) -> p j c t", t=128)[:, :, gi * CG:(gi + 1) * CG, :],
                       o.t[:, :, :, :], o, reads=[o])
            fw.barrier()

    def rope_tables(self):
        c, fw, s = self.c, self.fw, self.s
        TT = c["TT"]
        with contextlib.ExitStack() as st:
            pi_ = self.sb(st, [128, TT], I32, "posi", dma=True)
            ang = self.sb(st, [128, TT], F32, "ang")
            r = self.sb(st, [128, TT], F32, "rr")
            yv = self.sb(st, [128, TT], F32, "yv")
            tb = [self.sb(st, [128, TT], F32, "ropet", dma=True) for _ in range(2)]
            fw.dma("sp", pi_.t[:, :], self.i["pos"].ap()[0:1, :].broadcast_to([128, TT]), pi_, writes=[pi_])
            fw.op("dve", lambda e: e.tensor_copy(out=ang.t[:, :], in_=pi_.t[:, :]), reads=[pi_], writes=[ang])
            fw.op("dve", lambda e: e.tensor_scalar(out=ang.t[:, :], in0=ang.t[:, :], scalar1=self.spc("invf"),
                                                   scalar2=None, op0=ALU.mult), reads=[ang, self.smallp], join=[ang])
            for which, sh in ((0, 1.5 * PI), (1, PI)):
                fw.op("dve", lambda e: e.tensor_scalar(out=yv.t[:, :], in0=ang.t[:, :], scalar1=sh, scalar2=None,
                                                       op0=ALU.add), reads=[ang], writes=[yv])
                fw.op("dve", lambda e: e.tensor_scalar(out=r.t[:, :], in0=yv.t[:, :], scalar1=1.0 / (2 * PI), scalar2=None,
                                                       op0=ALU.mult), reads=[yv], writes=[r])
                fw.op("dve", lambda e: e.tensor_copy(out=pi_.t[:, :], in_=r.t[:, :]), reads=[r], writes=[pi_])
                fw.op("dve", lambda e: e.tensor_copy(out=r.t[:, :], in_=pi_.t[:, :]), reads=[pi_], writes=[r])
                fw.op("dve", lambda e: e.scalar_tensor_tensor(out=yv.t[:, :], in0=r.t[:, :], scalar=-2 * PI, in1=yv.t[:, :],
                                                              op0=ALU.mult, op1=ALU.add), reads=[r, yv], join=[yv])
                fw.op("dve", lambda e: e.tensor_scalar(out=r.t[:, :], in0=yv.t[:, :], scalar1=0.0, scalar2=2 * PI,
                                                       op0=ALU.is_lt, op1=ALU.mult), reads=[yv], writes=[r])
                fw.op("dve", lambda e: e.scalar_tensor_tensor(out=r.t[:, :], in0=yv.t[:, :], scalar=-PI, in1=r.t[:, :],
                                                              op0=ALU.add, op1=ALU.add), reads=[yv, r], join=[r])
                fw.op("dve", lambda e: e.tensor_scalar(out=r.t[:, :], in0=r.t[:, :], scalar1=-3.1415925, scalar2=3.1415925,
                                                       op0=ALU.max, op1=ALU.min), reads=[r], join=[r])
                if which == 0:
                    fw.op("act", lambda e: e.activation(out=tb[0].t[:, :], in_=r.t[:, :], func=AF.Sin), reads=[r],
                          writes=[tb[0]])
                else:
                    fw.op("act", lambda e: e.activation(out=tb[1].t[:, :], in_=r.t[:, :], func=AF.Sin,
                                                        scale=self.spc("sgn")), reads=[r, self.smallp], writes=[tb[1]])
                fw.dma("sp", s["rope"].ap()[which], tb[which].t[:, :], tb[which], reads=[tb[which]])
            fw.barrier()

    def apply_rope(self, st, src_sb, n, cos_sb, sin_sb, t0, out_sb, pr, tmp):
        fw = self.fw
        for o in range(0, n, 512):
            w = min(512, n - o)
            fw.op("pe", lambda e: e.matmul(pr.t[:, 0:w], self.perm(), src_sb.t[:, o:o + w], start=True, stop=True),
                  reads=[src_sb, self.constb], writes=[pr])
            fw.op("dve", lambda e: e.tensor_tensor(out=tmp.t[:, 0:w], in0=pr.t[:, 0:w], in1=sin_sb.t[:, t0 + o:t0 + o + w],
                                                   op=ALU.mult), reads=[pr, sin_sb], writes=[tmp])
            fw.op("pool", lambda e: e.tensor_tensor(out=out_sb.t[:, o:o + w], in0=src_sb.t[:, o:o + w],
                                                    in1=cos_sb.t[:, t0 + o:t0 + o + w], op=ALU.mult),
                  reads=[src_sb, cos_sb], **({"writes": [out_sb]} if o == 0 else {"join": [out_sb]}))
            fw.op("pool", lambda e: e.tensor_tensor(out=out_sb.t[:, o:o + w], in0=out_sb.t[:, o:o + w], in1=tmp.t[:, 0:w],
                                                    op=ALU.add), reads=[out_sb, tmp], join=[out_sb])

    def latent_norm(self, raw, nch, ntok, gname, dst, feat, rope_extra):
        c, fw, s = self.c, self.fw, self.s
        with contextlib.ExitStack() as st:
            xin = [self.sb(st, [128, nch, 512], BF16, "lin", dma=True) for _ in range(2)]
            sq = self.sb(st, [128, nch, 512], BF16, "lsq")
            rs = self.sb(st, [128, 512], F32, "lrs")
            xo = [self.sb(st, [128, 4, nch, 128], BF16, "lout", dma=True) for _ in range(2)]
            pss = self.ps(st, [128, 512], F32, "lps")
            if rope_extra:
                cos_sb = self.sb(st, [128, ntok], F32, "cos", dma=True)
                sin_sb = self.sb(st, [128, ntok], F32, "sin", dma=True)
                fw.dma("sp", cos_sb.t[:, :], s["rope"].ap()[0], cos_sb, writes=[cos_sb])
                fw.dma("sp", sin_sb.t[:, :], s["rope"].ap()[1], sin_sb, writes=[sin_sb])
                krin = [self.sb(st, [128, 512], BF16, "krin", dma=True) for _ in range(2)]
                krout = [self.sb(st, [128, 512], BF16, "krout", dma=True) for _ in range(2)]
                pr = self.ps(st, [128, 512], F32, "prp")
                rtmp = self.sb(st, [128, 512], F32, "rtmp")
            for gi in range(ntok // 512):
                xi, o = xin[gi % 2], xo[gi % 2]
                fw.dma("sp", xi.t[:, :, :], raw.ap()[0:nch, :, gi * 512:(gi + 1) * 512].rearrange("c p t -> p c t"), xi,
                       writes=[xi])
                fw.op("dve", lambda e: e.tensor_tensor(out=sq.t[:, :, :], in0=xi.t[:, :, :], in1=xi.t[:, :, :], op=ALU.mult),
                      reads=[xi], writes=[sq])
                for k in range(nch):
                    fw.op("pe", lambda e: e.matmul(pss.t[:, :], self.ones_bf(), sq.t[:, k, :], start=(k == 0),
                                                   stop=(k == nch - 1)), reads=[sq, self.constb],
                          **({"writes": [pss]} if k == 0 else {"join": [pss]}))
                fw.op("act", lambda e: e.activation(out=rs.t[:, :], in_=pss.t[:, :], func=AF.Ln, scale=1.0 / feat,
                                                    bias=self.spc("eps6")), reads=[pss, self.smallp], writes=[rs])
                fw.op("act", lambda e: e.activation(out=rs.t[:, :], in_=rs.t[:, :], func=AF.Exp, scale=-0.5), reads=[rs],
                      join=[rs])
                for k in range(nch):
                    eng = "dve"
                    fw.op(eng, lambda e: e.scalar_tensor_tensor(out=o.t[:, :, k, :],
                                                                in0=xi.t[:, k, :].rearrange("p (j t) -> p j t", t=128),
                                                                scalar=self.spc(gname, k),
                                                                in1=rs.t[:, :].rearrange("p (j t) -> p j t", t=128),
                                                                op0=ALU.mult, op1=ALU.mult),
                          reads=[xi, rs, self.smallp], **({"writes": [o]} if k == 0 else {"join": [o]}))
                fw.dma("sp", dst.ap()[gi * 4:(gi + 1) * 4].rearrange("j p (c t) -> p j c t", t=128), o.t[:, :, :, :], o,
                       reads=[o])
                if rope_extra:
                    ki, ko = krin[gi % 2], krout[gi % 2]
                    fw.dma("sp", ki.t[:, :], raw.ap()[4, :, gi * 512:(gi + 1) * 512], ki, writes=[ki])
                    self.apply_rope(st, ki, 512, cos_sb, sin_sb, gi * 512, ko, pr, rtmp)
                    fw.dma("sp", s["kr"].ap()[:, gi * 512:(gi + 1) * 512], ko.t[:, :], ko, reads=[ko])
            fw.barrier()

    def ssd(self):
        c, fw, s = self.c, self.fw, self.s
        T, PT, TT, DI, H, R, CONVC = (c[k] for k in ("T", "PT", "TT", "DI", "H", "R", "CONVC"))
        CC = CONVC // 128
        XC = DI // 128
        GW = R * 64
        GCH = GW // 128
        PW = min(512, GW)
        NPC = GW // PW
        HS = min(4, R)
        npre = PT // 128
        tri = lambda: self.spc("tri", 0, 128)
        ntri = lambda: self.spc("ntri", 0, 128)
        onesf = lambda: self.spc("onesf", 0, 128)
        Abc = lambda: self.derived.t[:, 0:H]
        dskip = lambda: self.rowp.t[:, 2 * H:3 * H]
        with contextlib.ExitStack() as st:
            sb, ps = (lambda sh, dt, nm, dma=False: self.sb(st, sh, dt, nm, dma=dma)), (lambda sh, dt, nm: self.ps(st, sh, dt, nm))
            xc = [sb([128, CC, 128], BF16, "xc", True) for _ in range(2)]
            dtb = [sb([128, H], F32, "dt", True) for _ in range(2)]
            szb = [sb([128, GW], BF16, "sz", True) for _ in range(2)]
            ynT = [sb([128, XC, 128], BF16, "ynT", True) for _ in range(2)]
            state = sb([128, DI], F32, "state")
            stbf = sb([128, DI], BF16, "stbf")
            a_sb = sb([128, H], F32, "a")
            acs_sb = sb([128, H], F32, "acs")
            eacs = sb([128, H], F32, "eacs")
            eL = sb([128, H], F32, "eL")
            wd = sb([128, H], F32, "wd")
            dtw = sb([128, H], F32, "dtw")
            xtok_r = [sb([128, GW], BF16, "xtok") for _ in range(2)]
            xdt_r = [sb([128, GW], BF16, "xdt") for _ in range(2)]
            xdtw_r = [sb([128, GW], BF16, "xdtw") for _ in range(2)]
            btok_r = [sb([128, 128], BF16, "btok") for _ in range(2)]
            cbm_r = [sb([128, 128], BF16, "cbm") for _ in range(2)]
            Yg_r = [sb([128, HS, 128], F32, "Yg") for _ in range(2)]
            Eg_r = [sb([128, HS, 128], BF16, "Eg") for _ in range(2)]
            Mg_r = [sb([128, HS, 128], BF16, "Mg") for _ in range(2)]
            t1_r = [sb([128, GW], F32, "t1") for _ in range(2)]
            t2_r = [sb([128, GW], F32, "t2") for _ in range(2)]
            yn_r = [sb([128, GW], BF16, "yn") for _ in range(2)]
            junk_r = [sb([128, GW], BF16, "junk") for _ in range(2)]
            ssq_r = [sb([128, 2], F32, "ssq") for _ in range(2)]
            psX = ps([128, GW], BF16, "psX")
            psBC = ps([128, 512], F32, "psBC")
            psX2 = ps([128, GW], BF16, "psX2")
            nsg = 0
            psBt = ps([128, 128], BF16, "psBt")
            psY2 = ps([128, PW], F32, "psY2")
            psS = ps([128, PW], F32, "psS")
            psD = ps([128, HS * 128], F32, "psD")
            psY1 = ps([128, PW], F32, "psY1")
            fw.op("dve", lambda e: e.memset(state.t[:, :], 0.0), writes=[state])
            fw.op("pool", lambda e: e.memset(stbf.t[:, :], 0.0), writes=[stbf])
            for j in range(TT // 128):
                own = j >= npre
                jo = j - npre
                x, d_, yo = xc[j % 2], dtb[j % 2], ynT[j % 2]
                fw.dma("sp", x.t[:, :, :], s["xc"].ap()[j].rearrange("p (c t) -> p c t", t=128), x, writes=[x])
                fw.dma("sp", d_.t[:, :], s["dt"].ap()[j * 128:(j + 1) * 128, :], d_, writes=[d_])
                fw.op("dve", lambda e: e.tensor_tensor(out=a_sb.t[:, :], in0=d_.t[:, :], in1=Abc(), op=ALU.mult),
                      reads=[d_, self.derived], writes=[a_sb])
                fw.op("pe", lambda e: e.matmul(psBC.t[:, 128:128 + H], tri(), a_sb.t[:, :], start=True, stop=True),
                      reads=[a_sb, self.smallp], writes=[psBC])
                fw.op("pe", lambda e: e.matmul(psBC.t[:, 128 + H:128 + 2 * H], onesf(), a_sb.t[:, :], start=True, stop=True),
                      reads=[a_sb, self.smallp], join=[psBC])
                fw.op("act", lambda e: e.activation(out=acs_sb.t[:, :], in_=psBC.t[:, 128:128 + H], func=AF.Copy), reads=[psBC],
                      writes=[acs_sb])
                fw.op("act", lambda e: e.activation(out=eL.t[:, :], in_=psBC.t[:, 128 + H:128 + 2 * H], func=AF.Exp), reads=[psBC],
                      writes=[eL])
                fw.op("dve", lambda e: e.tensor_tensor(out=wd.t[:, :], in0=psBC.t[:, 128 + H:128 + 2 * H], in1=acs_sb.t[:, :],
                                                       op=ALU.subtract), reads=[psBC, acs_sb], writes=[wd])
                fw.op("act", lambda e: e.activation(out=wd.t[:, :], in_=wd.t[:, :], func=AF.Exp), reads=[wd], join=[wd])
                fw.op("dve", lambda e: e.tensor_tensor(out=dtw.t[:, :], in0=wd.t[:, :], in1=d_.t[:, :], op=ALU.mult),
                      reads=[wd, d_], writes=[dtw])
                if own:
                    fw.op("act", lambda e: e.activation(out=eacs.t[:, :], in_=acs_sb.t[:, :], func=AF.Exp),
                          reads=[acs_sb], writes=[eacs])
                for g in range(8):
                    h0 = g * R
                    gc0 = g * GW
                    bch = XC + g
                    cch = XC + 8 + g
                    rg = (j * 8 + g) % 2
                    xtok, xdt, xdtw, btok, cbm = xtok_r[rg], xdt_r[rg], xdtw_r[rg], btok_r[rg], cbm_r[rg]
                    t1, t2, yn, junk, ssq = t1_r[rg], t2_r[rg], yn_r[rg], junk_r[rg], ssq_r[rg]
                    for i in range(GCH):
                        fw.op("pe", lambda e: e.transpose(out=psX.t[:, i * 128:(i + 1) * 128], in_=x.t[:, g * GCH + i, :],
                                                          identity=self.ident()), reads=[x, self.constb],
                              **({"writes": [psX]} if i == 0 else {"join": [psX]}))
                    fw.op("pe", lambda e: e.transpose(out=psBt.t[:, :], in_=x.t[:, bch, :], identity=self.ident()),
                          reads=[x, self.constb], writes=[psBt])
                    fw.op("act", lambda e: e.activation(out=xtok.t[:, :], in_=psX.t[:, :], func=AF.Copy), reads=[psX],
                          writes=[xtok])
                    fw.op("act", lambda e: e.activation(out=btok.t[:, :], in_=psBt.t[:, :], func=AF.Copy), reads=[psBt],
                          writes=[btok])
                    fw.op("pool", lambda e: e.tensor_tensor(out=xdtw.t[:, :].rearrange("p (h d) -> p h d", d=64),
                                                            in0=xtok.t[:, :].rearrange("p (h d) -> p h d", d=64),
                                                            in1=bc_last(dtw.t[:, h0:h0 + R], 64), op=ALU.mult),
                          reads=[xtok, dtw], writes=[xdtw])
                    if own:
                        fw.op("dve", lambda e: e.tensor_tensor(out=xdt.t[:, :].rearrange("p (h d) -> p h d", d=64),
                                                               in0=xtok.t[:, :].rearrange("p (h d) -> p h d", d=64),
                                                               in1=bc_last(d_.t[:, h0:h0 + R], 64), op=ALU.mult),
                              reads=[xtok, d_], writes=[xdt])
                        fw.op("pe", lambda e: e.matmul(psBC.t[:, 0:128], x.t[:, bch, :], x.t[:, cch, :], start=True,
                                                       stop=True), reads=[x], writes=[psBC])
                        fw.op("dve", lambda e: e.tensor_tensor(out=cbm.t[:, :], in0=psBC.t[:, 0:128], in1=tri(),
                                                               op=ALU.mult), reads=[psBC, self.smallp], writes=[cbm])
                        fw.op("pool", lambda e: e.memset(ssq.t[:, :], 0.0), writes=[ssq])
                        fw.dma("sp", szb[g % 2].t[:, :], s["sz"].ap()[jo * 128:(jo + 1) * 128, gc0:gc0 + GW], szb[g % 2],
                               writes=[szb[g % 2]])
                    for pc in range(NPC):
                        c0 = pc * PW
                        hp0 = h0 + c0 // 64
                        nhp = PW // 64
                        if own:
                            fw.op("pe", lambda e: e.matmul(psY2.t[:, :], x.t[:, cch, :], stbf.t[:, gc0 + c0:gc0 + c0 + PW],
                                                           start=True, stop=True), reads=[x, stbf], writes=[psY2])
                            for sg in range(nhp // HS):
                                hh0 = hp0 + sg * HS
                                Yg, Eg, Mg = Yg_r[nsg % 2], Eg_r[nsg % 2], Mg_r[nsg % 2]
                                nsg += 1
                                fw.op("dve", lambda e: e.tensor_tensor(out=Yg.t[:, :, :], in0=bc_mid(tri(), HS),
                                                                       in1=bc_last(a_sb.t[:, hh0:hh0 + HS], 128),
                                                                       op=ALU.mult), reads=[a_sb, self.smallp],
                                      writes=[Yg])
                                fw.op("pe", lambda e: e.matmul(psD.t[:, :], ntri(),
                                                               Yg.t[:, :, :].rearrange("p h l -> p (h l)"), start=True,
                                                               stop=True), reads=[Yg, self.smallp], writes=[psD])
                                fw.op("act", lambda e: e.activation(out=Eg.t[:, :, :].rearrange("p h l -> p (h l)"),
                                                                    in_=psD.t[:, :], func=AF.Exp), reads=[psD],
                                      writes=[Eg])
                                fw.op("pool", lambda e: e.tensor_tensor(out=Mg.t[:, :, :], in0=Eg.t[:, :, :],
                                                                        in1=bc_mid(cbm.t[:, :], HS), op=ALU.mult),
                                      reads=[Eg, cbm], writes=[Mg])
                                for hi in range(HS):
                                    lc = (sg * HS + hi) * 64
                                    fw.op("pe", lambda e: e.matmul(psY1.t[:, lc:lc + 64], Mg.t[:, hi, :],
                                                                   xdt.t[:, c0 + lc:c0 + lc + 64], start=True, stop=True),
                                          reads=[Mg, xdt], **({"writes": [psY1]} if (sg == 0 and hi == 0) else {"join": [psY1]}))
                            v3 = lambda ap: ap.rearrange("p (h d) -> p h d", d=64)
                            fw.op("dve", lambda e: e.tensor_tensor(out=v3(t1.t[:, c0:c0 + PW]), in0=v3(psY2.t[:, :]),
                                                                   in1=bc_last(eacs.t[:, hp0:hp0 + nhp], 64), op=ALU.mult),
                                  reads=[psY2, eacs], **({"writes": [t1]} if pc == 0 else {"join": [t1]}))
                            fw.op("dve", lambda e: e.tensor_tensor(out=t1.t[:, c0:c0 + PW], in0=t1.t[:, c0:c0 + PW],
                                                                   in1=psY1.t[:, :], op=ALU.add), reads=[t1, psY1],
                                  join=[t1])
                        fw.op("pe", lambda e: e.matmul(psS.t[:, :], btok.t[:, :], xdtw.t[:, c0:c0 + PW], start=True,
                                                       stop=True), reads=[btok, xdtw], writes=[psS])
                        sv = state.t[:, gc0 + c0:gc0 + c0 + PW]
                        fw.op("pool", lambda e: e.tensor_tensor(out=sv.rearrange("p (h d) -> p h d", d=64),
                                                                in0=sv.rearrange("p (h d) -> p h d", d=64),
                                                                in1=bc_last(eL.t[:, hp0:hp0 + nhp], 64), op=ALU.mult),
                              reads=[state, eL, stbf], join=[state])
                        fw.op("dve", lambda e: e.tensor_tensor(out=sv, in0=sv, in1=psS.t[:, :], op=ALU.add),
                              reads=[state, psS], join=[state])
                        fw.op("act", lambda e: e.activation(out=stbf.t[:, gc0 + c0:gc0 + c0 + PW], in_=sv, func=AF.Copy),
                              reads=[state], join=[stbf])
                    if own:
                        z = szb[g % 2]
                        fw.op("pool", lambda e: e.tensor_tensor(out=t2.t[:, :].rearrange("p (h d) -> p h d", d=64),
                                                                in0=xtok.t[:, :].rearrange("p (h d) -> p h d", d=64),
                                                                in1=bc_last(dskip()[:, h0:h0 + R], 64), op=ALU.mult),
                              reads=[xtok, self.rowp], writes=[t2])
                        fw.op("dve", lambda e: e.tensor_tensor(out=t1.t[:, :], in0=t1.t[:, :], in1=t2.t[:, :], op=ALU.add),
                              reads=[t1, t2], join=[t1])
                        fw.op("dve", lambda e: e.tensor_tensor(out=t1.t[:, :], in0=t1.t[:, :], in1=z.t[:, :], op=ALU.mult),
                              reads=[t1, z], join=[t1])
                        fw.op("act", lambda e: e.activation(out=junk.t[:, :], in_=t1.t[:, :], func=AF.Square,
                                                            accum_out=ssq.t[:, 0:1]), reads=[t1, ssq], writes=[junk],
                              join=[ssq])
                        fw.op("act", lambda e: e.activation(out=ssq.t[:, 1:2], in_=ssq.t[:, 0:1], func=AF.Ln, scale=1.0 / GW,
                                                            bias=self.spc("eps5")), reads=[ssq, self.smallp], join=[ssq])
                        fw.op("act", lambda e: e.activation(out=ssq.t[:, 1:2], in_=ssq.t[:, 1:2], func=AF.Exp, scale=-0.5),
                              reads=[ssq], join=[ssq])
                        fw.op("dve", lambda e: e.tensor_scalar(out=yn.t[:, :], in0=t1.t[:, :], scalar1=ssq.t[:, 1:2],
                                                               scalar2=None, op0=ALU.mult), reads=[t1, ssq], writes=[yn])
                        for i in range(GCH):
                            ch = g * GCH + i
                            fw.op("pe", lambda e: e.transpose(out=psX2.t[:, i * 128:(i + 1) * 128],
                                                              in_=yn.t[:, i * 128:(i + 1) * 128], identity=self.ident()),
                                  reads=[yn, self.constb], **({"writes": [psX2]} if i == 0 else {"join": [psX2]}))
                        for i in range(GCH):
                            ch = g * GCH + i
                            fw.op("act", lambda e: e.activation(out=yo.t[:, ch, :], in_=psX2.t[:, i * 128:(i + 1) * 128],
                                                                func=AF.Copy, scale=self.spc("ssm_g", ch)),
                                  reads=[psX2, self.smallp], **({"writes": [yo]} if (g == 0 and i == 0) else {"join": [yo]}))
                if own:
                    fw.dma("sp", s["ynT"].ap()[jo].rearrange("p (c t) -> p c t", t=128), yo.t[:, :, :], yo, reads=[yo])
            fw.barrier()

    def mla(self):
        c, fw, s = self.c, self.fw, self.s
        T, PT, TT, MH, QL, D = (c[k] for k in ("T", "PT", "TT", "MH", "QL", "D"))
        QC = QL // 128
        npre = PT // 128
        allt = list(range(TT // 128))
        ownq = list(range(T // 128))

        def mkF(dst, tok_off):
            def ex(st):
                return self.ring(st, 2, [128, 4, 1024], BF16, "stF")

            def epi(k):
                b = k["ctx"]()
                nci_n = k["cw"] // 128
                fz = True
                for nci in range(nci_n):
                    for th in range(k["NTH"]):
                        idx = nci * k["NTH"] + th
                        src = k["ps"].t[:, idx * 512:idx * 512 + k["THW"]]
                        dv = b.t[:, nci, th * 512:th * 512 + k["THW"]]
                        if idx % 2:
                            fw.op("dve", lambda e: e.tensor_copy(out=dv, in_=src), reads=[k["ps"]],
                                  **({"writes": [b]} if fz else {"join": [b]}))
                        else:
                            fw.op("act", lambda e: e.activation(out=dv, in_=src, func=AF.Copy), reads=[k["ps"]],
                                  **({"writes": [b]} if fz else {"join": [b]}))
                        fz = False
                t0 = k["tiles"][0] * 128 - tok_off
                for nci in range(nci_n):
                    fw.dma("sp", dst.ap()[k["c0"] // 128 + nci, :, t0:t0 + k["TG"]], b.t[:, nci, 0:k["TG"]], b, reads=[b])
            return ex, epi

        ex, epi = mkF(s["kn"], 0)
        self.gemm(s["ckv"], allt, 4, [self.i["w_kn"]], 0, MH * 128, "F", epi, extra=ex)

        def ex_v(st):
            return self.ring(st, 2, [128, 8, 256], BF16, "stv")

        def epi_v(k):
            b = k["ctx"]()
            for tt in range(k["TGt"]):
                src = k["ps"].t[:, tt * k["CW"]:tt * k["CW"] + k["cw"]]
                if tt % 2:
                    fw.op("dve", lambda e: e.tensor_copy(out=b.t[:, tt, 0:k["cw"]], in_=src), reads=[k["ps"]],
                          **({"writes": [b]} if tt == 0 else {"join": [b]}))
                else:
                    fw.op("act", lambda e: e.activation(out=b.t[:, tt, 0:k["cw"]], in_=src, func=AF.Copy),
                          reads=[k["ps"]], **({"writes": [b]} if tt == 0 else {"join": [b]}))
            r0 = k["tiles"][0] * 128
            fw.dma("sp", s["v"].ap()[r0:r0 + k["TG"], k["c0"]:k["c0"] + k["cw"]].rearrange("(j p) n -> p j n", p=128),
                   b.t[:, 0:k["TGt"], 0:k["cw"]], b, reads=[b])
        self.gemm(s["ckv"], allt, 4, [self.i["w_v"]], 0, MH * 128, "T", epi_v, extra=ex_v)
        ex, epi = mkF(s["qn"], 0)
        self.gemm(s["cq"], ownq, QC, [self.i["w_qn"]], 0, MH * 128, "F", epi, extra=ex)
        ex, epi = mkF(s["qrr"], 0)
        self.gemm(s["cq"], ownq, QC, [self.i["w_qr"]], 0, MH * 64, "F", epi, extra=ex)
        with contextlib.ExitStack() as st:
            cos_sb = self.sb(st, [128, T], F32, "cos", dma=True)
            sin_sb = self.sb(st, [128, T], F32, "sin", dma=True)
            fw.dma("sp", cos_sb.t[:, :], s["rope"].ap()[0, :, PT:TT], cos_sb, writes=[cos_sb])
            fw.dma("sp", sin_sb.t[:, :], s["rope"].ap()[1, :, PT:TT], sin_sb, writes=[sin_sb])
            qi = [self.sb(st, [128, T], BF16, "qri", dma=True) for _ in range(2)]
            qo = [self.sb(st, [128, T], BF16, "qro", dma=True) for _ in range(2)]
            pr = self.ps(st, [128, 512], F32, "prq")
            rtmp = self.sb(st, [128, 512], F32, "rtmpq")
            for hp in range(MH // 2):
                a, o = qi[hp % 2], qo[hp % 2]
                fw.dma("sp", a.t[:, :], s["qrr"].ap()[hp], a, writes=[a])
                self.apply_rope(st, a, T, cos_sb, sin_sb, 0, o, pr, rtmp)
                fw.dma("sp", s["qr"].ap()[hp], o.t[:, :], o, reads=[o])
            fw.barrier()
        scale = float((128 + 64) ** -0.5)
        NK = TT // 128
        NQS = T // 512
        with contextlib.ExitStack() as st:
            kr = self.sb(st, [128, TT], BF16, "kr", dma=True)
            fw.dma("sp", kr.t[:, :], s["kr"].ap(), kr, writes=[kr])
            kn = [self.sb(st, [128, TT], BF16, "kn", dma=True) for _ in range(2)]
            qn = [self.sb(st, [128, T], BF16, "qn", dma=True) for _ in range(2)]
            qr = [self.sb(st, [128, T], BF16, "qr", dma=True) for _ in range(2)]
            vv = [self.sb(st, [128, NK, 128], BF16, "vv", dma=True) for _ in range(2)]
            pt = [self.sb(st, [128, 512], BF16, "pt") for _ in range(3)]
            psS = [self.ps(st, [128, 512], F32, "psS") for _ in range(2)]
            acc = [self.ps(st, [128, 4, 256], F32, "acc") for _ in range(2)]
            psT = self.ps(st, [128, 4, 128], BF16, "psT")
            rden = self.sb(st, [128, 4], F32, "rden")
            abf = self.sb(st, [128, 4, 128], BF16, "abf")
            aT = [self.sb(st, [128, 4, 128], BF16, "aT", dma=True) for _ in range(2)]
            npt = 0
            nsb = 0
            nacc = 0
            for h in range(MH):
                k_, q_, v_ = kn[h % 2], qn[h % 2], vv[h % 2]
                half = (h % 2) * 64
                fw.dma("sp", k_.t[:, :], s["kn"].ap()[h], k_, writes=[k_])
                fw.dma("sp", q_.t[:, :], s["qn"].ap()[h], q_, writes=[q_])
                fw.dma("sp", v_.t[:, :, :], s["v"].ap()[:, h * 128:(h + 1) * 128].rearrange("(j p) d -> p j d", p=128), v_,
                       writes=[v_])
                if h % 2 == 0:
                    qrb = qr[(h // 2) % 2]
                    fw.dma("sp", qrb.t[:, :], s["qr"].ap()[h // 2], qrb, writes=[qrb])
                for qs in range(NQS):
                    ac = acc[nacc % 2]
                    nacc += 1
                    nkt = npre + 4 * qs + 4
                    for kt in range(nkt):
                        o = kt - npre
                        qi0 = 0 if (kt < npre or o < 4 * qs) else (o - 4 * qs)
                        q0 = qi0 * 128
                        diag = (kt >= npre and o >= 4 * qs)
                        pS = psS[nsb % 2]
                        nsb += 1
                        p_ = pt[npt % 3]
                        npt += 1
                        qa, qb = qs * 512 + q0, (qs + 1) * 512
                        fw.op("pe", lambda e: e.matmul(pS.t[:, q0:512], k_.t[:, kt * 128:(kt + 1) * 128], q_.t[:, qa:qb],
                                                       start=True, stop=False), reads=[k_, q_], writes=[pS], inc=False)
                        fw.op("pe", lambda e: e.matmul(pS.t[:, q0:512], kr.t[half:half + 64, kt * 128:(kt + 1) * 128],
                                                       qrb.t[half:half + 64, qa:qb], start=False, stop=True),
                              reads=[kr, qrb], join=[pS])
                        if not diag:
                            bias = self.derived.t[:, c["H"]:c["H"] + 1] if kt < npre else 0.0
                            fw.op("act", lambda e: e.activation(out=p_.t[:, 0:512], in_=pS.t[:, 0:512], func=AF.Exp,
                                                                bias=bias, scale=scale), reads=[pS, self.derived],
                                  writes=[p_])
                        else:
                            fw.op("act", lambda e: e.activation(out=p_.t[0:64, q0:512], in_=pS.t[0:64, q0:512],
                                                                func=AF.Exp, scale=scale), reads=[pS], writes=[p_])
                            if q0 + 64 < 512 or True:
                                fw.op("act", lambda e: e.activation(out=p_.t[64:128, q0 + 64:512],
                                                                    in_=pS.t[64:128, q0 + 64:512], func=AF.Exp,
                                                                    scale=scale), reads=[pS], join=[p_])
                            fw.op("pool", lambda e: e.memset(p_.t[64:128, q0:q0 + 64], 0.0), join=[p_])
                        for qi in range(qi0, 4):
                            own_tile = 4 * qs + qi
                            st_ = (kt == 0)
                            sp_ = (kt == npre + own_tile)
                            fw.op("pe", lambda e: e.matmul(ac.t[:, qi, 0:128], p_.t[:, qi * 128:(qi + 1) * 128],
                                                           v_.t[:, kt, :], start=(st_ and qi % 2 == 0), stop=sp_,
                                                           skip_group_check=True), reads=[p_, v_], inc=False,
                                  **({"writes": [ac]} if (kt == 0 and qi == qi0) else {"join": [ac]}))
                            fw.op("pe", lambda e: e.matmul(ac.t[:, qi, 128:129], p_.t[:, qi * 128:(qi + 1) * 128],
                                                           self.ones_bf()[:, 0:1], start=False, stop=sp_,
                                                           skip_group_check=True),
                                  reads=[p_, self.constb], join=[ac], inc=(qi == 3))
                    fw.op("dve", lambda e: e.reciprocal(out=rden.t[:, :], in_=ac.t[:, :, 128]), reads=[ac], writes=[rden])
                    for qi in range(4):
                        fw.op("dve", lambda e: e.tensor_scalar(out=abf.t[:, qi, :], in0=ac.t[:, qi, 0:128],
                                                               scalar1=rden.t[:, qi:qi + 1], scalar2=None, op0=ALU.mult),
                              reads=[ac, rden], **({"writes": [abf]} if qi == 0 else {"join": [abf]}))
                    for qi in range(4):
                        fw.op("pe", lambda e: e.transpose(out=psT.t[:, qi, :], in_=abf.t[:, qi, :], identity=self.ident()),
                              reads=[abf, self.constb], **({"writes": [psT]} if qi == 0 else {"join": [psT]}))
                    at = aT[(h * NQS + qs) % 2]
                    fw.op("act", lambda e: e.activation(out=at.t[:, :, :], in_=psT.t[:, :, :], func=AF.Copy), reads=[psT],
                          writes=[at])
                    fw.dma("sp", s["attnT"].ap()[qs * 4:(qs + 1) * 4, :, h * 128:(h + 1) * 128].rearrange("j p t -> p j t"),
                           at.t[:, :, :], at, reads=[at])
            fw.barrier()

    def merge_out(self):
        c, fw, s = self.c, self.fw, self.s
        T, D, DI, MH = c["T"], c["D"], c["DI"], c["MH"]
        DC = D // 128
        own = list(range(T // 128))

        def mk(first):
            def ex(st):
                return (self.ring(st, 2, [128, 4, 1024], BF16, "gt"), self.ring(st, 2, [128, 4, 1024], BF16, "yg"),
                        self.ring(st, 2, [128, 8, 4, 128], BF16, "mo"))

            def epi(k):
                gring, yring, oring = k["ctx"]
                gt = gring()
                nci_n = k["cw"] // 128
                t0 = k["tiles"][0] * 128
                TG = k["TG"]
                for nci in range(nci_n):
                    ch = k["c0"] // 128 + nci + (0 if first else DC)
                    fw.dma("sp", gt.t[:, nci, 0:TG], s["gates"].ap()[ch, :, t0:t0 + TG], gt,
                           **({"writes": [gt]} if nci == 0 else {"join": [gt]}))
                if first:
                    o = yring()
                else:
                    yg = yring()
                    for nci in range(nci_n):
                        ch = k["c0"] // 128 + nci
                        fw.dma("sp", yg.t[:, nci, 0:TG], s["yg"].ap()[ch, :, t0:t0 + TG], yg,
                               **({"writes": [yg]} if nci == 0 else {"join": [yg]}))
                    o = oring()
                fz = True
                for nci in range(nci_n):
                    for th in range(k["NTH"]):
                        idx = nci * k["NTH"] + th
                        w = k["THW"]
                        src = k["ps"].t[:, idx * 512:idx * 512 + w]
                        gv = gt.t[:, nci, th * 512:th * 512 + w]
                        if first:
                            fw.op("dve", lambda e: e.tensor_tensor(out=o.t[:, nci, th * 512:th * 512 + w], in0=src, in1=gv,
                                                                   op=ALU.mult), reads=[k["ps"], gt],
                                  **({"writes": [o]} if fz else {"join": [o]}))
                        else:
                            yv = yg.t[:, nci, th * 512:th * 512 + w]
                            fw.op("dve", lambda e: e.tensor_tensor(out=gv, in0=src, in1=gv, op=ALU.mult),
                                  reads=[k["ps"], gt], join=[gt])
                            fw.op("pool", lambda e: e.tensor_tensor(
                                out=o.t[:, th * 4:th * 4 + w // 128, nci, :],
                                in0=gv.rearrange("p (j t) -> p j t", t=128), in1=yv.rearrange("p (j t) -> p j t", t=128),
                                op=ALU.add), reads=[gt, yg], **({"writes": [o]} if fz else {"join": [o]}))
                        fz = False
                if first:
                    for nci in range(nci_n):
                        ch = k["c0"] // 128 + nci
                        fw.dma("sp", s["yg"].ap()[ch, :, t0:t0 + TG], o.t[:, nci, 0:TG], o, reads=[o])
                else:
                    cb = k["c0"] // 128
                    fw.dma("sp", s["mg"].ap()[k["tiles"][0]:k["tiles"][0] + k["TGt"]].rearrange(
                        "j p (c t) -> p j c t", t=128)[:, :, cb:cb + nci_n, :], o.t[:, 0:k["TGt"], 0:nci_n, :], o, reads=[o])
            return ex, epi

        ex, epi = mk(True)
        self.gemm(s["ynT"], own, DI // 128, [self.i["w_ssm_out"]], 0, D, "F", epi, extra=ex)
        ex, epi = mk(False)
        self.gemm(s["attnT"], own, MH, [self.i["w_mla_out"]], 0, D, "F", epi, extra=ex)
        self.resid_gemm(s["mg"], DC, self.i["w_out"], self.i["x_own"], s["h"])

    def resid_gemm(self, A_d, KC, W, res_d, dst_d):
        c, fw = self.c, self.fw
        T, D = c["T"], c["D"]
        own = list(range(T // 128))

        def ex(st):
            return self.ring(st, 2, [128, 8, 256], F32, "rs")

        def epi(k):
            b = k["ctx"]()
            r0 = k["tiles"][0] * 128
            rv = lambda d_: d_.ap()[r0:r0 + k["TG"], k["c0"]:k["c0"] + k["cw"]].rearrange("(j p) n -> p j n", p=128)
            fw.dma("sp", b.t[:, 0:k["TGt"], 0:k["cw"]], rv(res_d), b, writes=[b])
            for tt in range(k["TGt"]):
                fw.op("dve", lambda e: e.tensor_tensor(out=b.t[:, tt, 0:k["cw"]], in0=b.t[:, tt, 0:k["cw"]],
                                                       in1=k["ps"].t[:, tt * k["CW"]:tt * k["CW"] + k["cw"]], op=ALU.add),
                      reads=[b, k["ps"]], join=[b])
            fw.dma("sp", rv(dst_d), b.t[:, 0:k["TGt"], 0:k["cw"]], b, reads=[b])
        self.gemm(A_d, own, KC, [W], 0, D, "T", epi, extra=ex)

    def ffn(self):
        c, fw, s = self.c, self.fw, self.s
        T, D, DFF = c["T"], c["D"], c["DFF"]
        DC, FC = D // 128, DFF // 128
        own = list(range(T // 128))

        def ex(st):
            return (self.sb(st, [128, 1024], F32, "sg"), self.ring(st, 2, [128, 8, 128], BF16, "ao"))

        def epi(k):
            sg, oring = k["ctx"]
            o = oring()
            for th in range(k["NTH"]):
                w = k["THW"]
                fw.op("act", lambda e: e.activation(out=sg.t[:, th * 512:th * 512 + w], in_=k["ps"].t[:, th * 512:th * 512 + w],
                                                    func=AF.Silu), reads=[k["ps"]],
                      **({"writes": [sg]} if th == 0 else {"join": [sg]}))
                fw.op("dve", lambda e: e.tensor_tensor(
                    out=o.t[:, th * 4:th * 4 + w // 128, :],
                    in0=sg.t[:, th * 512:th * 512 + w].rearrange("p (j t) -> p j t", t=128),
                    in1=k["ps"].t[:, (k["NTH"] + th) * 512:(k["NTH"] + th) * 512 + w].rearrange("p (j t) -> p j t", t=128),
                    op=ALU.mult), reads=[sg, k["ps"]], **({"writes": [o]} if th == 0 else {"join": [o]}))
            ch = k["c0"] // 128
            fw.dma("sp", s["act"].ap()[k["tiles"][0]:k["tiles"][0] + k["TGt"], :, ch * 128:(ch + 1) * 128].rearrange(
                "j p t -> p j t"), o.t[:, 0:k["TGt"], :], o, reads=[o])
        self.gemm(s["nT"], own, DC, [self.i["w_gate"], self.i["w_up"]], 0, DFF, "F", epi, extra=ex)
        self.resid_gemm(s["act"], FC, self.i["w_down"], s["h"], s["h2"])

    def final_norm(self):
        c, fw, s = self.c, self.fw, self.s
        T, D, H = c["T"], c["D"], c["H"]
        with contextlib.ExitStack() as st:
            gb = self.sb(st, [128, D], F32, "gfin", dma=True)
            fw.dma("sp", gb.t[:, :], self.i["rowp"].ap()[0:1, 3 * H:3 * H + D].broadcast_to([128, D]), gb, writes=[gb])
            xb = [self.sb(st, [128, D], F32, "fx", dma=True) for _ in range(2)]
            junk = self.sb(st, [128, D], BF16, "fj")
            ss = [self.sb(st, [128, 2], F32, "fs") for _ in range(2)]
            for j in range(T // 128):
                x, s_ = xb[j % 2], ss[j % 2]
                fw.dma("sp", x.t[:, :], s["h2"].ap()[j * 128:(j + 1) * 128, :], x, writes=[x])
                fw.op("dve", lambda e: e.memset(s_.t[:, 0:2], 0.0), writes=[s_])
                fw.op("act", lambda e: e.activation(out=junk.t[:, :], in_=x.t[:, :], func=AF.Square, accum_out=s_.t[:, 0:1]),
                      reads=[x, s_], writes=[junk], join=[s_])
                fw.op("act", lambda e: e.activation(out=s_.t[:, 1:2], in_=s_.t[:, 0:1], func=AF.Ln, scale=1.0 / D,
                                                    bias=self.spc("eps6")), reads=[s_, self.smallp], join=[s_])
                fw.op("act", lambda e: e.activation(out=s_.t[:, 1:2], in_=s_.t[:, 1:2], func=AF.Exp, scale=-0.5),
                      reads=[s_], join=[s_])
                fw.op("dve", lambda e: e.scalar_tensor_tensor(out=x.t[:, :], in0=x.t[:, :], scalar=s_.t[:, 1:2],
                                                              in1=gb.t[:, :], op0=ALU.mult, op1=ALU.mult),
                      reads=[x, s_, gb], join=[x])
                fw.dma("sp", self.out.ap()[j * 128:(j + 1) * 128, :], x.t[:, :], x, reads=[x])
            fw.barrier()

    def build(self, upto=99):
        c = self.c
        self.declare()
        self.fw = FW(self.nc, self.top)
        self.load_consts()
        s, i = self.s, self.i
        NP, NO = c["PT"] // 128, c["T"] // 128
        srcs = [i["x_pre"].ap()[j * 128:(j + 1) * 128, :] for j in range(NP)] + \
               [i["x_own"].ap()[j * 128:(j + 1) * 128, :] for j in range(NO)]
        phases = [
            lambda: self.norm_transpose(srcs, "g_mix", s["uT"].ap()),
            self.in_proj,
            self.conv_phase,
            self.rope_tables,
            lambda: self.latent_norm(s["cqr"], c["QL"] // 128, c["T"], "q_g", s["cq"], c["QL"], False),
            lambda: self.latent_norm(s["ckvr"], 4, c["TT"], "kv_g", s["ckv"], c["KVL"], True),
            self.ssd,
            self.mla,
            self.merge_out,
            lambda: self.norm_transpose([s["h"].ap()[j * 128:(j + 1) * 128, :] for j in range(NO)], "g_ffn", s["nT"].ap()),
            self.ffn,
            self.final_norm,
        ]
        for pi, ph in enumerate(phases):
            if pi < upto:
                ph()
        self.top.close()
        return self.nc


def small_layout(c):
    D, DI, H, CONVC, QL = c["D"], c["DI"], c["H"], c["CONVC"], c["QL"]
    cols = {}
    n = 0
    for name, w in (("g_mix", D // 128), ("g_ffn", D // 128), ("conv_w", CONVC // 128 * 4), ("conv_b", CONVC // 128),
                    ("ssm_g", DI // 128), ("q_g", QL // 128), ("kv_g", 4), ("gate_bias", 2 * D // 128), ("flag", 1),
                    ("invf", 1), ("sgn", 1), ("eps6", 1), ("eps5", 1), ("one", 1), ("tri", 128), ("ntri", 128), ("onesf", 128)):
        cols[name] = n
        n += w
    cols["_n"] = n
    return cols


def host_prep(c, inp):
    D, T, PT, TT, DI, H, CONVC, QL, KVL, MH, SEQ = (c[k] for k in
                                                     ("D", "T", "PT", "TT", "DI", "H", "CONVC", "QL", "KVL", "MH", "SEQ"))
    f = lambda a: np.ascontiguousarray(np.asarray(a), dtype=np.float32)
    pc = lambda v: f(v).reshape(-1, 128).T
    cols = small_layout(c)
    sp = np.zeros((128, cols["_n"]), np.float32)

    def put(name, a):
        a = np.asarray(a, np.float32)
        sp[:, cols[name]:cols[name] + a.shape[1]] = a
    put("g_mix", pc(inp["g_mix"][0]))
    put("g_ffn", pc(inp["g_ffn"][0]))
    cw = f(inp["conv_w"][0])
    put("conv_w", cw.T.reshape(CONVC // 128, 128, 4).transpose(1, 0, 2).reshape(128, -1))
    put("conv_b", pc(inp["conv_b"][0]))
    put("ssm_g", pc(inp["ssm_norm_g"][0]))
    put("q_g", pc(inp["q_norm_g"][0]))
    put("kv_g", pc(inp["kv_norm_g"][0]))
    put("gate_bias", pc(f(inp["gate_bias"][0]).reshape(-1)))
    half = 32
    invf = (np.float32(10000.0) ** (-np.arange(half, dtype=np.float32) / np.float32(half))).astype(np.float32)
    put("invf", np.tile(invf, 4)[:, None])
    put("sgn", np.tile(np.concatenate([-np.ones(32), np.ones(32)]), 2)[:, None])
    put("eps6", np.full((128, 1), 1e-6))
    put("eps5", np.full((128, 1), 1e-5))
    put("one", np.ones((128, 1)))
    k = np.arange(128)
    put("tri", (k[:, None] <= k[None, :]).astype(np.float32))
    put("ntri", (k[:, None] > k[None, :]).astype(np.float32))
    put("onesf", np.ones((128, 128), np.float32))
    constb = np.zeros((128, 384), np.float32)
    constb[:, 0:128] = np.eye(128)
    constb[:, 128:256] = 1.0
    pm = np.zeros((128, 128), np.float32)
    for m in range(128):
        pm[(m // 64) * 64 + ((m % 64) + 32) % 64, m] = 1.0
    constb[:, 256:384] = pm
    constb = constb.astype(ml_dtypes.bfloat16)
    rowp = np.concatenate([f(inp["dt_bias"][0]), f(inp["a_log"][0]), f(inp["d_skip"][0]), f(inp["g_final"])])[None, :]
    w_in = f(inp["w_in"][0])
    o = DI + CONVC + H + QL
    w_ckv = np.ascontiguousarray(np.concatenate([w_in[:, o:o + 512], w_in[:, o + 512:o + 576], w_in[:, o + 512:o + 576]], 1))
    wq = f(inp["w_q_up"][0]).reshape(QL, MH, 192)
    wkv = f(inp["w_kv_up"][0]).reshape(KVL, MH, 256)
    shared = {
        "w_in_z": np.ascontiguousarray(w_in[:, 0:DI]), "w_in_xbc": np.ascontiguousarray(w_in[:, DI:DI + CONVC]),
        "w_in_r": np.ascontiguousarray(np.concatenate([w_in[:, DI + CONVC:o], w_in[:, o + 576:]], 1)), "w_ckv": w_ckv, "w_ssm_out": f(inp["w_ssm_out"][0]),
        "w_qn": np.ascontiguousarray(wq[:, :, :128].reshape(QL, -1)),
        "w_qr": np.ascontiguousarray(wq[:, :, 128:].reshape(QL, -1)),
        "w_kn": np.ascontiguousarray(wkv[:, :, :128].reshape(KVL, -1)),
        "w_v": np.ascontiguousarray(wkv[:, :, 128:].reshape(KVL, -1)),
        "w_mla_out": f(inp["w_mla_out"][0]), "w_out": f(inp["w_out"][0]), "w_gate": f(inp["w_ffn_gate"][0]),
        "w_up": f(inp["w_ffn_up"][0]), "w_down": f(inp["w_ffn_down"][0]), "constb": constb, "rowp": rowp,
    }
    x = np.asarray(inp["x"], np.float32)
    pos = np.asarray(inp["positions"], np.int32)
    maps = []
    for core in range(8):
        b, hf = core // 2, core % 2
        m = dict(shared)
        m["x_own"] = np.ascontiguousarray(x[b, hf * T:(hf + 1) * T])
        m["x_pre"] = np.ascontiguousarray(x[b, 0:PT]) if hf else np.zeros((PT, D), np.float32)
        p_own = pos[b, hf * T:(hf + 1) * T]
        p_pre = pos[b, 0:PT] if hf else np.zeros(PT, np.int32)
        m["pos"] = np.ascontiguousarray(np.concatenate([p_pre, p_own])[None, :].astype(np.int32))
        spc = sp.copy()
        spc[:, cols["flag"]] = float(hf)
        m["smallp"] = spc
        maps.append(m)
    return maps


_CACHE = {}


def run(cfg, inp):
    key = (cfg["D"], cfg["SEQ"])
    if key not in _CACHE:
        _CACHE[key] = Prog(cfg).build()
    nc = _CACHE[key]
    maps = host_prep(cfg, inp)
    res = run_bass_kernel_spmd(nc, maps, core_ids=list(range(8)))
    T, D = cfg["T"], cfg["D"]
    out = np.zeros((4, cfg["SEQ"], D), np.float32)
    for core in range(8):
        b, hf = core // 2, core % 2
        out[b, hf * T:(hf + 1) * T] = res.results[core]["out"]
    return out


def kernel(**inputs):
    cfg = make_cfg(4096, 4096)
    return run(cfg, inputs)
```

```python
import contextlib
import numpy as np
import ml_dtypes
import concourse.bass as bass
import concourse.mybir as mybir
from concourse.bass_utils import run_bass_kernel_spmd

F32 = mybir.dt.float32
BF16 = mybir.dt.bfloat16
I32 = mybir.dt.int32
ALU = mybir.AluOpType
AF = mybir.ActivationFunctionType
PI = float(np.pi)


def make_cfg(D, SEQ):
    c = dict(D=D, SEQ=SEQ, T=SEQ // 2, PT=SEQ // 2)
    c["TT"] = c["T"] + c["PT"]
    c["DI"] = 2 * D
    c["H"] = c["DI"] // 64
    c["G"] = 8
    c["R"] = c["H"] // 8
    c["NST"] = 128
    c["CONVC"] = c["DI"] + 2 * 8 * 128
    c["QL"] = D // 4
    c["KVL"] = 512
    c["MH"] = D // 128
    c["DFF"] = ((8 * D // 3 + 255) // 256) * 256
    c["INW"] = c["DI"] + c["CONVC"] + c["H"] + c["QL"] + 576 + 2 * D
    return c


class Buf:
    __slots__ = ("t", "w", "r", "sem", "name")

    def __init__(self, t, name):
        self.t = t
        self.w = {}
        self.r = {}
        self.sem = None
        self.name = name


class DSem:
    def __init__(self, sem):
        self.sem = sem
        self.total = 0


class FW:
    def __init__(self, nc, stack, n_dsem=84):
        self.nc = nc
        self.eng = {"pe": nc.tensor, "act": nc.scalar, "dve": nc.vector, "pool": nc.gpsimd, "sp": nc.sync}
        self.psem = {}
        self.cnt = {}
        for e in ("pe", "act", "dve", "pool"):
            self.psem[e] = stack.enter_context(nc.semaphore("p_" + e))
            self.cnt[e] = 0
        self.seen = {e: {} for e in self.eng}
        self.dsems = [DSem(stack.enter_context(nc.semaphore("d%d" % i))) for i in range(n_dsem)]
        self.free = list(range(n_dsem))
        self.phase_sems = []

    def buf(self, t, name="", dma=False):
        b = Buf(t, name)
        if dma:
            b.sem = self.free.pop(0)
            self.phase_sems.append(b.sem)
        return b

    def _wait(self, e, key, val):
        if key == ("e", "pe") and e == "pe":
            return
        if self.seen[e].get(key, 0) >= val:
            return
        if key[0] == "e":
            assert val <= self.cnt[key[1]], ("wait on a not-yet-signalled instruction", e, key, val)
        self.seen[e][key] = val
        sem = self.psem[key[1]] if key[0] == "e" else self.dsems[key[1]].sem
        self.eng[e].wait_ge(sem, val)

    def _deps(self, e, reads, writes, join):
        me = ("e", e)
        for b in reads:
            for k, v in b.w.items():
                self._wait(e, k, v)
        for b in writes:
            for k, v in b.w.items():
                if k != me:
                    self._wait(e, k, v)
            for k, v in b.r.items():
                if k != me:
                    self._wait(e, k, v)
        for b in join:
            for k, v in b.w.items():
                if k != me:
                    self._wait(e, k, v)
            for k, v in b.r.items():
                if k != me:
                    self._wait(e, k, v)

    def _upd(self, key, val, reads, writes, join):
        for b in reads:
            b.r[key] = val
        for b in writes:
            b.w = {key: val}
            b.r = {}
        for b in join:
            b.w[key] = val

    def op(self, e, fn, reads=(), writes=(), join=(), inc=True):
        self._deps(e, reads, writes, join)
        ins = fn(self.eng[e])
        if inc:
            self.cnt[e] += 1
            ins.then_inc(self.psem[e], 1)
            self._upd(("e", e), self.cnt[e], reads, writes, join)
        else:
            self._upd(("e", e), self.cnt[e] + 1, reads, writes, join)

    def dma(self, q, out, in_, semb, reads=(), writes=(), join=()):
        self._deps(q, reads, writes, join)
        d = self.dsems[semb.sem]
        d.total += 16
        self.eng[q].dma_start(out=out, in_=in_).then_inc(d.sem, 16)
        self._upd(("d", semb.sem), d.total, reads, writes, join)

    def barrier(self):
        for e in self.eng:
            for e2 in self.psem:
                if self.cnt[e2]:
                    self._wait(e, ("e", e2), self.cnt[e2])
            for i, d in enumerate(self.dsems):
                if d.total:
                    self._wait(e, ("d", i), d.total)
        self.free = self.free + self.phase_sems
        self.phase_sems = []


def bc_last(ap, n):
    s = list(ap.shape)
    return ap.unsqueeze(len(s)).broadcast_to(s + [n])


def bc_mid(ap, n):
    s = list(ap.shape)
    return ap.unsqueeze(1).broadcast_to([s[0], n] + s[1:])


class Prog:
    def __init__(self, cfg, dbg=()):
        self.c = cfg
        self.dbg = set(dbg)
        self.nc = bass.Bass("TRN2", target_bir_lowering=False)
        self.top = contextlib.ExitStack()
        self.fw = None
        self.uid = 0

    def din(self, name, shape, dt=F32):
        return self.nc.dram_tensor(name, list(shape), dt, kind="ExternalInput")

    def dscr(self, name, shape, dt=BF16):
        kind = "ExternalOutput" if name in self.dbg else "Internal"
        return self.nc.dram_tensor(name, list(shape), dt, kind=kind)

    def sb(self, st, shape, dt, name=None, dma=False):
        self.uid += 1
        nm = "%s_%d" % (name or "sb", self.uid)
        t = st.enter_context(self.nc.sbuf_tensor(nm, list(shape), dt))
        return self.fw.buf(t, nm, dma=dma)

    def ps(self, st, shape, dt, name=None):
        self.uid += 1
        nm = "%s_%d" % (name or "ps", self.uid)
        t = st.enter_context(self.nc.psum_tensor(nm, list(shape), dt))
        return self.fw.buf(t, nm)

    def declare(self):
        c = self.c
        D, T, PT, TT, DI, H, CONVC, QL, KVL, MH, DFF = (c[k] for k in
                                                          ("D", "T", "PT", "TT", "DI", "H", "CONVC", "QL", "KVL", "MH", "DFF"))
        DC, CC, QC, FC = D // 128, CONVC // 128, QL // 128, DFF // 128
        i = {}
        i["x_own"] = self.din("x_own", [T, D])
        i["x_pre"] = self.din("x_pre", [PT, D])
        i["pos"] = self.din("pos", [1, TT], I32)
        i["w_in_z"] = self.din("w_in_z", [D, DI])
        i["w_in_xbc"] = self.din("w_in_xbc", [D, CONVC])
        i["w_in_r"] = self.din("w_in_r", [D, H + QL + 2 * D])
        i["w_ckv"] = self.din("w_ckv", [D, KVL + 128])
        i["w_ssm_out"] = self.din("w_ssm_out", [DI, D])
        i["w_qn"] = self.din("w_qn", [QL, MH * 128])
        i["w_qr"] = self.din("w_qr", [QL, MH * 64])
        i["w_kn"] = self.din("w_kn", [KVL, MH * 128])
        i["w_v"] = self.din("w_v", [KVL, MH * 128])
        i["w_mla_out"] = self.din("w_mla_out", [MH * 128, D])
        i["w_out"] = self.din("w_out", [D, D])
        i["w_gate"] = self.din("w_gate", [D, DFF])
        i["w_up"] = self.din("w_up", [D, DFF])
        i["w_down"] = self.din("w_down", [DFF, D])
        self.sp_cols = small_layout(c)
        i["smallp"] = self.din("smallp", [128, self.sp_cols["_n"]])
        i["constb"] = self.din("constb", [128, 3 * 128], BF16)
        i["rowp"] = self.din("rowp", [1, 3 * H + D])
        self.i = i
        self.out = self.nc.dram_tensor("out", [T, D], F32, kind="ExternalOutput")
        s = {}

        class View:
            def __init__(self, a):
                self._a = a

            def ap(self):
                return self._a

        def arena(name, nelem):
            return self.nc.dram_tensor(name, [int(nelem)], BF16, kind="Internal")

        def carve(ar, off, shape, dt=BF16, pat=None):
            n = int(np.prod(shape)) * (2 if dt == F32 else 1)
            a = ar.ap()[off:off + n]
            if dt == F32:
                a = a.bitcast(F32)
            names = " ".join("d%d" % k for k in range(len(shape)))
            kw = {"d%d" % k: int(shape[k]) for k in range(len(shape) - 1)}
            return View(a.rearrange("(%s) -> %s" % (names, names), **kw)), off + n

        MHd = MH * 128
        n_xp = CC * 128 * TT
        n_p4a = MHd * TT * 2 + MHd * T
        n_act = T * FC * 128
        R1 = arena("R1", max(n_xp, n_p4a, n_act))
        n_p45 = (MH // 2) * 128 * T * 2 + T * MHd + D * T * 2
        R2 = arena("R2", max(n_xp, n_p45, T * D + 2 * T * D))
        R3 = arena("R3", max(TT * D, T * DI))
        R4 = arena("R4", max(T * DI, 2 * T * D))
        R5 = arena("R5", 2 * D * T)
        s["xp"], _ = carve(R1, 0, [CC, 128, TT])
        s["kn"], o = carve(R1, 0, [MH, 128, TT])
        s["v"], o = carve(R1, o, [TT, MHd])
        s["qn"], o = carve(R1, o, [MH, 128, T])
        s["act"], _ = carve(R1, 0, [T // 128, 128, FC * 128])
        s["xc"], _ = carve(R2, 0, [TT // 128, 128, CC * 128])
        s["qrr"], o = carve(R2, 0, [MH // 2, 128, T])
        s["qr"], o = carve(R2, o, [MH // 2, 128, T])
        s["attnT"], o = carve(R2, o, [T // 128, 128, MHd])
        s["yg"], o = carve(R2, o, [DC, 128, T])
        s["mg"], o = carve(R2, o, [T // 128, 128, DC * 128])
        s["nT"], o = carve(R2, 0, [T // 128, 128, DC * 128])
        s["h2"], o = carve(R2, o, [T, D], F32)
        s["uT"], _ = carve(R3, 0, [TT // 128, 128, DC * 128])
        s["ynT"], _ = carve(R3, 0, [T // 128, 128, DI])
        s["sz"], _ = carve(R4, 0, [T, DI])
        s["h"], _ = carve(R4, 0, [T, D], F32)
        s["gates"], _ = carve(R5, 0, [2 * DC, 128, T])
        s["dt"] = self.dscr("dt_d", [TT, H], F32)
        s["cqr"] = self.dscr("cqr_d", [QC, 128, T])
        s["cq"] = self.dscr("cq_d", [T // 128, 128, QC * 128])
        s["ckvr"] = self.dscr("ckvr_d", [5, 128, TT])
        s["ckv"] = self.dscr("ckv_d", [TT // 128, 128, 4 * 128])
        s["kr"] = self.dscr("kr_d", [128, TT])
        s["rope"] = self.dscr("rope_d", [2, 128, TT], F32)
        self.s = s

    def load_consts(self):
        fw, st = self.fw, self.top
        n = self.sp_cols["_n"]
        H, D = self.c["H"], self.c["D"]
        self.smallp = self.sb(st, [128, n], F32, "smallp", dma=True)
        self.constb = self.sb(st, [128, 384], BF16, "constb", dma=True)
        self.rowp = self.sb(st, [128, 3 * H], F32, "rowp", dma=True)
        fw.dma("sp", self.smallp.t[:, :], self.i["smallp"].ap(), self.smallp, writes=[self.smallp])
        fw.dma("sp", self.constb.t[:, :], self.i["constb"].ap(), self.constb, writes=[self.constb])
        fw.dma("sp", self.rowp.t[:, :], self.i["rowp"].ap()[0:1, 0:3 * H].broadcast_to([128, 3 * H]), self.rowp,
               writes=[self.rowp])
        self.derived = self.sb(st, [128, H + 1], F32, "derived")
        fw.op("act", lambda e: e.activation(out=self.derived.t[:, 0:H], in_=self.rowp.t[:, H:2 * H], func=AF.Exp),
              reads=[self.rowp], writes=[self.derived])
        fw.op("dve", lambda e: e.tensor_scalar(out=self.derived.t[:, 0:H], in0=self.derived.t[:, 0:H], scalar1=-1.0,
                                               scalar2=None, op0=ALU.mult), reads=[self.derived], join=[self.derived])
        fc = self.sp_cols["flag"]
        fw.op("dve", lambda e: e.tensor_scalar(out=self.derived.t[:, H:H + 1], in0=self.smallp.t[:, fc:fc + 1],
                                               scalar1=-1.0, scalar2=30000.0, op0=ALU.add, op1=ALU.mult),
              reads=[self.smallp, self.derived], join=[self.derived])

    def spc(self, name, j=0, n=1):
        o = self.sp_cols[name] + j
        return self.smallp.t[:, o:o + n]

    def ident(self):
        return self.constb.t[:, 0:128]

    def ones_bf(self):
        return self.constb.t[:, 128:256]

    def perm(self):
        return self.constb.t[:, 256:384]

    def norm_transpose(self, srcs, gname, dst):
        c, fw = self.c, self.fw
        D = c["D"]
        DC = D // 128
        with contextlib.ExitStack() as st:
            xb = [self.sb(st, [128, D], F32, "xb", dma=True) for _ in range(2)]
            junk = self.sb(st, [128, D], BF16, "junk")
            xs = [self.sb(st, [128, D], BF16, "xs") for _ in range(2)]
            ss = [self.sb(st, [128, 2], F32, "ss") for _ in range(2)]
            uT = [self.sb(st, [128, DC, 128], BF16, "uT", dma=True) for _ in range(2)]
            pst = [self.ps(st, [128, 4, 128], BF16, "pst") for _ in range(2)]
            npt = 0
            for j, src in enumerate(srcs):
                x, s_, xs_, u = xb[j % 2], ss[j % 2], xs[j % 2], uT[j % 2]
                fw.dma("sp", x.t[:, :], src, x, writes=[x])
                fw.op("dve", lambda e: e.memset(s_.t[:, 0:2], 0.0), writes=[s_])
                fw.op("act", lambda e: e.activation(out=junk.t[:, :], in_=x.t[:, :], func=AF.Square,
                                                    accum_out=s_.t[:, 0:1]), reads=[x, s_], writes=[junk], join=[s_])
                fw.op("act", lambda e: e.activation(out=s_.t[:, 1:2], in_=s_.t[:, 0:1], func=AF.Ln, scale=1.0 / D,
                                                    bias=self.spc("eps6")), reads=[s_, self.smallp], join=[s_])
                fw.op("act", lambda e: e.activation(out=s_.t[:, 1:2], in_=s_.t[:, 1:2], func=AF.Exp, scale=-0.5),
                      reads=[s_], join=[s_])
                fw.op("dve", lambda e: e.tensor_scalar(out=xs_.t[:, :], in0=x.t[:, :], scalar1=s_.t[:, 1:2],
                                                       scalar2=None, op0=ALU.mult), reads=[x, s_], writes=[xs_])
                first = True
                for c4 in range(0, DC, 4):
                    p = pst[npt % 2]
                    npt += 1
                    nn = min(4, DC - c4)
                    for k in range(nn):
                        cc = c4 + k
                        fw.op("pe", lambda e: e.transpose(out=p.t[:, k, :], in_=xs_.t[:, cc * 128:(cc + 1) * 128],
                                                          identity=self.ident()), reads=[xs_, self.constb],
                              **({"writes": [p]} if k == 0 else {"join": [p]}))
                    for k in range(nn):
                        cc = c4 + k
                        fw.op("act", lambda e: e.activation(out=u.t[:, cc, :], in_=p.t[:, k, :], func=AF.Copy,
                                                            scale=self.spc(gname, cc)), reads=[p, self.smallp],
                              **({"writes": [u]} if first else {"join": [u]}))
                        first = False
                fw.dma("sp", dst[j].rearrange("p (c t) -> p c t", t=128), u.t[:, :, :], u, reads=[u])
            fw.barrier()

    def gemm(self, A_d, tiles, KC, Ws, col0, N, mode, epi, big=True, extra=None):
        c, fw = self.c, self.fw
        nw = len(Ws)
        ntile = len(tiles)
        TGt = 8 if (KC <= 32 and ntile % 8 == 0 and big) else 4
        if ntile % TGt:
            TGt = ntile
        TG = TGt * 128
        NTH = max(1, TG // 512)
        THW = min(TG, 512)
        CW = 128 if nw == 2 else 256
        KS = 32
        nks = (KC + KS - 1) // KS
        with contextlib.ExitStack() as st:
            A = [self.sb(st, [128, TGt, KC, 128], BF16, "A", dma=True)]
            nsl = 3
            slabs = [self.sb(st, [128, min(KS, KC), CW], BF16, "slab", dma=True) for _ in range(3 if nw == 1 else 4)]
            pss = [self.ps(st, [128, 2048], F32, "gps") for _ in range(2)]
            ectx = extra(st) if extra else None
            if ntile // TGt > 1 and self.nc.sbuf_bytes_remaining >= TGt * KC * 256 + 6144:
                A.append(self.sb(st, [128, TGt, KC, 128], BF16, "A", dma=True))
            nsl_i = 0
            nblk = 0
            for gi in range(ntile // TGt):
                tl = tiles[gi * TGt:(gi + 1) * TGt]
                a = A[gi % len(A)]
                fw.dma("sp", a.t[:, :, :, :].rearrange("p j c t -> p j (c t)"),
                       A_d.ap()[tl[0]:tl[0] + TGt].rearrange("j p f -> p j f"), a, writes=[a])
                for c0 in range(0, N, CW):
                    cw = min(CW, N - c0)
                    psb = pss[nblk % 2]
                    nblk += 1
                    firstmm = True
                    for ks in range(nks):
                        kn = min(KS, KC - ks * KS)
                        sl = []
                        for wi in range(nw):
                            s_ = slabs[nsl_i % len(slabs)]
                            nsl_i += 1
                            wv = Ws[wi].ap()[ks * KS * 128:(ks * KS + kn) * 128, col0 + c0:col0 + c0 + cw]
                            fw.dma("pool", s_.t[:, 0:kn, 0:cw], wv.rearrange("(kc p) n -> p kc n", p=128), s_,
                                   writes=[s_])
                            sl.append(s_)
                        for kc in range(kn):
                            kk = ks * KS + kc
                            last = (ks == nks - 1 and kc == kn - 1)
                            first = (ks == 0 and kc == 0)
                            if mode == "F":
                                for wi in range(nw):
                                    for nci in range((cw + 127) // 128):
                                        mc = min(128, cw - nci * 128)
                                        for th in range(NTH):
                                            idx = ((wi if nw == 2 else nci) * NTH + th)
                                            o = psb.t[0:mc, idx * 512:idx * 512 + THW]
                                            l_ = sl[wi].t[:, kc, nci * 128:nci * 128 + mc]
                                            r_ = a.t[:, th * 4:th * 4 + THW // 128, kk, :]
                                            inc_ = (kc == kn - 1 and wi == nw - 1 and nci == (cw + 127) // 128 - 1
                                                    and th == NTH - 1)
                                            fw.op("pe", lambda e: e.matmul(o, l_, r_, start=first, stop=last),
                                                  reads=[a, sl[wi]], inc=inc_,
                                                  **({"writes": [psb]} if firstmm else {"join": [psb]}))
                                            firstmm = False
                            else:
                                for tt in range(TGt):
                                    o = psb.t[:, tt * CW:tt * CW + cw]
                                    l_ = a.t[:, tt, kk, :]
                                    r_ = sl[0].t[:, kc, 0:cw]
                                    st0 = first and ((tt * CW * 4) % 2048 == 0)
                                    fw.op("pe", lambda e: e.matmul(o, l_, r_, start=st0, stop=last, skip_group_check=True),
                                          reads=[a, sl[0]], inc=(kc == kn - 1 and tt == TGt - 1),
                                          **({"writes": [psb]} if firstmm else {"join": [psb]}))
                                    firstmm = False
                    epi(dict(tiles=tl, gi=gi, TGt=TGt, TG=TG, NTH=NTH, THW=THW, CW=CW, c0=c0, cw=cw, ps=psb, st=st,
                             ctx=ectx))
            fw.barrier()

    def ring(self, st, n, shape, dt, name, dma=True):
        bufs = [self.sb(st, shape, dt, name, dma=dma) for _ in range(n)]
        state = {"i": 0}

        def nxt():
            b = bufs[state["i"] % n]
            state["i"] += 1
            return b
        return nxt

    def in_proj(self):
        c, fw, s = self.c, self.fw, self.s
        D, T, PT, TT, DI, H, CONVC, QL = (c[k] for k in ("D", "T", "PT", "TT", "DI", "H", "CONVC", "QL"))
        DC = D // 128
        allt = list(range(TT // 128))
        own = list(range(PT // 128, TT // 128))
        npre = PT // 128
        Wz, Wx, Wr = self.i["w_in_z"], self.i["w_in_xbc"], self.i["w_in_r"]

        def ex_z(st):
            return self.ring(st, 2, [128, 8, 256], BF16, "stz")

        def epi_z(k):
            b = k["ctx"]()
            for tt in range(k["TGt"]):
                fw.op("act", lambda e: e.activation(out=b.t[:, tt, 0:k["cw"]],
                                                    in_=k["ps"].t[:, tt * k["CW"]:tt * k["CW"] + k["cw"]], func=AF.Silu),
                      reads=[k["ps"]], **({"writes": [b]} if tt == 0 else {"join": [b]}))
            r0 = (k["tiles"][0] - npre) * 128
            fw.dma("sp", s["sz"].ap()[r0:r0 + k["TG"], k["c0"]:k["c0"] + k["cw"]].rearrange("(j p) n -> p j n", p=128),
                   b.t[:, 0:k["TGt"], 0:k["cw"]], b, reads=[b])
        self.gemm(s["uT"], own, DC, [Wz], 0, DI, "T", epi_z, extra=ex_z)

        def mk_epiF(dst, tok_off, func=None, bias_name=None, ch_off=0):
            def ex(st):
                return self.ring(st, 2, [128, 4, 1024], BF16, "stF")

            def epi(k):
                b = k["ctx"]()
                nci_n = (k["cw"] + 127) // 128
                firstw = True
                for nci in range(nci_n):
                    mc = min(128, k["cw"] - nci * 128)
                    ch = (k["c0"] // 128) + nci
                    for th in range(k["NTH"]):
                        idx = nci * k["NTH"] + th
                        src = k["ps"].t[0:mc, idx * 512:idx * 512 + k["THW"]]
                        dstv = b.t[0:mc, nci, th * 512:th * 512 + k["THW"]]
                        if func is None:
                            eng = "dve" if (idx % 2) else "act"
                            if eng == "act":
                                fw.op("act", lambda e: e.activation(out=dstv, in_=src, func=AF.Copy), reads=[k["ps"]],
                                      **({"writes": [b]} if firstw else {"join": [b]}))
                            else:
                                fw.op("dve", lambda e: e.tensor_copy(out=dstv, in_=src), reads=[k["ps"]],
                                      **({"writes": [b]} if firstw else {"join": [b]}))
                        else:
                            fw.op("act", lambda e: e.activation(out=dstv, in_=src, func=func,
                                                                bias=self.spc(bias_name, ch)[0:mc, :]),
                                  reads=[k["ps"], self.smallp], **({"writes": [b]} if firstw else {"join": [b]}))
                        firstw = False
                t0 = k["tiles"][0] * 128 - tok_off
                for nci in range(nci_n):
                    mc = min(128, k["cw"] - nci * 128)
                    ch = (k["c0"] // 128) + nci + ch_off
                    fw.dma("sp", dst.ap()[ch, 0:mc, t0:t0 + k["TG"]], b.t[0:mc, nci, 0:k["TG"]], b, reads=[b])
            return ex, epi

        ex, epi = mk_epiF(s["xp"], 0)
        self.gemm(s["uT"], allt, DC, [Wx], 0, CONVC, "F", epi, extra=ex)
        ex, epi = mk_epiF(s["cqr"], PT)
        self.gemm(s["uT"], own, DC, [Wr], H, QL, "F", epi, extra=ex)
        ex, epi = mk_epiF(s["ckvr"], 0)
        self.gemm(s["uT"], allt, DC, [self.i["w_ckv"]], 0, 640, "F", epi, extra=ex)
        ex, epi = mk_epiF(s["gates"], PT, func=AF.Sigmoid, bias_name="gate_bias")
        self.gemm(s["uT"], own, DC, [Wr], H + QL, 2 * D, "F", epi, extra=ex)

        def ex_dt(st):
            return (self.ring(st, 2, [128, 8, H], F32, "stdt"), self.sb(st, [128, H], F32, "dtt"))

        def epi_dt(k):
            nxt, tmp = k["ctx"]
            b = nxt()
            for tt in range(k["TGt"]):
                src = k["ps"].t[:, tt * k["CW"]:tt * k["CW"] + H]
                fw.op("dve", lambda e: e.tensor_tensor(out=tmp.t[:, :], in0=src, in1=self.rowp.t[:, 0:H], op=ALU.add),
                      reads=[k["ps"], self.rowp], writes=[tmp])
                fw.op("act", lambda e: e.activation(out=tmp.t[:, :], in_=tmp.t[:, :], func=AF.Exp), reads=[tmp],
                      join=[tmp])
                fw.op("act", lambda e: e.activation(out=b.t[:, tt, :], in_=tmp.t[:, :], func=AF.Ln, bias=self.spc("one")),
                      reads=[tmp], **({"writes": [b]} if tt == 0 else {"join": [b]}))
                if k["tiles"][tt] < npre:
                    fw.op("dve", lambda e: e.tensor_scalar(out=b.t[:, tt, :], in0=b.t[:, tt, :],
                                                           scalar1=self.spc("flag"), scalar2=None, op0=ALU.mult),
                          reads=[b, self.smallp], join=[b])
            r0 = k["tiles"][0] * 128
            fw.dma("sp", s["dt"].ap()[r0:r0 + k["TG"], :].rearrange("(j p) n -> p j n", p=128),
                   b.t[:, 0:k["TGt"], :], b, reads=[b])
        self.gemm(s["uT"], allt, DC, [Wr], 0, H, "T", epi_dt, extra=ex_dt)

    def conv_phase(self):
        c, fw, s = self.c, self.fw, self.s
        TT, CONVC = c["TT"], c["CONVC"]
        CC = CONVC // 128
        NJ = TT // 128
        CG = 4
        with contextlib.ExitStack() as st:
            xin = [self.sb(st, [128, CG, TT + 4], BF16, "cin", dma=True) for _ in range(2)]
            acc = [self.sb(st, [128, TT], F32, "cacc") for _ in range(2)]
            xo = [self.sb(st, [128, NJ, CG, 128], BF16, "cout", dma=True) for _ in range(2)]
            for b in xin:
                fw.op("pool", lambda e: e.memset(b.t[:, :, 0:4], 0.0), writes=[b])
            for gi in range(CC // CG):
                xi, o = xin[gi % 2], xo[gi % 2]
                fw.dma("sp", xi.t[:, :, 4:4 + TT], s["xp"].ap()[gi * CG:(gi + 1) * CG].rearrange("c p t -> p c t"), xi,
                       join=[xi])
                for k in range(CG):
                    ch = gi * CG + k
                    a = acc[k % 2]
                    eng = "dve"
                    fw.op(eng, lambda e: e.tensor_scalar(out=a.t[:, :], in0=xi.t[:, k, 4:4 + TT],
                                                         scalar1=self.spc("conv_w", ch * 4 + 3),
                                                         scalar2=self.spc("conv_b", ch), op0=ALU.mult, op1=ALU.add),
                          reads=[xi, self.smallp], writes=[a])
                    for j in range(3):
                        sh = 3 - j
                        fw.op(eng, lambda e: e.scalar_tensor_tensor(out=a.t[:, :], in0=xi.t[:, k, 4 - sh:4 - sh + TT],
                                                                    scalar=self.spc("conv_w", ch * 4 + j), in1=a.t[:, :],
                                                                    op0=ALU.mult, op1=ALU.add),
                              reads=[xi, a, self.smallp], join=[a])
                    fw.op("act", lambda e: e.activation(out=o.t[:, :, k, :], in_=a.t[:, :].rearrange("p (j t) -> p j t", t=128),
                                                        func=AF.Silu), reads=[a],
                          **({"writes": [o]} if k == 0 else {"join": [o]}))
                fw.dma("sp", s["xc"].ap().rearrange("j p (c t) -> p j c t", t=128)[:, :, gi * CG:(gi + 1) * CG, :],
                       o.t[:, :, :, :], o, reads=[o])
            fw.barrier()

    def rope_tables(self):
        c, fw, s = self.c, self.fw, self.s
        TT = c["TT"]
        with contextlib.ExitStack() as st:
            pi_ = self.sb(st, [128, TT], I32, "posi", dma=True)
            ang = self.sb(st, [128, TT], F32, "ang")
            r = self.sb(st, [128, TT], F32, "rr")
            yv = self.sb(st, [128, TT], F32, "yv")
            tb = [self.sb(st, [128, TT], F32, "ropet", dma=True) for _ in range(2)]
            fw.dma("sp", pi_.t[:, :], self.i["pos"].ap()[0:1, :].broadcast_to([128, TT]), pi_, writes=[pi_])
            fw.op("dve", lambda e: e.tensor_copy(out=ang.t[:, :], in_=pi_.t[:, :]), reads=[pi_], writes=[ang])
            fw.op("dve", lambda e: e.tensor_scalar(out=ang.t[:, :], in0=ang.t[:, :], scalar1=self.spc("invf"),
                                                   scalar2=None, op0=ALU.mult), reads=[ang, self.smallp], join=[ang])
            for which, sh in ((0, 1.5 * PI), (1, PI)):
                fw.op("dve", lambda e: e.tensor_scalar(out=yv.t[:, :], in0=ang.t[:, :], scalar1=sh, scalar2=None,
                                                       op0=ALU.add), reads=[ang], writes=[yv])
                fw.op("dve", lambda e: e.tensor_scalar(out=r.t[:, :], in0=yv.t[:, :], scalar1=1.0 / (2 * PI), scalar2=None,
                                                       op0=ALU.mult), reads=[yv], writes=[r])
                fw.op("dve", lambda e: e.tensor_copy(out=pi_.t[:, :], in_=r.t[:, :]), reads=[r], writes=[pi_])
                fw.op("dve", lambda e: e.tensor_copy(out=r.t[:, :], in_=pi_.t[:, :]), reads=[pi_], writes=[r])
                fw.op("dve", lambda e: e.scalar_tensor_tensor(out=yv.t[:, :], in0=r.t[:, :], scalar=-2 * PI, in1=yv.t[:, :],
                                                              op0=ALU.mult, op1=ALU.add), reads=[r, yv], join=[yv])
                fw.op("dve", lambda e: e.tensor_scalar(out=r.t[:, :], in0=yv.t[:, :], scalar1=0.0, scalar2=2 * PI,
                                                       op0=ALU.is_lt, op1=ALU.mult), reads=[yv], writes=[r])
                fw.op("dve", lambda e: e.scalar_tensor_tensor(out=r.t[:, :], in0=yv.t[:, :], scalar=-PI, in1=r.t[:, :],
                                                              op0=ALU.add, op1=ALU.add), reads=[yv, r], join=[r])
                fw.op("dve", lambda e: e.tensor_scalar(out=r.t[:, :], in0=r.t[:, :], scalar1=-3.1415925, scalar2=3.1415925,
                                                       op0=ALU.max, op1=ALU.min), reads=[r], join=[r])
                if which == 0:
                    fw.op("act", lambda e: e.activation(out=tb[0].t[:, :], in_=r.t[:, :], func=AF.Sin), reads=[r],
                          writes=[tb[0]])
                else:
                    fw.op("act", lambda e: e.activation(out=tb[1].t[:, :], in_=r.t[:, :], func=AF.Sin,
                                                        scale=self.spc("sgn")), reads=[r, self.smallp], writes=[tb[1]])
                fw.dma("sp", s["rope"].ap()[which], tb[which].t[:, :], tb[which], reads=[tb[which]])
            fw.barrier()

    def apply_rope(self, st, src_sb, n, cos_sb, sin_sb, t0, out_sb, pr, tmp):
        fw = self.fw
        for o in range(0, n, 512):
            w = min(512, n - o)
            fw.op("pe", lambda e: e.matmul(pr.t[:, 0:w], self.perm(), src_sb.t[:, o:o + w], start=True, stop=True),
                  reads=[src_sb, self.constb], writes=[pr])
            fw.op("dve", lambda e: e.tensor_tensor(out=tmp.t[:, 0:w], in0=pr.t[:, 0:w], in1=sin_sb.t[:, t0 + o:t0 + o + w],
                                                   op=ALU.mult), reads=[pr, sin_sb], writes=[tmp])
            fw.op("pool", lambda e: e.tensor_tensor(out=out_sb.t[:, o:o + w], in0=src_sb.t[:, o:o + w],
                                                    in1=cos_sb.t[:, t0 + o:t0 + o + w], op=ALU.mult),
                  reads=[src_sb, cos_sb], **({"writes": [out_sb]} if o == 0 else {"join": [out_sb]}))
            fw.op("pool", lambda e: e.tensor_tensor(out=out_sb.t[:, o:o + w], in0=out_sb.t[:, o:o + w], in1=tmp.t[:, 0:w],
                                                    op=ALU.add), reads=[out_sb, tmp], join=[out_sb])

    def latent_norm(self, raw, nch, ntok, gname, dst, feat, rope_extra):
        c, fw, s = self.c, self.fw, self.s
        with contextlib.ExitStack() as st:
            xin = [self.sb(st, [128, nch, 512], BF16, "lin", dma=True) for _ in range(2)]
            sq = self.sb(st, [128, nch, 512], BF16, "lsq")
            rs = self.sb(st, [128, 512], F32, "lrs")
            xo = [self.sb(st, [128, 4, nch, 128], BF16, "lout", dma=True) for _ in range(2)]
            pss = self.ps(st, [128, 512], F32, "lps")
            if rope_extra:
                cos_sb = self.sb(st, [128, ntok], F32, "cos", dma=True)
                sin_sb = self.sb(st, [128, ntok], F32, "sin", dma=True)
                fw.dma("sp", cos_sb.t[:, :], s["rope"].ap()[0], cos_sb, writes=[cos_sb])
                fw.dma("sp", sin_sb.t[:, :], s["rope"].ap()[1], sin_sb, writes=[sin_sb])
                krin = [self.sb(st, [128, 512], BF16, "krin", dma=True) for _ in range(2)]
                krout = [self.sb(st, [128, 512], BF16, "krout", dma=True) for _ in range(2)]
                pr = self.ps(st, [128, 512], F32, "prp")
                rtmp = self.sb(st, [128, 512], F32, "rtmp")
            for gi in range(ntok // 512):
                xi, o = xin[gi % 2], xo[gi % 2]
                fw.dma("sp", xi.t[:, :, :], raw.ap()[0:nch, :, gi * 512:(gi + 1) * 512].rearrange("c p t -> p c t"), xi,
                       writes=[xi])
                fw.op("dve", lambda e: e.tensor_tensor(out=sq.t[:, :, :], in0=xi.t[:, :, :], in1=xi.t[:, :, :], op=ALU.mult),
                      reads=[xi], writes=[sq])
                for k in range(nch):
                    fw.op("pe", lambda e: e.matmul(pss.t[:, :], self.ones_bf(), sq.t[:, k, :], start=(k == 0),
                                                   stop=(k == nch - 1)), reads=[sq, self.constb],
                          **({"writes": [pss]} if k == 0 else {"join": [pss]}))
                fw.op("act", lambda e: e.activation(out=rs.t[:, :], in_=pss.t[:, :], func=AF.Ln, scale=1.0 / feat,
                                                    bias=self.spc("eps6")), reads=[pss, self.smallp], writes=[rs])
                fw.op("act", lambda e: e.activation(out=rs.t[:, :], in_=rs.t[:, :], func=AF.Exp, scale=-0.5), reads=[rs],
                      join=[rs])
                for k in range(nch):
                    eng = "dve"
                    fw.op(eng, lambda e: e.scalar_tensor_tensor(out=o.t[:, :, k, :],
                                                                in0=xi.t[:, k, :].rearrange("p (j t) -> p j t", t=128),
                                                                scalar=self.spc(gname, k),
                                                                in1=rs.t[:, :].rearrange("p (j t) -> p j t", t=128),
                                                                op0=ALU.mult, op1=ALU.mult),
                          reads=[xi, rs, self.smallp], **({"writes": [o]} if k == 0 else {"join": [o]}))
                fw.dma("sp", dst.ap()[gi * 4:(gi + 1) * 4].rearrange("j p (c t) -> p j c t", t=128), o.t[:, :, :, :], o,
                       reads=[o])
                if rope_extra:
                    ki, ko = krin[gi % 2], krout[gi % 2]
                    fw.dma("sp", ki.t[:, :], raw.ap()[4, :, gi * 512:(gi + 1) * 512], ki, writes=[ki])
                    self.apply_rope(st, ki, 512, cos_sb, sin_sb, gi * 512, ko, pr, rtmp)
                    fw.dma("sp", s["kr"].ap()[:, gi * 512:(gi + 1) * 512], ko.t[:, :], ko, reads=[ko])
            fw.barrier()

    def ssd(self):
        c, fw, s = self.c, self.fw, self.s
        T, PT, TT, DI, H, R, CONVC = (c[k] for k in ("T", "PT", "TT", "DI", "H", "R", "CONVC"))
        CC = CONVC // 128
        XC = DI // 128
        GW = R * 64
        GCH = GW // 128
        PW = min(512, GW)
        NPC = GW // PW
        HS = min(4, R)
        npre = PT // 128
        tri = lambda: self.spc("tri", 0, 128)
        ntri = lambda: self.spc("ntri", 0, 128)
        onesf = lambda: self.spc("onesf", 0, 128)
        Abc = lambda: self.derived.t[:, 0:H]
        dskip = lambda: self.rowp.t[:, 2 * H:3 * H]
        with contextlib.ExitStack() as st:
            sb, ps = (lambda sh, dt, nm, dma=False: self.sb(st, sh, dt, nm, dma=dma)), (lambda sh, dt, nm: self.ps(st, sh, dt, nm))
            xc = [sb([128, CC, 128], BF16, "xc", True) for _ in range(2)]
            dtb = [sb([128, H], F32, "dt", True) for _ in range(2)]
            szb = [sb([128, GW], BF16, "sz", True) for _ in range(2)]
            ynT = [sb([128, XC, 128], BF16, "ynT", True) for _ in range(2)]
            state = sb([128, DI], F32, "state")
            stbf = sb([128, DI], BF16, "stbf")
            a_sb = sb([128, H], F32, "a")
            acs_sb = sb([128, H], F32, "acs")
            eacs = sb([128, H], F32, "eacs")
            eL = sb([128, H], F32, "eL")
            wd = sb([128, H], F32, "wd")
            dtw = sb([128, H], F32, "dtw")
            xtok_r = [sb([128, GW], BF16, "xtok") for _ in range(2)]
            xdt_r = [sb([128, GW], BF16, "xdt") for _ in range(2)]
            xdtw_r = [sb([128, GW], BF16, "xdtw") for _ in range(2)]
            btok_r = [sb([128, 128], BF16, "btok") for _ in range(2)]
            cbm_r = [sb([128, 128], BF16, "cbm") for _ in range(2)]
            Yg_r = [sb([128, HS, 128], F32, "Yg") for _ in range(2)]
            Eg_r = [sb([128, HS, 128], BF16, "Eg") for _ in range(2)]
            Mg_r = [sb([128, HS, 128], BF16, "Mg") for _ in range(2)]
            t1_r = [sb([128, GW], F32, "t1") for _ in range(2)]
            t2_r = [sb([128, GW], F32, "t2") for _ in range(2)]
            yn_r = [sb([128, GW], BF16, "yn") for _ in range(2)]
            junk_r = [sb([128, GW], BF16, "junk") for _ in range(2)]
            ssq_r = [sb([128, 2], F32, "ssq") for _ in range(2)]
            psX = ps([128, GW], BF16, "psX")
            psBC = ps([128, 512], F32, "psBC")
            psX2 = ps([128, GW], BF16, "psX2")
            nsg = 0
            psBt = ps([128, 128], BF16, "psBt")
            psY2 = ps([128, PW], F32, "psY2")
            psS = ps([128, PW], F32, "psS")
            psD = ps([128, HS * 128], F32, "psD")
            psY1 = ps([128, PW], F32, "psY1")
            fw.op("dve", lambda e: e.memset(state.t[:, :], 0.0), writes=[state])
            fw.op("pool", lambda e: e.memset(stbf.t[:, :], 0.0), writes=[stbf])
            for j in range(TT // 128):
                own = j >= npre
                jo = j - npre
                x, d_, yo = xc[j % 2], dtb[j % 2], ynT[j % 2]
                fw.dma("sp", x.t[:, :, :], s["xc"].ap()[j].rearrange("p (c t) -> p c t", t=128), x, writes=[x])
                fw.dma("sp", d_.t[:, :], s["dt"].ap()[j * 128:(j + 1) * 128, :], d_, writes=[d_])
                fw.op("dve", lambda e: e.tensor_tensor(out=a_sb.t[:, :], in0=d_.t[:, :], in1=Abc(), op=ALU.mult),
                      reads=[d_, self.derived], writes=[a_sb])
                fw.op("pe", lambda e: e.matmul(psBC.t[:, 128:128 + H], tri(), a_sb.t[:, :], start=True, stop=True),
                      reads=[a_sb, self.smallp], writes=[psBC])
                fw.op("pe", lambda e: e.matmul(psBC.t[:, 128 + H:128 + 2 * H], onesf(), a_sb.t[:, :], start=True, stop=True),
                      reads=[a_sb, self.smallp], join=[psBC])
                fw.op("act", lambda e: e.activation(out=acs_sb.t[:, :], in_=psBC.t[:, 128:128 + H], func=AF.Copy), reads=[psBC],
                      writes=[acs_sb])
                fw.op("act", lambda e: e.activation(out=eL.t[:, :], in_=psBC.t[:, 128 + H:128 + 2 * H], func=AF.Exp), reads=[psBC],
                      writes=[eL])
                fw.op("dve", lambda e: e.tensor_tensor(out=wd.t[:, :], in0=psBC.t[:, 128 + H:128 + 2 * H], in1=acs_sb.t[:, :],
                                                       op=ALU.subtract), reads=[psBC, acs_sb], writes=[wd])
                fw.op("act", lambda e: e.activation(out=wd.t[:, :], in_=wd.t[:, :], func=AF.Exp), reads=[wd], join=[wd])
                fw.op("dve", lambda e: e.tensor_tensor(out=dtw.t[:, :], in0=wd.t[:, :], in1=d_.t[:, :], op=ALU.mult),
                      reads=[wd, d_], writes=[dtw])
                if own:
                    fw.op("act", lambda e: e.activation(out=eacs.t[:, :], in_=acs_sb.t[:, :], func=AF.Exp),
                          reads=[acs_sb], writes=[eacs])
                for g in range(8):
                    h0 = g * R
                    gc0 = g * GW
                    bch = XC + g
                    cch = XC + 8 + g
                    rg = (j * 8 + g) % 2
                    xtok, xdt, xdtw, btok, cbm = xtok_r[rg], xdt_r[rg], xdtw_r[rg], btok_r[rg], cbm_r[rg]
                    t1, t2, yn, junk, ssq = t1_r[rg], t2_r[rg], yn_r[rg], junk_r[rg], ssq_r[rg]
                    for i in range(GCH):
                        fw.op("pe", lambda e: e.transpose(out=psX.t[:, i * 128:(i + 1) * 128], in_=x.t[:, g * GCH + i, :],
                                                          identity=self.ident()), reads=[x, self.constb],
                              **({"writes": [psX]} if i == 0 else {"join": [psX]}))
                    fw.op("pe", lambda e: e.transpose(out=psBt.t[:, :], in_=x.t[:, bch, :], identity=self.ident()),
                          reads=[x, self.constb], writes=[psBt])
                    fw.op("act", lambda e: e.activation(out=xtok.t[:, :], in_=psX.t[:, :], func=AF.Copy), reads=[psX],
                          writes=[xtok])
                    fw.op("act", lambda e: e.activation(out=btok.t[:, :], in_=psBt.t[:, :], func=AF.Copy), reads=[psBt],
                          writes=[btok])
                    fw.op("pool", lambda e: e.tensor_tensor(out=xdtw.t[:, :].rearrange("p (h d) -> p h d", d=64),
                                                            in0=xtok.t[:, :].rearrange("p (h d) -> p h d", d=64),
                                                            in1=bc_last(dtw.t[:, h0:h0 + R], 64), op=ALU.mult),
                          reads=[xtok, dtw], writes=[xdtw])
                    if own:
                        fw.op("dve", lambda e: e.tensor_tensor(out=xdt.t[:, :].rearrange("p (h d) -> p h d", d=64),
                                                               in0=xtok.t[:, :].rearrange("p (h d) -> p h d", d=64),
                                                               in1=bc_last(d_.t[:, h0:h0 + R], 64), op=ALU.mult),
                              reads=[xtok, d_], writes=[xdt])
                        fw.op("pe", lambda e: e.matmul(psBC.t[:, 0:128], x.t[:, bch, :], x.t[:, cch, :], start=True,
                                                       stop=True), reads=[x], writes=[psBC])
                        fw.op("dve", lambda e: e.tensor_tensor(out=cbm.t[:, :], in0=psBC.t[:, 0:128], in1=tri(),
                                                               op=ALU.mult), reads=[psBC, self.smallp], writes=[cbm])
                        fw.op("pool", lambda e: e.memset(ssq.t[:, :], 0.0), writes=[ssq])
                        fw.dma("sp", szb[g % 2].t[:, :], s["sz"].ap()[jo * 128:(jo + 1) * 128, gc0:gc0 + GW], szb[g % 2],
                               writes=[szb[g % 2]])
                    for pc in range(NPC):
                        c0 = pc * PW
                        hp0 = h0 + c0 // 64
                        nhp = PW // 64
                        if own:
                            fw.op("pe", lambda e: e.matmul(psY2.t[:, :], x.t[:, cch, :], stbf.t[:, gc0 + c0:gc0 + c0 + PW],
                                                           start=True, stop=True), reads=[x, stbf], writes=[psY2])
                            for sg in range(nhp // HS):
                                hh0 = hp0 + sg * HS
                                Yg, Eg, Mg = Yg_r[nsg % 2], Eg_r[nsg % 2], Mg_r[nsg % 2]
                                nsg += 1
                                fw.op("dve", lambda e: e.tensor_tensor(out=Yg.t[:, :, :], in0=bc_mid(tri(), HS),
                                                                       in1=bc_last(a_sb.t[:, hh0:hh0 + HS], 128),
                                                                       op=ALU.mult), reads=[a_sb, self.smallp],
                                      writes=[Yg])
                                fw.op("pe", lambda e: e.matmul(psD.t[:, :], ntri(),
                                                               Yg.t[:, :, :].rearrange("p h l -> p (h l)"), start=True,
                                                               stop=True), reads=[Yg, self.smallp], writes=[psD])
                                fw.op("act", lambda e: e.activation(out=Eg.t[:, :, :].rearrange("p h l -> p (h l)"),
                                                                    in_=psD.t[:, :], func=AF.Exp), reads=[psD],
                                      writes=[Eg])
                                fw.op("pool", lambda e: e.tensor_tensor(out=Mg.t[:, :, :], in0=Eg.t[:, :, :],
                                                                        in1=bc_mid(cbm.t[:, :], HS), op=ALU.mult),
                                      reads=[Eg, cbm], writes=[Mg])
                                for hi in range(HS):
                                    lc = (sg * HS + hi) * 64
                                    fw.op("pe", lambda e: e.matmul(psY1.t[:, lc:lc + 64], Mg.t[:, hi, :],
                                                                   xdt.t[:, c0 + lc:c0 + lc + 64], start=True, stop=True),
                                          reads=[Mg, xdt], **({"writes": [psY1]} if (sg == 0 and hi == 0) else {"join": [psY1]}))
                            v3 = lambda ap: ap.rearrange("p (h d) -> p h d", d=64)
                            fw.op("dve", lambda e: e.tensor_tensor(out=v3(t1.t[:, c0:c0 + PW]), in0=v3(psY2.t[:, :]),
                                                                   in1=bc_last(eacs.t[:, hp0:hp0 + nhp], 64), op=ALU.mult),
                                  reads=[psY2, eacs], **({"writes": [t1]} if pc == 0 else {"join": [t1]}))
                            fw.op("dve", lambda e: e.tensor_tensor(out=t1.t[:, c0:c0 + PW], in0=t1.t[:, c0:c0 + PW],
                                                                   in1=psY1.t[:, :], op=ALU.add), reads=[t1, psY1],
                                  join=[t1])
                        fw.op("pe", lambda e: e.matmul(psS.t[:, :], btok.t[:, :], xdtw.t[:, c0:c0 + PW], start=True,
                                                       stop=True), reads=[btok, xdtw], writes=[psS])
                        sv = state.t[:, gc0 + c0:gc0 + c0 + PW]
                        fw.op("pool", lambda e: e.tensor_tensor(out=sv.rearrange("p (h d) -> p h d", d=64),
                                                                in0=sv.rearrange("p (h d) -> p h d", d=64),
                                                                in1=bc_last(eL.t[:, hp0:hp0 + nhp], 64), op=ALU.mult),
                              reads=[state, eL, stbf], join=[state])
                        fw.op("dve", lambda e: e.tensor_tensor(out=sv, in0=sv, in1=psS.t[:, :], op=ALU.add),
                              reads=[state, psS], join=[state])
                        fw.op("act", lambda e: e.activation(out=stbf.t[:, gc0 + c0:gc0 + c0 + PW], in_=sv, func=AF.Copy),
                              reads=[state], join=[stbf])
                    if own:
                        z = szb[g % 2]
                        fw.op("pool", lambda e: e.tensor_tensor(out=t2.t[:, :].rearrange("p (h d) -> p h d", d=64),
                                                                in0=xtok.t[:, :].rearrange("p (h d) -> p h d", d=64),
                                                                in1=bc_last(dskip()[:, h0:h0 + R], 64), op=ALU.mult),
                              reads=[xtok, self.rowp], writes=[t2])
                        fw.op("dve", lambda e: e.tensor_tensor(out=t1.t[:, :], in0=t1.t[:, :], in1=t2.t[:, :], op=ALU.add),
                              reads=[t1, t2], join=[t1])
                        fw.op("dve", lambda e: e.tensor_tensor(out=t1.t[:, :], in0=t1.t[:, :], in1=z.t[:, :], op=ALU.mult),
                              reads=[t1, z], join=[t1])
                        fw.op("act", lambda e: e.activation(out=junk.t[:, :], in_=t1.t[:, :], func=AF.Square,
                                                            accum_out=ssq.t[:, 0:1]), reads=[t1, ssq], writes=[junk],
                              join=[ssq])
                        fw.op("act", lambda e: e.activation(out=ssq.t[:, 1:2], in_=ssq.t[:, 0:1], func=AF.Ln, scale=1.0 / GW,
                                                            bias=self.spc("eps5")), reads=[ssq, self.smallp], join=[ssq])
                        fw.op("act", lambda e: e.activation(out=ssq.t[:, 1:2], in_=ssq.t[:, 1:2], func=AF.Exp, scale=-0.5),
                              reads=[ssq], join=[ssq])
                        fw.op("dve", lambda e: e.tensor_scalar(out=yn.t[:, :], in0=t1.t[:, :], scalar1=ssq.t[:, 1:2],
                                                               scalar2=None, op0=ALU.mult), reads=[t1, ssq], writes=[yn])
                        for i in range(GCH):
                            ch = g * GCH + i
                            fw.op("pe", lambda e: e.transpose(out=psX2.t[:, i * 128:(i + 1) * 128],
                                                              in_=yn.t[:, i * 128:(i + 1) * 128], identity=self.ident()),
                                  reads=[yn, self.constb], **({"writes": [psX2]} if i == 0 else {"join": [psX2]}))
                        for i in range(GCH):
                            ch = g * GCH + i
                            fw.op("act", lambda e: e.activation(out=yo.t[:, ch, :], in_=psX2.t[:, i * 128:(i + 1) * 128],
                                                                func=AF.Copy, scale=self.spc("ssm_g", ch)),
                                  reads=[psX2, self.smallp], **({"writes": [yo]} if (g == 0 and i == 0) else {"join": [yo]}))
                if own:
                    fw.dma("sp", s["ynT"].ap()[jo].rearrange("p (c t) -> p c t", t=128), yo.t[:, :, :], yo, reads=[yo])
            fw.barrier()

    def mla(self):
        c, fw, s = self.c, self.fw, self.s
        T, PT, TT, MH, QL, D = (c[k] for k in ("T", "PT", "TT", "MH", "QL", "D"))
        QC = QL // 128
        npre = PT // 128
        allt = list(range(TT // 128))
        ownq = list(range(T // 128))

        def mkF(dst, tok_off):
            def ex(st):
                return self.ring(st, 2, [128, 4, 1024], BF16, "stF")

            def epi(k):
                b = k["ctx"]()
                nci_n = k["cw"] // 128
                fz = True
                for nci in range(nci_n):
                    for th in range(k["NTH"]):
                        idx = nci * k["NTH"] + th
                        src = k["ps"].t[:, idx * 512:idx * 512 + k["THW"]]
                        dv = b.t[:, nci, th * 512:th * 512 + k["THW"]]
                        if idx % 2:
                            fw.op("dve", lambda e: e.tensor_copy(out=dv, in_=src), reads=[k["ps"]],
                                  **({"writes": [b]} if fz else {"join": [b]}))
                        else:
                            fw.op("act", lambda e: e.activation(out=dv, in_=src, func=AF.Copy), reads=[k["ps"]],
                                  **({"writes": [b]} if fz else {"join": [b]}))
                        fz = False
                t0 = k["tiles"][0] * 128 - tok_off
                for nci in range(nci_n):
                    fw.dma("sp", dst.ap()[k["c0"] // 128 + nci, :, t0:t0 + k["TG"]], b.t[:, nci, 0:k["TG"]], b, reads=[b])
            return ex, epi

        ex, epi = mkF(s["kn"], 0)
        self.gemm(s["ckv"], allt, 4, [self.i["w_kn"]], 0, MH * 128, "F", epi, extra=ex)

        def ex_v(st):
            return self.ring(st, 2, [128, 8, 256], BF16, "stv")

        def epi_v(k):
            b = k["ctx"]()
            for tt in range(k["TGt"]):
                src = k["ps"].t[:, tt * k["CW"]:tt * k["CW"] + k["cw"]]
                if tt % 2:
                    fw.op("dve", lambda e: e.tensor_copy(out=b.t[:, tt, 0:k["cw"]], in_=src), reads=[k["ps"]],
                          **({"writes": [b]} if tt == 0 else {"join": [b]}))
                else:
                    fw.op("act", lambda e: e.activation(out=b.t[:, tt, 0:k["cw"]], in_=src, func=AF.Copy),
                          reads=[k["ps"]], **({"writes": [b]} if tt == 0 else {"join": [b]}))
            r0 = k["tiles"][0] * 128
            fw.dma("sp", s["v"].ap()[r0:r0 + k["TG"], k["c0"]:k["c0"] + k["cw"]].rearrange("(j p) n -> p j n", p=128),
                   b.t[:, 0:k["TGt"], 0:k["cw"]], b, reads=[b])
        self.gemm(s["ckv"], allt, 4, [self.i["w_v"]], 0, MH * 128, "T", epi_v, extra=ex_v)
        ex, epi = mkF(s["qn"], 0)
        self.gemm(s["cq"], ownq, QC, [self.i["w_qn"]], 0, MH * 128, "F", epi, extra=ex)
        ex, epi = mkF(s["qrr"], 0)
        self.gemm(s["cq"], ownq, QC, [self.i["w_qr"]], 0, MH * 64, "F", epi, extra=ex)
        with contextlib.ExitStack() as st:
            cos_sb = self.sb(st, [128, T], F32, "cos", dma=True)
            sin_sb = self.sb(st, [128, T], F32, "sin", dma=True)
            fw.dma("sp", cos_sb.t[:, :], s["rope"].ap()[0, :, PT:TT], cos_sb, writes=[cos_sb])
            fw.dma("sp", sin_sb.t[:, :], s["rope"].ap()[1, :, PT:TT], sin_sb, writes=[sin_sb])
            qi = [self.sb(st, [128, T], BF16, "qri", dma=True) for _ in range(2)]
            qo = [self.sb(st, [128, T], BF16, "qro", dma=True) for _ in range(2)]
            pr = self.ps(st, [128, 512], F32, "prq")
            rtmp = self.sb(st, [128, 512], F32, "rtmpq")
            for hp in range(MH // 2):
                a, o = qi[hp % 2], qo[hp % 2]
                fw.dma("sp", a.t[:, :], s["qrr"].ap()[hp], a, writes=[a])
                self.apply_rope(st, a, T, cos_sb, sin_sb, 0, o, pr, rtmp)
                fw.dma("sp", s["qr"].ap()[hp], o.t[:, :], o, reads=[o])
            fw.barrier()
        scale = float((128 + 64) ** -0.5)
        NK = TT // 128
        NQS = T // 512
        with contextlib.ExitStack() as st:
            kr = self.sb(st, [128, TT], BF16, "kr", dma=True)
            fw.dma("sp", kr.t[:, :], s["kr"].ap(), kr, writes=[kr])
            kn = [self.sb(st, [128, TT], BF16, "kn", dma=True) for _ in range(2)]
            qn = [self.sb(st, [128, T], BF16, "qn", dma=True) for _ in range(2)]
            qr = [self.sb(st, [128, T], BF16, "qr", dma=True) for _ in range(2)]
            vv = [self.sb(st, [128, NK, 129], BF16, "vv", dma=True) for _ in range(2)]
            for v0 in vv:
                fw.op("pool", lambda e: e.memset(v0.t[:, :, 128:129], 1.0), writes=[v0])
            pt = [self.sb(st, [128, 512], BF16, "pt") for _ in range(4)]
            psS = [self.ps(st, [128, 512], F32, "psS") for _ in range(3)]
            acc = [self.ps(st, [128, 4, 256], F32, "acc") for _ in range(2)]
            psT = self.ps(st, [128, 4, 128], BF16, "psT")
            rden = self.sb(st, [128, 4], F32, "rden")
            abf = self.sb(st, [128, 4, 128], BF16, "abf")
            aT = [self.sb(st, [128, 4, 128], BF16, "aT", dma=True) for _ in range(2)]
            npt = 0
            nsb = 0
            nacc = 0
            for h in range(MH):
                k_, q_, v_ = kn[h % 2], qn[h % 2], vv[h % 2]
                half = (h % 2) * 64
                fw.dma("sp", k_.t[:, :], s["kn"].ap()[h], k_, writes=[k_])
                fw.dma("sp", q_.t[:, :], s["qn"].ap()[h], q_, writes=[q_])
                fw.dma("sp", v_.t[:, :, 0:128], s["v"].ap()[:, h * 128:(h + 1) * 128].rearrange("(j p) d -> p j d", p=128),
                       v_, join=[v_])
                if h % 2 == 0:
                    qrb = qr[(h // 2) % 2]
                    fw.dma("sp", qrb.t[:, :], s["qr"].ap()[h // 2], qrb, writes=[qrb])
                for qs in range(NQS):
                    ac = acc[nacc % 2]
                    nacc += 1
                    nkt = npre + 4 * qs + 4
                    for kt in range(nkt):
                        o = kt - npre
                        qi0 = 0 if (kt < npre or o < 4 * qs) else (o - 4 * qs)
                        q0 = qi0 * 128
                        diag = (kt >= npre and o >= 4 * qs)
                        pS = psS[nsb % 3]
                        nsb += 1
                        p_ = pt[npt % 4]
                        npt += 1
                        qa, qb = qs * 512 + q0, (qs + 1) * 512
                        fw.op("pe", lambda e: e.matmul(pS.t[:, q0:512], k_.t[:, kt * 128:(kt + 1) * 128], q_.t[:, qa:qb],
                                                       start=True, stop=False), reads=[k_, q_], writes=[pS], inc=False)
                        fw.op("pe", lambda e: e.matmul(pS.t[:, q0:512], kr.t[half:half + 64, kt * 128:(kt + 1) * 128],
                                                       qrb.t[half:half + 64, qa:qb], start=False, stop=True),
                              reads=[kr, qrb], join=[pS])
                        if not diag:
                            bias = self.derived.t[:, c["H"]:c["H"] + 1] if kt < npre else 0.0
                            fw.op("act", lambda e: e.activation(out=p_.t[:, 0:512], in_=pS.t[:, 0:512], func=AF.Exp,
                                                                bias=bias, scale=scale), reads=[pS, self.derived],
                                  writes=[p_])
                        else:
                            fw.op("act", lambda e: e.activation(out=p_.t[0:64, q0:512], in_=pS.t[0:64, q0:512],
                                                                func=AF.Exp, scale=scale), reads=[pS], writes=[p_])
                            if q0 + 64 < 512 or True:
                                fw.op("act", lambda e: e.activation(out=p_.t[64:128, q0 + 64:512],
                                                                    in_=pS.t[64:128, q0 + 64:512], func=AF.Exp,
                                                                    scale=scale), reads=[pS], join=[p_])
                            fw.op("pool", lambda e: e.memset(p_.t[64:128, q0:q0 + 64], 0.0), join=[p_])
                        for qi in range(qi0, 4):
                            own_tile = 4 * qs + qi
                            st_ = (kt == 0)
                            sp_ = (kt == npre + own_tile)
                            fw.op("pe", lambda e: e.matmul(ac.t[:, qi, 0:129], p_.t[:, qi * 128:(qi + 1) * 128],
                                                           v_.t[:, kt, :], start=(st_ and qi % 2 == 0), stop=sp_,
                                                           skip_group_check=True), reads=[p_, v_], inc=(qi == 3),
                                  **({"writes": [ac]} if (kt == 0 and qi == qi0) else {"join": [ac]}))
                    fw.op("dve", lambda e: e.reciprocal(out=rden.t[:, :], in_=ac.t[:, :, 128]), reads=[ac], writes=[rden])
                    for qi in range(4):
                        fw.op("dve", lambda e: e.tensor_scalar(out=abf.t[:, qi, :], in0=ac.t[:, qi, 0:128],
                                                               scalar1=rden.t[:, qi:qi + 1], scalar2=None, op0=ALU.mult),
                              reads=[ac, rden], **({"writes": [abf]} if qi == 0 else {"join": [abf]}))
                    for qi in range(4):
                        fw.op("pe", lambda e: e.transpose(out=psT.t[:, qi, :], in_=abf.t[:, qi, :], identity=self.ident()),
                              reads=[abf, self.constb], **({"writes": [psT]} if qi == 0 else {"join": [psT]}))
                    at = aT[(h * NQS + qs) % 2]
                    fw.op("act", lambda e: e.activation(out=at.t[:, :, :], in_=psT.t[:, :, :], func=AF.Copy), reads=[psT],
                          writes=[at])
                    fw.dma("sp", s["attnT"].ap()[qs * 4:(qs + 1) * 4, :, h * 128:(h + 1) * 128].rearrange("j p t -> p j t"),
                           at.t[:, :, :], at, reads=[at])
            fw.barrier()

    def merge_out(self):
        c, fw, s = self.c, self.fw, self.s
        T, D, DI, MH = c["T"], c["D"], c["DI"], c["MH"]
        DC = D // 128
        own = list(range(T // 128))

        def mk(first):
            def ex(st):
                return (self.ring(st, 2, [128, 4, 1024], BF16, "gt"), self.ring(st, 2, [128, 4, 1024], BF16, "yg"),
                        self.ring(st, 2, [128, 8, 4, 128], BF16, "mo"))

            def epi(k):
                gring, yring, oring = k["ctx"]
                gt = gring()
                nci_n = k["cw"] // 128
                t0 = k["tiles"][0] * 128
                TG = k["TG"]
                for nci in range(nci_n):
                    ch = k["c0"] // 128 + nci + (0 if first else DC)
                    fw.dma("sp", gt.t[:, nci, 0:TG], s["gates"].ap()[ch, :, t0:t0 + TG], gt,
                           **({"writes": [gt]} if nci == 0 else {"join": [gt]}))
                if first:
                    o = yring()
                else:
                    yg = yring()
                    for nci in range(nci_n):
                        ch = k["c0"] // 128 + nci
                        fw.dma("sp", yg.t[:, nci, 0:TG], s["yg"].ap()[ch, :, t0:t0 + TG], yg,
                               **({"writes": [yg]} if nci == 0 else {"join": [yg]}))
                    o = oring()
                fz = True
                for nci in range(nci_n):
                    for th in range(k["NTH"]):
                        idx = nci * k["NTH"] + th
                        w = k["THW"]
                        src = k["ps"].t[:, idx * 512:idx * 512 + w]
                        gv = gt.t[:, nci, th * 512:th * 512 + w]
                        if first:
                            fw.op("dve", lambda e: e.tensor_tensor(out=o.t[:, nci, th * 512:th * 512 + w], in0=src, in1=gv,
                                                                   op=ALU.mult), reads=[k["ps"], gt],
                                  **({"writes": [o]} if fz else {"join": [o]}))
                        else:
                            yv = yg.t[:, nci, th * 512:th * 512 + w]
                            fw.op("dve", lambda e: e.tensor_tensor(out=gv, in0=src, in1=gv, op=ALU.mult),
                                  reads=[k["ps"], gt], join=[gt])
                            fw.op("pool", lambda e: e.tensor_tensor(
                                out=o.t[:, th * 4:th * 4 + w // 128, nci, :],
                                in0=gv.rearrange("p (j t) -> p j t", t=128), in1=yv.rearrange("p (j t) -> p j t", t=128),
                                op=ALU.add), reads=[gt, yg], **({"writes": [o]} if fz else {"join": [o]}))
                        fz = False
                if first:
                    for nci in range(nci_n):
                        ch = k["c0"] // 128 + nci
                        fw.dma("sp", s["yg"].ap()[ch, :, t0:t0 + TG], o.t[:, nci, 0:TG], o, reads=[o])
                else:
                    cb = k["c0"] // 128
                    fw.dma("sp", s["mg"].ap()[k["tiles"][0]:k["tiles"][0] + k["TGt"]].rearrange(
                        "j p (c t) -> p j c t", t=128)[:, :, cb:cb + nci_n, :], o.t[:, 0:k["TGt"], 0:nci_n, :], o, reads=[o])
            return ex, epi

        ex, epi = mk(True)
        self.gemm(s["ynT"], own, DI // 128, [self.i["w_ssm_out"]], 0, D, "F", epi, extra=ex)
        ex, epi = mk(False)
        self.gemm(s["attnT"], own, MH, [self.i["w_mla_out"]], 0, D, "F", epi, extra=ex)
        self.resid_gemm(s["mg"], DC, self.i["w_out"], self.i["x_own"], s["h"])

    def resid_gemm(self, A_d, KC, W, res_d, dst_d):
        c, fw = self.c, self.fw
        T, D = c["T"], c["D"]
        own = list(range(T // 128))

        def ex(st):
            return self.ring(st, 2, [128, 8, 256], F32, "rs")

        def epi(k):
            b = k["ctx"]()
            r0 = k["tiles"][0] * 128
            rv = lambda d_: d_.ap()[r0:r0 + k["TG"], k["c0"]:k["c0"] + k["cw"]].rearrange("(j p) n -> p j n", p=128)
            fw.dma("sp", b.t[:, 0:k["TGt"], 0:k["cw"]], rv(res_d), b, writes=[b])
            for tt in range(k["TGt"]):
                fw.op("dve", lambda e: e.tensor_tensor(out=b.t[:, tt, 0:k["cw"]], in0=b.t[:, tt, 0:k["cw"]],
                                                       in1=k["ps"].t[:, tt * k["CW"]:tt * k["CW"] + k["cw"]], op=ALU.add),
                      reads=[b, k["ps"]], join=[b])
            fw.dma("sp", rv(dst_d), b.t[:, 0:k["TGt"], 0:k["cw"]], b, reads=[b])
        self.gemm(A_d, own, KC, [W], 0, D, "T", epi, extra=ex)

    def ffn(self):
        c, fw, s = self.c, self.fw, self.s
        T, D, DFF = c["T"], c["D"], c["DFF"]
        DC, FC = D // 128, DFF // 128
        own = list(range(T // 128))

        def ex(st):
            return (self.sb(st, [128, 1024], F32, "sg"), self.ring(st, 2, [128, 8, 128], BF16, "ao"))

        def epi(k):
            sg, oring = k["ctx"]
            o = oring()
            for th in range(k["NTH"]):
                w = k["THW"]
                fw.op("act", lambda e: e.activation(out=sg.t[:, th * 512:th * 512 + w], in_=k["ps"].t[:, th * 512:th * 512 + w],
                                                    func=AF.Silu), reads=[k["ps"]],
                      **({"writes": [sg]} if th == 0 else {"join": [sg]}))
                fw.op("dve", lambda e: e.tensor_tensor(
                    out=o.t[:, th * 4:th * 4 + w // 128, :],
                    in0=sg.t[:, th * 512:th * 512 + w].rearrange("p (j t) -> p j t", t=128),
                    in1=k["ps"].t[:, (k["NTH"] + th) * 512:(k["NTH"] + th) * 512 + w].rearrange("p (j t) -> p j t", t=128),
                    op=ALU.mult), reads=[sg, k["ps"]], **({"writes": [o]} if th == 0 else {"join": [o]}))
            ch = k["c0"] // 128
            fw.dma("sp", s["act"].ap()[k["tiles"][0]:k["tiles"][0] + k["TGt"], :, ch * 128:(ch + 1) * 128].rearrange(
                "j p t -> p j t"), o.t[:, 0:k["TGt"], :], o, reads=[o])
        self.gemm(s["nT"], own, DC, [self.i["w_gate"], self.i["w_up"]], 0, DFF, "F", epi, extra=ex)
        self.resid_gemm(s["act"], FC, self.i["w_down"], s["h"], s["h2"])

    def final_norm(self):
        c, fw, s = self.c, self.fw, self.s
        T, D, H = c["T"], c["D"], c["H"]
        with contextlib.ExitStack() as st:
            gb = self.sb(st, [128, D], F32, "gfin", dma=True)
            fw.dma("sp", gb.t[:, :], self.i["rowp"].ap()[0:1, 3 * H:3 * H + D].broadcast_to([128, D]), gb, writes=[gb])
            xb = [self.sb(st, [128, D], F32, "fx", dma=True) for _ in range(2)]
            junk = self.sb(st, [128, D], BF16, "fj")
            ss = [self.sb(st, [128, 2], F32, "fs") for _ in range(2)]
            for j in range(T // 128):
                x, s_ = xb[j % 2], ss[j % 2]
                fw.dma("sp", x.t[:, :], s["h2"].ap()[j * 128:(j + 1) * 128, :], x, writes=[x])
                fw.op("dve", lambda e: e.memset(s_.t[:, 0:2], 0.0), writes=[s_])
                fw.op("act", lambda e: e.activation(out=junk.t[:, :], in_=x.t[:, :], func=AF.Square, accum_out=s_.t[:, 0:1]),
                      reads=[x, s_], writes=[junk], join=[s_])
                fw.op("act", lambda e: e.activation(out=s_.t[:, 1:2], in_=s_.t[:, 0:1], func=AF.Ln, scale=1.0 / D,
                                                    bias=self.spc("eps6")), reads=[s_, self.smallp], join=[s_])
                fw.op("act", lambda e: e.activation(out=s_.t[:, 1:2], in_=s_.t[:, 1:2], func=AF.Exp, scale=-0.5),
                      reads=[s_], join=[s_])
                fw.op("dve", lambda e: e.scalar_tensor_tensor(out=x.t[:, :], in0=x.t[:, :], scalar=s_.t[:, 1:2],
                                                              in1=gb.t[:, :], op0=ALU.mult, op1=ALU.mult),
                      reads=[x, s_, gb], join=[x])
                fw.dma("sp", self.out.ap()[j * 128:(j + 1) * 128, :], x.t[:, :], x, reads=[x])
            fw.barrier()

    def build(self, upto=99):
        c = self.c
        self.declare()
        self.fw = FW(self.nc, self.top)
        self.load_consts()
        s, i = self.s, self.i
        NP, NO = c["PT"] // 128, c["T"] // 128
        srcs = [i["x_pre"].ap()[j * 128:(j + 1) * 128, :] for j in range(NP)] + \
               [i["x_own"].ap()[j * 128:(j + 1) * 128, :] for j in range(NO)]
        phases = [
            lambda: self.norm_transpose(srcs, "g_mix", s["uT"].ap()),
            self.in_proj,
            self.conv_phase,
            self.rope_tables,
            lambda: self.latent_norm(s["cqr"], c["QL"] // 128, c["T"], "q_g", s["cq"], c["QL"], False),
            lambda: self.latent_norm(s["ckvr"], 4, c["TT"], "kv_g", s["ckv"], c["KVL"], True),
            self.ssd,
            self.mla,
            self.merge_out,
            lambda: self.norm_transpose([s["h"].ap()[j * 128:(j + 1) * 128, :] for j in range(NO)], "g_ffn", s["nT"].ap()),
            self.ffn,
            self.final_norm,
        ]
        for pi, ph in enumerate(phases):
            if pi < upto:
                ph()
        self.top.close()
        return self.nc


def small_layout(c):
    D, DI, H, CONVC, QL = c["D"], c["DI"], c["H"], c["CONVC"], c["QL"]
    cols = {}
    n = 0
    for name, w in (("g_mix", D // 128), ("g_ffn", D // 128), ("conv_w", CONVC // 128 * 4), ("conv_b", CONVC // 128),
                    ("ssm_g", DI // 128), ("q_g", QL // 128), ("kv_g", 4), ("gate_bias", 2 * D // 128), ("flag", 1),
                    ("invf", 1), ("sgn", 1), ("eps6", 1), ("eps5", 1), ("one", 1), ("tri", 128), ("ntri", 128), ("onesf", 128)):
        cols[name] = n
        n += w
    cols["_n"] = n
    return cols


def host_prep(c, inp):
    D, T, PT, TT, DI, H, CONVC, QL, KVL, MH, SEQ = (c[k] for k in
                                                     ("D", "T", "PT", "TT", "DI", "H", "CONVC", "QL", "KVL", "MH", "SEQ"))
    f = lambda a: np.ascontiguousarray(np.asarray(a), dtype=np.float32)
    pc = lambda v: f(v).reshape(-1, 128).T
    cols = small_layout(c)
    sp = np.zeros((128, cols["_n"]), np.float32)

    def put(name, a):
        a = np.asarray(a, np.float32)
        sp[:, cols[name]:cols[name] + a.shape[1]] = a
    put("g_mix", pc(inp["g_mix"][0]))
    put("g_ffn", pc(inp["g_ffn"][0]))
    cw = f(inp["conv_w"][0])
    put("conv_w", cw.T.reshape(CONVC // 128, 128, 4).transpose(1, 0, 2).reshape(128, -1))
    put("conv_b", pc(inp["conv_b"][0]))
    put("ssm_g", pc(inp["ssm_norm_g"][0]))
    put("q_g", pc(inp["q_norm_g"][0]))
    put("kv_g", pc(inp["kv_norm_g"][0]))
    put("gate_bias", pc(f(inp["gate_bias"][0]).reshape(-1)))
    half = 32
    invf = (np.float32(10000.0) ** (-np.arange(half, dtype=np.float32) / np.float32(half))).astype(np.float32)
    put("invf", np.tile(invf, 4)[:, None])
    put("sgn", np.tile(np.concatenate([-np.ones(32), np.ones(32)]), 2)[:, None])
    put("eps6", np.full((128, 1), 1e-6))
    put("eps5", np.full((128, 1), 1e-5))
    put("one", np.ones((128, 1)))
    k = np.arange(128)
    put("tri", (k[:, None] <= k[None, :]).astype(np.float32))
    put("ntri", (k[:, None] > k[None, :]).astype(np.float32))
    put("onesf", np.ones((128, 128), np.float32))
    constb = np.zeros((128, 384), np.float32)
    constb[:, 0:128] = np.eye(128)
    constb[:, 128:256] = 1.0
    pm = np.zeros((128, 128), np.float32)
    for m in range(128):
        pm[(m // 64) * 64 + ((m % 64) + 32) % 64, m] = 1.0
    constb[:, 256:384] = pm
    constb = constb.astype(ml_dtypes.bfloat16)
    rowp = np.concatenate([f(inp["dt_bias"][0]), f(inp["a_log"][0]), f(inp["d_skip"][0]), f(inp["g_final"])])[None, :]
    w_in = f(inp["w_in"][0])
    o = DI + CONVC + H + QL
    w_ckv = np.ascontiguousarray(np.concatenate([w_in[:, o:o + 512], w_in[:, o + 512:o + 576], w_in[:, o + 512:o + 576]], 1))
    wq = f(inp["w_q_up"][0]).reshape(QL, MH, 192)
    wkv = f(inp["w_kv_up"][0]).reshape(KVL, MH, 256)
    shared = {
        "w_in_z": np.ascontiguousarray(w_in[:, 0:DI]), "w_in_xbc": np.ascontiguousarray(w_in[:, DI:DI + CONVC]),
        "w_in_r": np.ascontiguousarray(np.concatenate([w_in[:, DI + CONVC:o], w_in[:, o + 576:]], 1)), "w_ckv": w_ckv, "w_ssm_out": f(inp["w_ssm_out"][0]),
        "w_qn": np.ascontiguousarray(wq[:, :, :128].reshape(QL, -1)),
        "w_qr": np.ascontiguousarray(wq[:, :, 128:].reshape(QL, -1)),
        "w_kn": np.ascontiguousarray(wkv[:, :, :128].reshape(KVL, -1)),
        "w_v": np.ascontiguousarray(wkv[:, :, 128:].reshape(KVL, -1)),
        "w_mla_out": f(inp["w_mla_out"][0]), "w_out": f(inp["w_out"][0]), "w_gate": f(inp["w_ffn_gate"][0]),
        "w_up": f(inp["w_ffn_up"][0]), "w_down": f(inp["w_ffn_down"][0]), "constb": constb, "rowp": rowp,
    }
    x = np.asarray(inp["x"], np.float32)
    pos = np.asarray(inp["positions"], np.int32)
    maps = []
    for core in range(8):
        b, hf = core // 2, core % 2
        m = dict(shared)
        m["x_own"] = np.ascontiguousarray(x[b, hf * T:(hf + 1) * T])
        m["x_pre"] = np.ascontiguousarray(x[b, 0:PT]) if hf else np.zeros((PT, D), np.float32)
        p_own = pos[b, hf * T:(hf + 1) * T]
        p_pre = pos[b, 0:PT] if hf else np.zeros(PT, np.int32)
        m["pos"] = np.ascontiguousarray(np.concatenate([p_pre, p_own])[None, :].astype(np.int32))
        spc = sp.copy()
        spc[:, cols["flag"]] = float(hf)
        m["smallp"] = spc
        maps.append(m)
    return maps


_CACHE = {}


def run(cfg, inp):
    key = (cfg["D"], cfg["SEQ"])
    if key not in _CACHE:
        _CACHE[key] = Prog(cfg).build()
    nc = _CACHE[key]
    maps = host_prep(cfg, inp)
    res = run_bass_kernel_spmd(nc, maps, core_ids=list(range(8)))
    T, D = cfg["T"], cfg["D"]
    out = np.zeros((4, cfg["SEQ"], D), np.float32)
    for core in range(8):
        b, hf = core // 2, core % 2
        out[b, hf * T:(hf + 1) * T] = res.results[core]["out"]
    return out


def kernel(**inputs):
    cfg = make_cfg(4096, 4096)
    return run(cfg, inputs)
```
